# Optimizing a Trainium2 kernel written in Bass

```python
import jax, jax.numpy as jnp
from jax import lax
import numpy as np

D_MODEL = 1024
BATCH = 4
SEQ = 8192
DEPTH = 1

ATTN_HEADS = 8
HEAD_DIM = D_MODEL // 16
ATTN_WIDTH = ATTN_HEADS * HEAD_DIM
ROPE_DIM = HEAD_DIM // 4
ROPE_THETA = 500000.0
MOBA_BLOCK = 256
MOBA_TOP_K = 3
QUERY_CHUNK = 64
POOL_WINDOWS = (2, 4, 8, 16)
POOL_GROUPS = len(POOL_WINDOWS)
POOL_WIDTH = D_MODEL // 2
POOL_GROUP_DIM = POOL_WIDTH // POOL_GROUPS
N_BRANCHES = 2
IN_WIDTH = 3 * ATTN_WIDTH + POOL_WIDTH + N_BRANCHES * D_MODEL
D_FF = 2816
CONV_WIDTH = 3
EPS = 1e-6
NEG_INF = -1e30

kernel_name = "hybrid_moba_pool_convffn"


def rmsnorm(x, g):
    xf = x.astype(jnp.float32)
    y = xf * lax.rsqrt(jnp.mean(xf * xf, axis=-1, keepdims=True) + EPS)
    return (y * g.astype(jnp.float32)).astype(x.dtype)


def apply_partial_rotary(x, positions):
    half = ROPE_DIM // 2
    inv_freq = ROPE_THETA ** (-jnp.arange(half, dtype=jnp.float32) / half)
    ang = positions.astype(jnp.float32)[:, None] * inv_freq[None, :]
    cos = jnp.cos(ang).astype(x.dtype)
    sin = jnp.sin(ang).astype(x.dtype)
    x1 = x[..., :half]
    x2 = x[..., half:ROPE_DIM]
    return jnp.concatenate([x1 * cos - x2 * sin, x2 * cos + x1 * sin, x[..., ROPE_DIM:]], axis=-1)


def moba_attention(q, k, v):
    B, H, S, Dh = q.shape
    nb = -(-S // MOBA_BLOCK)
    pad = nb * MOBA_BLOCK - S
    k_p = jnp.pad(k, ((0, 0), (0, 0), (0, pad), (0, 0)))
    v_p = jnp.pad(v, ((0, 0), (0, 0), (0, pad), (0, 0)))
    k_blk = k_p.reshape(B, H, nb, MOBA_BLOCK, Dh)
    v_blk = v_p.reshape(B, H, nb, MOBA_BLOCK, Dh)
    k_mean = jnp.mean(k_blk.astype(jnp.float32), axis=3).astype(k.dtype)
    top_k = min(MOBA_TOP_K, nb)
    scale = Dh ** -0.5
    n_chunks = S // QUERY_CHUNK
    gather_blocks = jax.vmap(jax.vmap(lambda kb, idx: kb[idx]))

    def chunk(c):
        start = c * QUERY_CHUNK
        qc = lax.dynamic_slice_in_dim(q, start, QUERY_CHUNK, axis=2)
        q_pos = start + jnp.arange(QUERY_CHUNK)
        own = start // MOBA_BLOCK
        s_blk = jnp.einsum('bhqd,bhnd->bhqn', qc, k_mean).astype(jnp.float32)
        s_blk = jnp.where(jnp.arange(nb) < own, s_blk, -jnp.inf)
        _, sel = lax.top_k(s_blk, top_k)
        sel_valid = sel < own
        k_sel = gather_blocks(k_blk, sel)
        v_sel = gather_blocks(v_blk, sel)
        logit_sel = jnp.einsum('bhqd,bhqkld->bhqkl', qc, k_sel).astype(jnp.float32) * scale
        logit_sel = jnp.where(sel_valid[..., None], logit_sel, NEG_INF)
        k_own = lax.dynamic_index_in_dim(k_blk, own, axis=2, keepdims=False)
        v_own = lax.dynamic_index_in_dim(v_blk, own, axis=2, keepdims=False)
        logit_own = jnp.einsum('bhqd,bhld->bhql', qc, k_own).astype(jnp.float32) * scale
        k_pos_own = own * MOBA_BLOCK + jnp.arange(MOBA_BLOCK)
        logit_own = jnp.where(k_pos_own[None, :] <= q_pos[:, None], logit_own, NEG_INF)
        n_sel = top_k * MOBA_BLOCK
        logits = jnp.concatenate(
            [logit_sel.reshape(B, H, QUERY_CHUNK, n_sel), logit_own], axis=-1)
        p = jax.nn.softmax(logits, axis=-1).astype(v.dtype)
        p_sel = p[..., :n_sel].reshape(B, H, QUERY_CHUNK, top_k, MOBA_BLOCK)
        p_own = p[..., n_sel:]
        return (jnp.einsum('bhqkl,bhqkld->bhqd', p_sel, v_sel)
                + jnp.einsum('bhql,bhld->bhqd', p_own, v_own))

    outs = lax.map(chunk, jnp.arange(n_chunks))
    return outs.transpose(1, 2, 0, 3, 4).reshape(B, H, S, Dh)


def multiscale_pool_mixer(u, w_pool, pool_scale):
    B, S, _ = u.shape
    ug = u.reshape(B, S, POOL_GROUPS, POOL_GROUP_DIM).astype(jnp.float32)
    cs = jnp.cumsum(ug, axis=1)
    t = jnp.arange(S, dtype=jnp.float32)[None, :, None]
    groups = []
    for g, w in enumerate(POOL_WINDOWS):
        c = cs[:, :, g]
        c_prev = jnp.pad(c, ((0, 0), (w, 0), (0, 0)))[:, :S]
        count = jnp.minimum(t + 1.0, float(w))
        groups.append((c - c_prev) / count - ug[:, :, g])
    pooled = jnp.stack(groups, axis=2).astype(u.dtype)
    mixed = jnp.einsum('bsgc,gcd->bsgd', pooled, w_pool)
    return mixed.reshape(B, S, POOL_WIDTH) * pool_scale


def causal_depthwise_conv(u, w, b):
    S = u.shape[1]
    kw = w.shape[0]
    up = jnp.pad(u, ((0, 0), (kw - 1, 0), (0, 0)))
    out = up[:, 0:S] * w[0]
    for j in range(1, kw):
        out = out + up[:, j:j + S] * w[j]
    return out + b


def hybrid_layer(x, norm_mix_g, w_in, b_gate, q_norm_g, k_norm_g, w_pool, pool_scale,
                 w_branch_attn, w_branch_pool, w_out, norm_ffn_g, w_up, conv_w, conv_b, w_down):
    B, S, D = x.shape
    h = rmsnorm(x, norm_mix_g)
    proj = h @ w_in
    q, k, v, u_pool, gate_logits = jnp.split(
        proj, [ATTN_WIDTH, 2 * ATTN_WIDTH, 3 * ATTN_WIDTH, 3 * ATTN_WIDTH + POOL_WIDTH], axis=-1)

    def heads(t):
        return t.reshape(B, S, ATTN_HEADS, HEAD_DIM).transpose(0, 2, 1, 3)

    pos = jnp.arange(S)
    q = apply_partial_rotary(rmsnorm(heads(q), q_norm_g), pos)
    k = apply_partial_rotary(rmsnorm(heads(k), k_norm_g), pos)
    attn = moba_attention(q, k, heads(v)).transpose(0, 2, 1, 3).reshape(B, S, ATTN_WIDTH)
    pooled = multiscale_pool_mixer(u_pool, w_pool, pool_scale)

    gates = jax.nn.sigmoid(gate_logits + b_gate).reshape(B, S, N_BRANCHES, D)
    mixed = gates[:, :, 0] * (attn @ w_branch_attn) + gates[:, :, 1] * (pooled @ w_branch_pool)
    x = x + mixed @ w_out

    h2 = rmsnorm(x, norm_ffn_g)
    up = causal_depthwise_conv(h2 @ w_up, conv_w, conv_b)
    gate, val = jnp.split(up, 2, axis=-1)
    return x + (jax.nn.silu(gate) * val) @ w_down


def setup_inputs(seed: int = 0) -> dict:
    key = jax.random.key(seed)
    ks = jax.random.split(key, 17)
    L = DEPTH
    f32 = jnp.float32

    def nrm(k, shape, scale):
        return jax.random.normal(k, shape, f32) * scale

    return {
        "x": nrm(ks[0], (BATCH, SEQ, D_MODEL), 1.0),
        "norm_mix_g": 1.0 + nrm(ks[1], (L, D_MODEL), 0.02),
        "w_in": nrm(ks[2], (L, D_MODEL, IN_WIDTH), D_MODEL ** -0.5),
        "b_gate": nrm(ks[3], (L, N_BRANCHES * D_MODEL), 0.1),
        "q_norm_g": 1.0 + nrm(ks[4], (L, HEAD_DIM), 0.02),
        "k_norm_g": 1.0 + nrm(ks[5], (L, HEAD_DIM), 0.02),
        "w_pool": nrm(ks[6], (L, POOL_GROUPS, POOL_GROUP_DIM, POOL_GROUP_DIM), POOL_GROUP_DIM ** -0.5),
        "pool_scale": 1.0 + nrm(ks[7], (L, POOL_WIDTH), 0.1),
        "w_branch_attn": nrm(ks[8], (L, ATTN_WIDTH, D_MODEL), ATTN_WIDTH ** -0.5),
        "w_branch_pool": nrm(ks[9], (L, POOL_WIDTH, D_MODEL), POOL_WIDTH ** -0.5),
        "w_out": nrm(ks[10], (L, D_MODEL, D_MODEL), D_MODEL ** -0.5),
        "norm_ffn_g": 1.0 + nrm(ks[11], (L, D_MODEL), 0.02),
        "w_up": nrm(ks[12], (L, D_MODEL, 2 * D_FF), D_MODEL ** -0.5),
        "conv_w": nrm(ks[13], (L, CONV_WIDTH, 2 * D_FF), CONV_WIDTH ** -0.5),
        "conv_b": nrm(ks[14], (L, 2 * D_FF), 0.02),
        "w_down": nrm(ks[15], (L, D_FF, D_MODEL), D_FF ** -0.5),
    }


def reference(x, norm_mix_g, w_in, b_gate, q_norm_g, k_norm_g, w_pool, pool_scale,
              w_branch_attn, w_branch_pool, w_out, norm_ffn_g, w_up, conv_w, conv_b, w_down):
    for l in range(DEPTH):
        x = hybrid_layer(x, norm_mix_g[l], w_in[l], b_gate[l], q_norm_g[l], k_norm_g[l],
                         w_pool[l], pool_scale[l], w_branch_attn[l], w_branch_pool[l], w_out[l],
                         norm_ffn_g[l], w_up[l], conv_w[l], conv_b[l], w_down[l])
    return x
```

```python
import numpy as np
import ml_dtypes
from contextlib import ExitStack
import concourse.bass as bass
import concourse.mybir as mybir
from concourse.bass_utils import run_bass_kernel_spmd

F32 = mybir.dt.float32
BF16 = mybir.dt.bfloat16
AF = mybir.ActivationFunctionType
ALU = mybir.AluOpType
AX = mybir.AxisListType

D = 1024
NTOK = 4096
NH = 8
HD = 64
DFF = 2816
EPS = 1e-6
BIG = 30000.0
NQT = NTOK + 128
ENGS = ["pe", "act", "dve", "pool", "sp"]


class Prog:
    def __init__(self):
        self.q = {e: [] for e in ENGS}
        self.cnt = {}
        self.seen = {e: {} for e in ENGS}
        self.pending = {e: [] for e in ENGS}

    def _filter(self, eng, waits):
        mx = {}
        for w in list(self.pending[eng]) + list(waits):
            if w is None:
                continue
            k, v = w
            if v > mx.get(k, 0):
                mx[k] = v
        out = []
        for k, v in mx.items():
            if self.seen[eng].get(k, 0) >= v:
                continue
            self.seen[eng][k] = v
            out.append((k, v))
        self.pending[eng] = []
        return out

    def I(self, eng, method, waits=(), sig=True, **kw):
        w = self._filter(eng, waits)
        inc = None
        tok = None
        if sig:
            self.cnt[eng] = self.cnt.get(eng, 0) + 1
            inc = (eng, 1)
            tok = (eng, self.cnt[eng])
        self.q[eng].append((w, method, kw, inc))
        return tok

    def dma(self, queue, semkey, waits=(), **kw):
        w = self._filter(queue, waits)
        self.cnt[semkey] = self.cnt.get(semkey, 0) + 16
        self.q[queue].append((w, "dma_start", kw, (semkey, 16)))
        return (semkey, self.cnt[semkey])

    def barrier(self):
        toks = [(k, v) for k, v in self.cnt.items()]
        for e in ENGS:
            self.pending[e] = list(toks)

    def final_wait(self, eng, toks):
        w = self._filter(eng, toks)
        self.q[eng].append((w, None, None, None))

    def sem_keys(self):
        return list(self.cnt.keys())


class Buf:
    def __init__(self, t):
        self.t = t
        self.free = []
        self.closed = True

    def acquire(self):
        assert self.closed, "buffer re-acquired while a previous user is still emitting"
        self.closed = False
        return self.take()

    def close(self):
        self.closed = True

    def take(self):
        f = self.free
        self.free = []
        return f

    def used(self, tok):
        if tok is not None:
            self.free.append(tok)


class Ring:
    def __init__(self, bufs):
        self.bufs = [Buf(b) for b in bufs]
        self.i = 0

    def next(self):
        b = self.bufs[self.i % len(self.bufs)]
        self.i += 1
        return b


def run_tasks(gens, width):
    active = []
    it = iter(gens)
    exhausted = False
    while True:
        if not exhausted and len(active) < width:
            g = next(it, None)
            if g is None:
                exhausted = True
            else:
                active.append(g)
        if not active and exhausted:
            break
        for g in list(active):
            try:
                next(g)
            except StopIteration:
                active.remove(g)


def build_program(do_attn=True):
    nc = bass.Bass("TRN2", target_bir_lowering=False)
    P = Prog()

    def din(name, shape, dt=F32):
        return nc.dram_tensor(name, list(shape), dt, kind="ExternalInput").ap()

    x_own = din("x_own", [NTOK, D])
    x_pre = din("x_pre", [NTOK, D])
    w_in = din("w_in", [D, 4096])
    w_pool = din("w_pool", [4, 128, 128])
    w_ba = din("w_ba", [512, D])
    w_bp = din("w_bp", [512, D])
    w_out = din("w_out", [D, D])
    w_up = din("w_up", [D, 2 * DFF])
    w_down = din("w_down", [DFF, D])
    gmix_b = din("gmix_b", [128, D])
    gffn_b = din("gffn_b", [128, D])
    gq_b = din("gq_b", [128, 512])
    gk_b = din("gk_b", [128, 512])
    bgate_t = din("bgate_t", [128, 16])
    pscale_t = din("pscale_t", [128, 4])
    convw_t = din("convw_t", [128, 3, 44])
    convb_t = din("convb_t", [128, 44])
    ident_d = din("ident_bf", [128, 128], BF16)
    cbias_d = din("cbias_bf", [128, 128], BF16)
    cos_d = din("cos_t", [128, 64, 8])
    sin_d = din("sin_t", [128, 64, 8])
    rbias_d = din("rbias", [128, 33, 32])
    invcnt_d = din("invcnt", [128, 4, 16])
    hflag_d = din("hflag", [128, 1])
    y = nc.dram_tensor("y", [NTOK, D], F32, kind="ExternalOutput").ap()

    KT = nc.dram_tensor("KT_s", [NH, 96, 2 * NTOK], BF16).ap()
    VS = nc.dram_tensor("VS_s", [NH, 128, 64, 64], BF16).ap()
    QT = nc.dram_tensor("QT_s", [NH, 96, NQT], BF16).ap()
    G0 = nc.dram_tensor("G0_s", [8, 128, NQT], BF16).ap()
    GP = nc.dram_tensor("GP_s", [8, 128, NQT], BF16).ap()
    AT = nc.dram_tensor("AT_s", [4, 128, NQT], BF16).ap()

    es_all = ExitStack()
    esA = ExitStack()

    def sb(es, name, shape, dt):
        return es.enter_context(nc.sbuf_tensor("sb_" + name, list(shape), dt))

    def ps(es, name, shape, dt):
        return es.enter_context(nc.psum_tensor("ps_" + name, list(shape), dt))

    gffn = sb(es_all, "gffn", [128, D], F32)
    bgate = sb(es_all, "bgate", [128, 16], F32)
    pscale = sb(es_all, "pscale", [128, 4], F32)
    convw = sb(es_all, "convw", [128, 3, 44], F32)
    convb = sb(es_all, "convb", [128, 44], F32)
    ident = sb(es_all, "ident", [128, 128], BF16)
    cbias = sb(es_all, "cbias", [128, 128], BF16)
    hflag = sb(es_all, "hflag", [128, 1], F32)
    neghalf = sb(es_all, "neghalf", [128, 8], F32)
    ones_f = sb(es_all, "ones_f", [128, 64], F32)
    ssq_all = sb(es_all, "ssq_all", [128, 160], F32)
    rs_all = sb(es_all, "rs_all", [128, 160], F32)
    junk = sb(es_all, "junk", [128, D], F32)
    carry = sb(es_all, "carry", [128, 44, 2], F32)
    gmix = sb(esA, "gmix", [128, D], F32)
    gq = sb(esA, "gq", [128, 512], F32)
    gk = sb(esA, "gk", [128, 512], F32)
    cos_t = sb(esA, "cos_t", [128, 64, 8], F32)
    sin_t = sb(esA, "sin_t", [128, 64, 8], F32)
    rbias = sb(esA, "rbias", [128, 33, 32], F32)
    invcnt = sb(esA, "invcnt", [128, 4, 16], F32)

    c_toks = []
    for dst, src in [(gmix, gmix_b), (gffn, gffn_b), (gq, gq_b), (gk, gk_b), (bgate, bgate_t),
                     (pscale, pscale_t), (convw, convw_t), (convb, convb_t), (ident, ident_d),
                     (cbias, cbias_d), (cos_t, cos_d), (sin_t, sin_d), (rbias, rbias_d),
                     (invcnt, invcnt_d), (hflag, hflag_d)]:
        c_toks.append(P.dma("sp", "const", out=dst[:], in_=src))
    CONST = c_toks[-1]
    t_nh = P.I("pool", "memset", ap=neghalf[:], constant=-0.5)
    t_ones = P.I("pool", "memset", ap=ones_f[:], constant=1.0)
    norm_idx = [0]
    junk_tok = [None]

    def rmsnorm_tile(x_ap, g_t, out_bf, waits):
        i = norm_idx[0]
        norm_idx[0] += 1
        t1 = P.I("act", "activation", waits=list(waits) + [junk_tok[0]], out=junk[:], in_=x_ap, func=AF.Square,
                 accum_out=ssq_all[:, i:i + 1])
        junk_tok[0] = t1
        t2 = P.I("pool", "tensor_scalar", waits=[t1, t_nh], out=rs_all[:, i:i + 1], in0=ssq_all[:, i:i + 1],
                 scalar1=1.0 / D, scalar2=EPS, op0=ALU.mult, op1=ALU.add)
        t3 = P.I("pool", "tensor_tensor", waits=[t2], out=rs_all[:, i:i + 1], in0=rs_all[:, i:i + 1],
                 in1=neghalf[:, 0:1], op=ALU.pow)
        return t3, i

    spill_toks = []
    w_in_v = w_in.rearrange("(kt p) n -> p kt n", p=128)
    xr = Ring([sb(esA, f"xa{i}", [128, D], F32) for i in range(6)])
    hbr = Ring([sb(esA, f"hb{i}", [128, D], BF16) for i in range(3)])

    def x_chain(src, row0, trx_ring, get_dst):
        xb = xr.next()
        w = xb.acquire()
        t_ld = P.dma("sp", f"xa{(xr.i - 1) % len(xr.bufs)}", waits=w, out=xb.t[:], in_=src[row0:row0 + 128, :])
        yield
        i = norm_idx[0]
        norm_idx[0] += 1
        t1 = P.I("act", "activation", waits=[t_ld, junk_tok[0]], out=junk[:], in_=xb.t[:], func=AF.Square, accum_out=ssq_all[:, i:i + 1])
        junk_tok[0] = t1
        yield
        t2 = P.I("pool", "tensor_scalar", waits=[t1, t_nh], out=rs_all[:, i:i + 1], in0=ssq_all[:, i:i + 1],
                 scalar1=1.0 / D, scalar2=EPS, op0=ALU.mult, op1=ALU.add)
        t_rs = P.I("pool", "tensor_tensor", waits=[t2], out=rs_all[:, i:i + 1], in0=rs_all[:, i:i + 1],
                   in1=neghalf[:, 0:1], op=ALU.pow)
        yield
        hb = hbr.next()
        w = hb.acquire()
        t_h = P.I("dve", "scalar_tensor_tensor", waits=[t_rs, t_ld, CONST] + w, out=hb.t[:], in0=xb.t[:],
                  scalar=rs_all[:, i:i + 1], in1=gmix[:], op0=ALU.mult, op1=ALU.mult)
        xb.used(t_h)
        xb.close()
        yield
        tr = trx_ring.next()
        wtr = tr.acquire()
        for kt in range(8):
            t_tr = P.I("pe", "transpose", waits=([t_h, CONST] + wtr) if kt == 0 else (), sig=(kt == 7),
                       out=tr.t[:, kt, :], in_=hb.t[:, kt * 128:(kt + 1) * 128], identity=ident[:])
        hb.used(t_tr)
        hb.close()
        dst_ap, dst_waits = get_dst()
        t_cp = P.I("act", "activation", waits=[t_tr] + dst_waits, out=dst_ap, in_=tr.t[:], func=AF.Copy)
        tr.used(t_cp)
        tr.close()
        return t_cp

    if do_attn:
        esA1 = ExitStack()
        w_qkv = sb(esA1, "w_qkv", [128, 8, 1536], BF16)
        hTt = Ring([sb(esA1, f"hTt{i}", [128, 8, 128], BF16) for i in range(3)])
        Kaug = Ring([sb(esA1, f"Kaug{i}", [128, 8, 96], BF16) for i in range(4)])
        Qaug = Ring([sb(esA1, f"Qaug{i}", [128, 8, 96], BF16) for i in range(7)])
        KTst = Ring([sb(esA1, f"KTst{i}", [96, 8, 512], BF16) for i in range(2)])
        QTst = Ring([sb(esA1, f"QTst{i}", [96, 8, 512], BF16) for i in range(2)])
        Vst = Ring([sb(esA1, f"Vst{i}", [128, 8, 4, 64], BF16) for i in range(2)])
        tq = Ring([sb(esA1, f"tq{i}", [128, 8, 64], F32) for i in range(10)])
        sqb = Ring([sb(esA1, f"sqb{i}", [128, 8, 64], F32) for i in range(6)])
        ssqh = sb(esA1, "ssqh", [128, 100, 8], F32)
        rp = Ring([sb(esA1, f"rp{i}", [128, 4, 8, 8], F32) for i in range(4)])
        kmT = sb(esA1, "kmT", [64, 8, 32], BF16)
        kms = sb(esA1, "kms", [64, 8], F32)
        sbr = Ring([sb(esA1, f"sbr{i}", [128, 8, 32], F32) for i in range(3)])
        m8 = Ring([sb(esA1, f"m8_{i}", [128, 8, 8], F32) for i in range(2)])
        thr = Ring([sb(esA1, f"thr{i}", [128, 8], F32) for i in range(2)])
        ger = Ring([sb(esA1, f"ge{i}", [128, 8, 32], F32) for i in range(2)])
        trx1 = Ring([ps(esA1, f"trx1_{i}", [128, 8, 128], BF16) for i in range(2)])
        pq = Ring([ps(esA1, f"pq{i}", [128, 512], F32) for i in range(3)])
        trk = Ring([ps(esA1, f"trk{i}", [128, 8, 128], BF16) for i in range(2)])
        prt = Ring([ps(esA1, "prt", [128, 8, 32], F32)])

        WK = None
        for (c0, c1) in [(512, 1536), (0, 512)]:
            for kt in range(8):
                tk = P.dma("pool", f"w{c0}", out=w_qkv[:, kt, c0:c1], in_=w_in_v[:, kt, c0:c1])
            if c0 == 512:
                WKV = tk
            else:
                WQ = tk
        t_km0 = P.I("pool", "memset", ap=kmT[:], constant=0.0)
        kmT_tok = [t_km0]
        qk_idx = [0]
        kst_state, vst_state, qst_state = {}, {}, {}

        def qk_evac(pb, t_mm, g_t):
            pqv = pb.t[:].rearrange("p (h d) -> p h d", h=8)
            sq = sqb.next()
            t_sq = P.I("act", "activation", waits=[t_mm] + sq.acquire(), out=sq.t[:], in_=pqv, func=AF.Square)
            t = tq.next()
            t_g = P.I("dve", "tensor_tensor", waits=[t_mm, t_sq, CONST] + t.acquire(), out=t.t[:], in0=pqv,
                      in1=g_t[:].rearrange("p (h d) -> p h d", h=8), op=ALU.mult)
            pb.used(t_g)
            pb.used(t_sq)
            pb.close()
            return {"sq": sq, "t_sq": t_sq, "t": t, "t_g": t_g}

        def qk_reduce(st):
            i = qk_idx[0]
            qk_idx[0] += 1
            st["i"] = i
            t_red = P.I("dve", "tensor_reduce", waits=[st["t_sq"]], out=ssqh[:, i, :], in_=st["sq"].t[:], axis=AX.X, op=ALU.add)
            st["sq"].used(t_red)
            st["sq"].close()
            st["t_red"] = t_red

        def qk_pow(st):
            i = st["i"]
            t_r1 = P.I("pool", "tensor_scalar", waits=[st["t_red"], t_nh], out=ssqh[:, i, :], in0=ssqh[:, i, :],
                       scalar1=1.0 / HD, scalar2=EPS, op0=ALU.mult, op1=ALU.add)
            st["t_r2"] = P.I("pool", "tensor_tensor", waits=[t_r1], out=ssqh[:, i, :], in0=ssqh[:, i, :],
                             in1=neghalf[:], op=ALU.pow)

        def qk_rope(st, tile_idx, aug):
            t, t_g, t_r2, i = st["t"], st["t_g"], st["t_r2"], st["i"]
            r = rp.next()
            cosb = cos_t[:, tile_idx, :].unsqueeze(1).to_broadcast([128, 8, 8])
            sinb = sin_t[:, tile_idx, :].unsqueeze(1).to_broadcast([128, 8, 8])
            x1 = t.t[:, :, 0:8]
            x2 = t.t[:, :, 8:16]
            wr = r.acquire()
            ta = P.I("dve", "tensor_tensor", waits=[t_g] + wr, out=r.t[:, 0], in0=x1, in1=cosb, op=ALU.mult)
            tb = P.I("dve", "tensor_tensor", waits=[t_g], out=r.t[:, 1], in0=x2, in1=sinb, op=ALU.mult)
            tc = P.I("dve", "tensor_tensor", waits=[t_g], out=r.t[:, 2], in0=x2, in1=cosb, op=ALU.mult)
            td = P.I("dve", "tensor_tensor", waits=[t_g], out=r.t[:, 3], in0=x1, in1=sinb, op=ALU.mult)
            te = P.I("dve", "tensor_tensor", waits=[ta, tb, tc, td], out=x1, in0=r.t[:, 0], in1=r.t[:, 1], op=ALU.subtract)
            tf = P.I("dve", "tensor_tensor", waits=[tc, td], out=x2, in0=r.t[:, 2], in1=r.t[:, 3], op=ALU.add)
            r.used(tf)
            r.close()
            t_fin = P.I("dve", "tensor_tensor", waits=[te, tf, t_r2] + aug.acquire(), out=aug.t[:, :, 0:64], in0=t.t[:],
                        in1=ssqh[:, i, :].unsqueeze(2).to_broadcast([128, 8, 64]), op=ALU.mult)
            t.used(t_fin)
            t.close()
            return t_fin

        def stage_get(state, key, ring):
            if key not in state:
                b = ring.next()
                state[key] = {"buf": b, "w": b.acquire(), "toks": [], "n": 0, "slot": (ring.i - 1) % len(ring.bufs)}
            return state[key]

        def taskA1(kind, idx):
            if kind == "pre":
                src, row0, ktile, groups, mkey, j = x_pre, idx * 128, idx, ["k", "v"], ("pre", idx // 4), idx % 4
            elif kind == "halo":
                src, row0, ktile, groups, mkey, j = x_pre, 3968, 31, ["q"], ("halo", 0), 0
            else:
                src, row0, ktile, groups, mkey, j = x_own, (idx - 32) * 128, idx, ["q", "k", "v"], ("own", (idx - 32) // 4), idx % 4
            nt_macro = 1 if kind == "halo" else 4
            qtile = 0 if kind == "halo" else 1 + (idx - 32)
            slot = ktile // 2
            hb_box = []

            def get_dst():
                hb_ = hTt.next()
                hb_box.append(hb_)
                return hb_.t[:], hb_.acquire()

            t_hT = yield from x_chain(src, row0, trx1, get_dst)
            hb_ = hb_box[0]
            yield
            col0 = {"q": 0, "k": 512, "v": 1024}
            st = {}
            for gname in groups:
                pb = pq.next()
                wpb = pb.acquire()
                for kt in range(8):
                    t_mm = P.I("pe", "matmul", waits=([t_hT, WKV if gname != "q" else WQ] + wpb) if kt == 0 else (), sig=(kt == 7),
                               out=pb.t[:], lhsT=hb_.t[:, kt, :], rhs=w_qkv[:, kt, col0[gname]:col0[gname] + 512],
                               start=(kt == 0), stop=(kt == 7))
                hb_.used(t_mm)
                if gname == "v":
                    vs = stage_get(vst_state, mkey, Vst)
                    t_v = P.I("act", "activation", waits=[t_mm] + vs["w"], out=vs["buf"].t[:, :, j, :],
                              in_=pb.t[:].rearrange("p (h d) -> p h d", h=8), func=AF.Copy)
                    vs["w"] = []
                    pb.used(t_v)
                    pb.close()
                    vs["toks"].append(t_v)
                    vs["n"] += 1
                    if vs["n"] == 4:
                        t_sv = P.dma("sp", f"vst{vs['slot']}", waits=vs["toks"], out=VS[:, :, ktile - 3:ktile + 1, :].rearrange("h p t c -> p h t c"),
                                     in_=vs["buf"].t[:])
                        vs["buf"].used(t_sv)
                        vs["buf"].close()
                        spill_toks.append(t_sv)
                else:
                    st[gname] = qk_evac(pb, t_mm, gk if gname == "k" else gq)
            hb_.close()
            yield
            for gname in st:
                qk_reduce(st[gname])
            yield
            for gname in st:
                qk_pow(st[gname])
            yield
            if "k" in groups:
                ka = Kaug.next()
                t_k = qk_rope(st["k"], ktile, ka)
            if "q" in groups:
                qa = Qaug.next()
                t_q = qk_rope(st["q"], ktile, qa)
            yield
            if "k" in groups:
                t_z = P.I("pool", "memset", waits=[t_k], ap=ka.t[:, :, 64:96], constant=0.0)
                t_o = P.I("pool", "memset", waits=[t_z], ap=ka.t[:, :, 64 + slot:65 + slot], constant=1.0)
            if "q" in groups:
                tkb = trk.next()
                wtk = tkb.acquire()
                for h in range(8):
                    t_tq = P.I("pe", "transpose", waits=([t_q] + wtk) if h == 0 else (), sig=(h == 7),
                               out=tkb.t[0:64, h, :], in_=qa.t[:, h, 0:64], identity=ident[:])
                qs = stage_get(qst_state, mkey, QTst)
                t_qc = P.I("act", "activation", waits=[t_tq] + qs["w"], out=qs["buf"].t[0:64, :, j * 128:(j + 1) * 128],
                           in_=tkb.t[0:64], func=AF.Copy)
                qs["w"] = []
                tkb.used(t_qc)
                tkb.close()
            yield
            if "k" in groups:
                tkb = trk.next()
                wtk = tkb.acquire()
                for h in range(8):
                    t_tk = P.I("pe", "transpose", waits=([t_k, t_o] + wtk) if h == 0 else (), sig=(h == 7),
                               out=tkb.t[0:96, h, :], in_=ka.t[:, h, :], identity=ident[:])
                ka.used(t_tk)
                ka.close()
                ks = stage_get(kst_state, mkey, KTst)
                t_kc = P.I("act", "activation", waits=[t_tk] + ks["w"], out=ks["buf"].t[:, :, j * 128:(j + 1) * 128],
                           in_=tkb.t[0:96], func=AF.Copy)
                ks["w"] = []
                tkb.used(t_kc)
                tkb.close()
                ks["toks"].append(t_kc)
                ks["n"] += 1
                k_last = (ks["n"] == 4)
            if "q" in groups:
                pr = prt.next()
                wpr = pr.acquire()
                for h in range(8):
                    t_rt = P.I("pe", "matmul", waits=([t_qc, kmT_tok[0]] + wpr) if h == 0 else (), sig=(h == 7),
                               out=pr.t[:, h, :], lhsT=qs["buf"].t[0:64, h, j * 128:(j + 1) * 128],
                               rhs=kmT[:, h, :], start=True, stop=True)
                sbb = sbr.next()
                t_sb = P.I("dve", "tensor_tensor", waits=[t_rt, CONST] + sbb.acquire(), out=sbb.t[:], in0=pr.t[:],
                           in1=rbias[:, qtile, :].unsqueeze(1).to_broadcast([128, 8, 32]), op=ALU.add)
                pr.used(t_sb)
                pr.close()
            yield
            if "k" in groups:
                if ktile % 2 == 1:
                    jb = j - 1
                    t_ks = P.I("dve", "tensor_reduce", waits=[ks["toks"][-1], ks["toks"][-2], kmT_tok[0]], out=kms[:],
                               in_=ks["buf"].t[0:64, :, jb * 128:(jb + 2) * 128], axis=AX.X, op=ALU.add)
                    t_km = P.I("dve", "tensor_scalar", waits=[t_ks], out=kmT[:, :, slot], in0=kms[:],
                               scalar1=1.0 / 256.0, scalar2=None, op0=ALU.mult)
                    kmT_tok[0] = t_km
                    ks["buf"].used(t_km)
                if k_last:
                    kbase = (ktile - 3) * 128
                    t_st = P.dma("sp", f"kst{ks['slot']}", waits=ks["toks"], out=KT[:, :, kbase:kbase + 512].rearrange("h r t -> r h t"),
                                 in_=ks["buf"].t[:])
                    ks["buf"].used(t_st)
                    ks["buf"].close()
                    spill_toks.append(t_st)
            if "q" not in groups:
                return
            mb = m8.next()
            wm = mb.acquire()
            for h in range(8):
                t_m8 = P.I("dve", "max", waits=([t_sb] + wm) if h == 0 else (), sig=(h == 7),
                           out=mb.t[:, h, :], in_=sbb.t[:, h, :])
            th = thr.next()
            t_th = P.I("dve", "tensor_scalar", waits=[t_m8] + th.acquire(), out=th.t[:], in0=mb.t[:, :, 2],
                       scalar1=-1.0e4, scalar2=None, op0=ALU.max)
            mb.used(t_th)
            mb.close()
            gb = ger.next()
            t_ge = P.I("dve", "tensor_tensor", waits=[t_th, t_sb] + gb.acquire(), out=gb.t[:], in0=sbb.t[:],
                       in1=th.t[:].unsqueeze(2).to_broadcast([128, 8, 32]), op=ALU.is_ge)
            sbb.used(t_ge)
            sbb.close()
            th.used(t_ge)
            th.close()
            t_mk = P.I("dve", "tensor_scalar", waits=[t_ge, t_q, t_tq], out=qa.t[:, :, 64:96], in0=gb.t[:],
                       scalar1=-1.0, scalar2=BIG, op0=ALU.add, op1=ALU.mult)
            gb.used(t_mk)
            gb.close()
            t_mk = P.I("dve", "memset", waits=[t_mk], ap=qa.t[:, :, 64 + slot:65 + slot], constant=0.0)
            yield
            tkb = trk.next()
            wtk = tkb.acquire()
            for h in range(8):
                t_tm = P.I("pe", "transpose", waits=([t_mk] + wtk) if h == 0 else (), sig=(h == 7),
                           out=tkb.t[0:32, h, :], in_=qa.t[:, h, 64:96], identity=ident[:])
            qa.used(t_tm)
            qa.close()
            t_mc = P.I("dve", "tensor_copy", waits=[t_tm, t_rt], out=qs["buf"].t[64:96, :, j * 128:(j + 1) * 128],
                       in_=tkb.t[0:32])
            tkb.used(t_mc)
            tkb.close()
            qs["toks"].append(t_mc)
            qs["n"] += 1
            if qs["n"] == nt_macro:
                N = nt_macro * 128
                qoff = 0 if kind == "halo" else 128 + (idx - 32 - 3) * 128
                t_sq = P.dma("sp", f"qst{qs['slot']}", waits=qs["toks"], out=QT[:, :, qoff:qoff + N].rearrange("h r t -> r h t"),
                             in_=qs["buf"].t[:, :, 0:N])
                qs["buf"].used(t_sq)
                qs["buf"].close()
                spill_toks.append(t_sq)

        gensA1 = [taskA1("pre", i) for i in range(32)] + [taskA1("halo", 0)] + [taskA1("own", 32 + i) for i in range(32)]
        run_tasks(gensA1, 16)
        P.barrier()
        esA1.close()

    esA2 = ExitStack()
    w_pg = sb(esA2, "w_pg", [128, 8, 2560], BF16)
    w_bp_bf = sb(esA2, "w_bp_bf", [128, 4, D], BF16)
    w_pool_bf = sb(esA2, "w_pool_bf", [128, 4, 128], BF16)
    hTm = Ring([sb(esA2, f"hTm{i}", [128, 8, 512], BF16) for i in range(2)])
    G0st = Ring([sb(esA2, f"G0st{i}", [128, 8, 512], BF16) for i in range(2)])
    GPst = Ring([sb(esA2, f"GPst{i}", [128, 8, 512], BF16) for i in range(2)])
    ubs = [[sb(esA2, f"ub{g}_{i}", [128, 16 + 512], F32) for i in range(3)] for g in range(4)]
    ucarry = sb(esA2, "ucarry", [128, 4, 16], F32)
    pooled = Ring([sb(esA2, f"pooled{i}", [128, 512], BF16) for i in range(4)])
    pmr = Ring([sb(esA2, f"pm{i}", [128, 4, 512], BF16) for i in range(2)])
    g1r = Ring([sb(esA2, f"g1_{i}", [128, 512], F32) for i in range(5)])
    tmp16 = sb(esA2, "tmp16", [128, 4, 16], F32)
    trx2 = Ring([ps(esA2, f"trx2_{i}", [128, 8, 128], BF16) for i in range(2)])
    pf = Ring([ps(esA2, f"pf{i}", [128, 512], F32) for i in range(6)])

    for kt in range(8):
        WP = P.dma("pool", "wpg0", out=w_pg[:, kt, 0:512], in_=w_in_v[:, kt, 1536:2048])
    for kt in range(8):
        WG = P.dma("pool", "wpg1", out=w_pg[:, kt, 512:2560], in_=w_in_v[:, kt, 2048:4096])
    W_BP = P.dma("pool", "wbp", out=w_bp_bf[:], in_=w_bp.rearrange("(g p) n -> p g n", p=128))
    W_POOL = P.dma("pool", "wpool", out=w_pool_bf[:], in_=w_pool.rearrange("g c d -> c g d"))
    t_uc0 = P.I("pool", "memset", ap=ucarry[:], constant=0.0)
    ucarry_tok = [t_uc0] * 4
    ub_free = [[] for _ in range(4)]
    for g_ in range(4):
        for i_ in (1, 2):
            ub_free[g_].append(P.I("pool", "memset", ap=ubs[g_][i_][:], constant=0.0))
    macros = [("halo", 0)] + [("own", m) for m in range(8)]
    mstate = {}

    def mget(mi):
        if mi not in mstate:
            hb_ = hTm.next()
            mstate[mi] = {"hT": hb_, "hT_w": hb_.acquire(), "hT_toks": [], "hT_users": 0,
                          "pm": None, "pm_toks": {}, "pm_users": 0, "g0": None, "gp": None, "g0_toks": [], "gp_toks": []}
        return mstate[mi]

    def hT_user_done(ms, tok, total):
        ms["hT"].used(tok)
        ms["hT_users"] += 1
        if ms["hT_users"] == total:
            ms["hT"].close()

    def taskX(mi, j):
        kind, m = macros[mi]
        src = x_pre if kind == "halo" else x_own
        row0 = 3968 if kind == "halo" else m * 512 + j * 128

        def get_dst():
            ms = mget(mi)
            w = ms["hT_w"]
            ms["hT_w"] = []
            return ms["hT"].t[:, :, j * 128:(j + 1) * 128], w

        t = yield from x_chain(src, row0, trx2, get_dst)
        mget(mi)["hT_toks"].append(t)

    def taskPool(mi, g):
        kind, m = macros[mi]
        N = 128 if kind == "halo" else 512
        ms = mget(mi)
        while len(ms["hT_toks"]) < N // 128:
            yield
        ub = ubs[g]
        w = 2 << g
        pb = pf.next()
        wpb = pb.acquire()
        for kt in range(8):
            t_mm = P.I("pe", "matmul", waits=(ms["hT_toks"] + [WP] + wpb) if kt == 0 else (), sig=(kt == 7),
                       out=pb.t[:, 0:N], lhsT=w_pg[:, kt, g * 128:(g + 1) * 128],
                       rhs=ms["hT"].t[:, kt, 0:N], start=(kt == 0), stop=(kt == 7))
        hT_user_done(ms, t_mm, 20)
        wub = ub_free[g]
        t_c0 = P.I("pool", "tensor_copy", waits=wub + [ucarry_tok[g]], out=ub[0][:, 0:16], in_=ucarry[:, g, :])
        t_u = P.I("act", "activation", waits=[t_mm] + wub, out=ub[0][:, 16:16 + N], in_=pb.t[:, 0:N], func=AF.Copy)
        pb.used(t_u)
        pb.close()
        if kind == "halo":
            ucarry_tok[g] = P.I("pool", "tensor_scalar", waits=[t_u, t_c0, CONST], out=ucarry[:, g, :], in0=ub[0][:, N:N + 16],
                                scalar1=hflag[:, 0:1], scalar2=None, op0=ALU.mult)
        else:
            ucarry_tok[g] = P.I("pool", "tensor_copy", waits=[t_u, t_c0], out=ucarry[:, g, :], in_=ub[0][:, N:N + 16])
        L = 16 + N
        src_i, last = 0, [t_u, t_c0]
        sh = 1
        for step in range(g + 1):
            dst_i = 1 if src_i != 1 else 2
            t_a = P.I("pool", "tensor_tensor", waits=last + wub, out=ub[dst_i][:, sh:L], in0=ub[src_i][:, sh:L],
                      in1=ub[src_i][:, 0:L - sh], op=ALU.add)
            last = [t_a]
            src_i = dst_i
            sh *= 2
        po = pooled.next()
        wpo = po.acquire()
        t_p = P.I("dve", "scalar_tensor_tensor", waits=last + [t_u, ucarry_tok[g]] + wpo, out=po.t[:, 0:N], in0=ub[src_i][:, 16:16 + N],
                  scalar=1.0 / w, in1=ub[0][:, 16:16 + N], op0=ALU.mult, op1=ALU.subtract)
        if kind == "own" and m == 0:
            t_p1 = P.I("dve", "tensor_tensor", waits=[t_p, CONST], out=tmp16[:, g, :], in0=ub[src_i][:, 16:32],
                       in1=invcnt[:, g, :], op=ALU.mult)
            t_p = P.I("dve", "tensor_tensor", waits=[t_p1], out=po.t[:, 0:16], in0=tmp16[:, g, :], in1=ub[0][:, 16:32],
                      op=ALU.subtract)
        ub_free[g] = [t_p]
        yield
        yield
        yield
        if ms["pm"] is None:
            ms["pm"] = pmr.next()
            ms["pm_w"] = ms["pm"].acquire()
        pb2 = pf.next()
        t_pm = P.I("pe", "matmul", waits=[t_p, W_POOL] + pb2.acquire(), out=pb2.t[:, 0:N], lhsT=w_pool_bf[:, g, :],
                   rhs=po.t[:, 0:N], start=True, stop=True)
        po.used(t_pm)
        po.close()
        t_ps = P.I("act", "activation", waits=[t_pm, CONST] + ms["pm_w"], out=ms["pm"].t[:, g, 0:N], in_=pb2.t[:, 0:N],
                   func=AF.Identity, scale=pscale[:, g:g + 1])
        ms["pm_w"] = []
        pb2.used(t_ps)
        pb2.close()
        ms["pm_toks"][g] = t_ps

    def taskGate(mi, f):
        kind, m = macros[mi]
        N = 128 if kind == "halo" else 512
        qoff = 0 if kind == "halo" else 128 + m * 512
        ms = mget(mi)
        while len(ms["hT_toks"]) < N // 128:
            yield
        if ms["g0"] is None:
            ms["g0"] = G0st.next()
            ms["gslot"] = (G0st.i - 1) % len(G0st.bufs)
            ms["g0_w"] = ms["g0"].acquire()
            ms["gp"] = GPst.next()
            ms["gp_w"] = ms["gp"].acquire()
        pb = pf.next()
        wpb = pb.acquire()
        for kt in range(8):
            t_mm = P.I("pe", "matmul", waits=(ms["hT_toks"] + [WG] + wpb) if kt == 0 else (), sig=(kt == 7),
                       out=pb.t[:, 0:N], lhsT=w_pg[:, kt, 512 + f * 128:512 + (f + 1) * 128],
                       rhs=ms["hT"].t[:, kt, 0:N], start=(kt == 0), stop=(kt == 7))
        hT_user_done(ms, t_mm, 20)
        t_s0 = P.I("act", "activation", waits=[t_mm, CONST] + ms["g0_w"], out=ms["g0"].t[:, f, 0:N], in_=pb.t[:, 0:N],
                   func=AF.Sigmoid, bias=bgate[:, f:f + 1])
        ms["g0_w"] = []
        pb.used(t_s0)
        pb.close()
        ms["g0_toks"].append(t_s0)
        pb = pf.next()
        wpb = pb.acquire()
        for kt in range(8):
            t_mm = P.I("pe", "matmul", waits=(ms["hT_toks"] + wpb) if kt == 0 else (), sig=(kt == 7),
                       out=pb.t[:, 0:N], lhsT=w_pg[:, kt, 1536 + f * 128:1536 + (f + 1) * 128],
                       rhs=ms["hT"].t[:, kt, 0:N], start=(kt == 0), stop=(kt == 7))
        hT_user_done(ms, t_mm, 20)
        g1 = g1r.next()
        t_s1 = P.I("act", "activation", waits=[t_mm] + g1.acquire(), out=g1.t[:, 0:N], in_=pb.t[:, 0:N],
                   func=AF.Sigmoid, bias=bgate[:, 8 + f:9 + f])
        pb.used(t_s1)
        pb.close()
        yield
        yield
        while len(ms["pm_toks"]) < 4:
            yield
        pb = pf.next()
        wpb = pb.acquire()
        for g in range(4):
            t_mm = P.I("pe", "matmul", waits=(list(ms["pm_toks"].values()) + [W_BP] + wpb) if g == 0 else (), sig=(g == 3),
                       out=pb.t[:, 0:N], lhsT=w_bp_bf[:, g, f * 128:(f + 1) * 128], rhs=ms["pm"].t[:, g, 0:N],
                       start=(g == 0), stop=(g == 3))
        ms["pm"].used(t_mm)
        ms["pm_users"] += 1
        if ms["pm_users"] == 8:
            ms["pm"].close()
        t_gp = P.I("dve", "tensor_tensor", waits=[t_mm, t_s1] + ms["gp_w"], out=ms["gp"].t[:, f, 0:N], in0=pb.t[:, 0:N],
                   in1=g1.t[:, 0:N], op=ALU.mult)
        ms["gp_w"] = []
        pb.used(t_gp)
        pb.close()
        g1.used(t_gp)
        g1.close()
        ms["gp_toks"].append(t_gp)
        if len(ms["gp_toks"]) == 8:
            t_s = P.dma("sp", f"g0st{ms['gslot']}", waits=ms["g0_toks"], out=G0[:, :, qoff:qoff + N].rearrange("f p t -> p f t"), in_=ms["g0"].t[:, :, 0:N])
            ms["g0"].used(t_s)
            ms["g0"].close()
            spill_toks.append(t_s)
            t_s = P.dma("sp", f"gpst{ms['gslot']}", waits=ms["gp_toks"], out=GP[:, :, qoff:qoff + N].rearrange("f p t -> p f t"), in_=ms["gp"].t[:, :, 0:N])
            ms["gp"].used(t_s)
            ms["gp"].close()
            spill_toks.append(t_s)

    gensA2 = []
    nx = lambda mi: (1 if macros[mi][0] == "halo" else 4)
    gensA2 += [taskX(0, 0)]
    for mi in range(len(macros)):
        fm = [taskPool(mi, g) for g in range(4)] + [taskGate(mi, f) for f in range(8)]
        nxt = [taskX(mi + 1, j) for j in range(nx(mi + 1))] if mi + 1 < len(macros) else []
        seq = []
        k = 0
        for i, t in enumerate(fm):
            seq.append(t)
            if i % 3 == 2 and k < len(nxt):
                seq.append(nxt[k])
                k += 1
        seq += nxt[k:]
        gensA2 += seq
    run_tasks(gensA2, 12)
    P.barrier()
    esA2.close()
    esA.close()

    esW = ExitStack()
    w_out_bf = sb(esW, "w_out_bf", [128, 8, D], BF16)
    w_down_bf = sb(esW, "w_down_bf", [128, 22, D], BF16)
    h2T_halo = sb(esW, "h2T_halo", [128, 8, 2], BF16)
    for kt in range(8):
        W_OUT = P.dma("pool", "wc1", out=w_out_bf[:, kt, :], in_=w_out[kt * 128:(kt + 1) * 128, :])
    for c in range(22):
        W_DN = P.dma("pool", "wc3", out=w_down_bf[:, c, :], in_=w_down[c * 128:(c + 1) * 128, :])
    esB = ExitStack()
    attnT = sb(esB, "attnT", [128, 4, NQT], BF16)
    if not do_attn:
        t_at = P.I("pool", "memset", ap=attnT[:], constant=0.0)
        at_toks = [t_at]
    if do_attn:
        at_toks = []
        KTh = Ring([sb(esB, f"KTh{i}", [96, 2 * NTOK], BF16) for i in range(2)])
        QTh = Ring([sb(esB, f"QTh{i}", [96, NQT], BF16) for i in range(2)])
        Vh = Ring([sb(esB, f"Vh{i}", [128, 64, 128], BF16) for i in range(2)])
        Pb = Ring([sb(esB, f"Pb{i}", [128, 3, 512], BF16) for i in range(4)])
        rden = Ring([sb(esB, f"rden{i}", [128, 512], F32) for i in range(2)])
        Sps = Ring([ps(esB, f"Sps{i}", [128, 3, 512], F32) for i in range(2)])
        Ops = Ring([ps(esB, f"Ops{i}", [128, 512], F32) for i in range(2)])
        for r in Vh.bufs:
            r.used(P.I("pool", "memset", ap=r.t[:, :, 64:128], constant=1.0))

        def group_items(gi):
            items = []
            if gi == 0:
                for s in range(15):
                    items += [(2 * s, 0, 128, False, False), (2 * s + 1, 0, 128, False, False)]
                items += [(30, 0, 128, False, True), (31, 0, 128, True, True)]
                return 0, 128, items
            i0 = 2 * (gi - 1)
            for s in range(16 + i0):
                items += [(2 * s, 0, 512, False, False), (2 * s + 1, 0, 512, False, False)]
            s0 = 16 + i0
            s1 = s0 + 1
            items += [(2 * s0, 256, 256, False, False), (2 * s0 + 1, 256, 256, False, False)]
            items += [(2 * s0, 0, 128, True, True), (2 * s0, 128, 128, False, True), (2 * s0 + 1, 128, 128, True, True)]
            items += [(2 * s1, 256, 128, True, True), (2 * s1, 384, 128, False, True), (2 * s1 + 1, 384, 128, True, True)]
            return 128 + i0 * 256, 512, items

        for h in range(NH):
            kb = KTh.next()
            qb = QTh.next()
            vb = Vh.next()
            ld_w = spill_toks if h < 2 else []
            t_lk = P.dma("sp", f"ldk{h % 2}", waits=ld_w + kb.take(), out=kb.t[:], in_=KT[h])
            t_lq = P.dma("sp", f"ldq{h % 2}", waits=qb.take(), out=qb.t[:], in_=QT[h])
            t_lv = P.dma("sp", f"ldv{h % 2}", waits=vb.take(), out=vb.t[:, :, 0:64], in_=VS[h])
            users = []
            for gi in range(9):
                q0, nq, items = group_items(gi)
                batches = []
                for it in items:
                    if batches and len(batches[-1]) < 3 and batches[-1][0][1:3] == it[1:3] and batches[-1][0][4] == it[4]:
                        batches[-1].append(it)
                    else:
                        batches.append([it])
                ob = Ops.next()
                ob_w = ob.take()
                nb = len(batches)
                exp_tok = [None] * nb
                pbuf = [None] * nb
                first_pv = [True]

                def emit_pv(bi):
                    for k, (ktile, qc0, qn, causal, own) in enumerate(batches[bi]):
                        is_last = (bi == nb - 1) and (k == len(batches[bi]) - 1)
                        t = P.I("pe", "matmul", waits=[exp_tok[bi], t_lv] + (ob_w if first_pv[0] else []), sig=True,
                                out=ob.t[:, qc0:qc0 + qn], lhsT=vb.t[:, ktile, :], rhs=pbuf[bi].t[:, k, 0:qn],
                                start=first_pv[0], stop=is_last, skip_group_check=True)
                        first_pv[0] = False
                    pbuf[bi].used(t)
                    return t

                t_pv = None
                for bi, batch in enumerate(batches):
                    sbuf_ = Sps.next()
                    ws = sbuf_.take()
                    for k, (ktile, qc0, qn, causal, own) in enumerate(batch):
                        nr = 96
                        t_qk = P.I("pe", "matmul", waits=([t_lk, t_lq] + ws) if k == 0 else (), sig=True,
                                   out=sbuf_.t[:, k, 0:qn], lhsT=kb.t[0:nr, ktile * 128:(ktile + 1) * 128],
                                   rhs=qb.t[0:nr, q0 + qc0:q0 + qc0 + qn], start=True, stop=not causal)
                        if causal:
                            t_qk = P.I("pe", "matmul", waits=[CONST], sig=True, out=sbuf_.t[:, k, 0:qn], lhsT=ident[:],
                                       rhs=cbias[:, 0:qn], start=False, stop=True)
                    qn = batch[0][2]
                    pb_ = Pb.next()
                    pbuf[bi] = pb_
                    exp_tok[bi] = P.I("act", "activation", waits=[t_qk] + pb_.take(), out=pb_.t[:, 0:len(batch), 0:qn],
                                      in_=sbuf_.t[:, 0:len(batch), 0:qn], func=AF.Exp, scale=HD ** -0.5)
                    sbuf_.used(exp_tok[bi])
                    if bi >= 1:
                        t_pv = emit_pv(bi - 1)
                t_pv = emit_pv(nb - 1)
                users.append(t_pv)
                rd = rden.next()
                t_rc = P.I("dve", "reciprocal", waits=[t_pv] + rd.take(), out=rd.t[64:128, 0:nq], in_=ob.t[64:128, 0:nq])
                po = (h % 2) * 64
                t_at = P.I("dve", "tensor_tensor", waits=[t_rc], out=attnT[po:po + 64, h // 2, q0:q0 + nq],
                           in0=ob.t[0:64, 0:nq], in1=rd.t[64:128, 0:nq], op=ALU.mult)
                ob.used(t_at)
                rd.used(t_at)
                at_toks.append(t_at)
            for u in users:
                kb.used(u)
                qb.used(u)
                vb.used(u)
    t_ats = P.dma("sp", "ats", waits=at_toks, out=AT.rearrange("g p t -> p g t"), in_=attnT[:])
    spill_toks.append(t_ats)
    P.barrier()
    esB.close()

    esC = ExitStack()
    w_ba_bf = sb(esC, "w_ba_bf", [128, 4, D], BF16)
    wup = Ring([sb(esC, f"wup{i}", [128, 8, 2, 256], BF16) for i in range(3)])
    G0l = Buf(sb(esC, "G0l", [128, 8, 512], BF16))
    GPl = Buf(sb(esC, "GPl", [128, 8, 512], BF16))
    attl = Buf(sb(esC, "attl", [128, 4, 512], BF16))
    tmpm = Ring([sb(esC, f"tmpm{i}", [128, 512], F32) for i in range(2)])
    x1 = Buf(sb(esC, "x1", [128, 4, D], F32))
    h2b = Ring([sb(esC, f"h2b{i}", [128, D], BF16) for i in range(2)])
    h2T = Buf(sb(esC, "h2T", [128, 8, 512], BF16))
    zb = Ring([sb(esC, f"zb{i}", [128, 2 + 512], F32) for i in range(2)])
    cv = Ring([sb(esC, f"cv{i}", [128, 512], F32) for i in range(3)])
    sg = Ring([sb(esC, f"sg{i}", [128, 512], F32) for i in range(2)])
    aT = Buf(sb(esC, "aT", [128, 22, 512], BF16))
    ob_ = Ring([sb(esC, f"ob{i}", [128, D], F32) for i in range(2)])
    pA = Ring([ps(esC, f"pA{i}", [128, 512], F32) for i in range(2)])
    pY = Ring([ps(esC, f"pY{i}", [128, 512], F32) for i in range(2)])
    trc = Ring([ps(esC, f"trc{i}", [128, 8, 128], BF16) for i in range(1)])
    pz = Ring([ps(esC, f"pz{i}", [128, 512], F32) for i in range(3)])

    W_BA = P.dma("pool", "wc0", out=w_ba_bf[:], in_=w_ba.rearrange("(g p) n -> p g n", p=128))
    w_up_v = w_up.rearrange("(kt p) n -> p kt n", p=128)

    chunk_tab = {}
    N_CHUNKS = 8 * 11
    chunk_ctr = [0]

    def issue_chunk(idx):
        if idx in chunk_tab or idx >= N_CHUNKS:
            return
        c2 = idx % 11
        wb = wup.next()
        key = f"wup{(wup.i - 1) % 3}"
        ww = wb.take()
        P.dma("pool", key, waits=ww, out=wb.t[:, :, 0, :], in_=w_up_v[:, :, c2 * 256:(c2 + 1) * 256])
        tk = P.dma("pool", key, out=wb.t[:, :, 1, :], in_=w_up_v[:, :, DFF + c2 * 256:DFF + (c2 + 1) * 256])
        chunk_tab[idx] = (wb, tk)

    def load_wup(c2):
        idx = chunk_ctr[0]
        chunk_ctr[0] += 1
        assert idx % 11 == c2
        issue_chunk(idx)
        issue_chunk(idx + 1)
        issue_chunk(idx + 2)
        return chunk_tab[idx]
    carry_tok = [None] * 44
    out_toks = []

    c_loads = {}
    halo_tok = [None]

    def issue_c_loads(kind, m):
        N = 128 if kind == "halo" else 512
        qoff = 0 if kind == "halo" else 128 + m * 512
        t_g0 = P.dma("sp", "ldg0", waits=spill_toks + G0l.take(), out=G0l.t[:, :, 0:N], in_=G0[:, :, qoff:qoff + N].rearrange("f p t -> p f t"))
        t_gp = P.dma("sp", "ldgp", waits=GPl.take(), out=GPl.t[:, :, 0:N], in_=GP[:, :, qoff:qoff + N].rearrange("f p t -> p f t"))
        t_al = P.dma("sp", "ldat", waits=attl.take(), out=attl.t[:, :, 0:N], in_=AT[:, :, qoff:qoff + N].rearrange("g p t -> p g t"))
        c_loads[(kind, m)] = (t_g0, t_gp, t_al)

    def phaseC_macro(kind, m):
        nt = 1 if kind == "halo" else 4
        N = nt * 128
        qoff = 0 if kind == "halo" else 128 + m * 512
        src = x_pre if kind == "halo" else x_own
        tok0 = 3968 if kind == "halo" else m * 512
        if (kind, m) not in c_loads:
            issue_c_loads(kind, m)
        t_g0, t_gp, t_al = c_loads[(kind, m)]
        t_x = P.dma("sp", "ldx1", waits=x1.take(), out=x1.t[:, 0:nt, :], in_=src[tok0:tok0 + N, :].rearrange("(j p) d -> p j d", p=128))
        mix_toks = []
        for f in range(8):
            pb = pA.next()
            wpb = pb.take()
            for g in range(4):
                t_mm = P.I("pe", "matmul", waits=([W_BA, t_al] + wpb) if g == 0 else (), sig=(g == 3), out=pb.t[:, 0:N],
                           lhsT=w_ba_bf[:, g, f * 128:(f + 1) * 128], rhs=attl.t[:, g, 0:N], start=(g == 0), stop=(g == 3))
            tm = tmpm.next()
            t_1 = P.I("dve", "tensor_tensor", waits=[t_mm, t_g0] + tm.take(), out=tm.t[:, 0:N], in0=pb.t[:, 0:N], in1=G0l.t[:, f, 0:N], op=ALU.mult)
            pb.used(t_1)
            t_2 = P.I("dve", "tensor_tensor", waits=[t_1, t_gp], out=GPl.t[:, f, 0:N], in0=tm.t[:, 0:N],
                      in1=GPl.t[:, f, 0:N], op=ALU.add)
            tm.used(t_2)
            mix_toks.append(t_2)
        attl.used(t_mm)
        G0l.used(mix_toks[-1])
        h2T_w = h2T.take()
        h2T_toks = []
        x1_toks = [None] * nt
        h2T_first = [True]
        last_h = [None]

        def c1_tile(j):
            xs = []
            for c in range(2):
                pb = pY.next()
                wpb = pb.take()
                for kt in range(8):
                    t_mm = P.I("pe", "matmul", waits=(mix_toks + [W_OUT] + wpb) if kt == 0 else (), sig=(kt == 7), out=pb.t[:],
                               lhsT=GPl.t[:, kt, j * 128:(j + 1) * 128], rhs=w_out_bf[:, kt, c * 512:(c + 1) * 512],
                               start=(kt == 0), stop=(kt == 7))
                GPl.used(t_mm)
                t_a = P.I("dve", "tensor_tensor", waits=[t_mm, t_x], out=x1.t[:, j, c * 512:(c + 1) * 512], in0=pb.t[:],
                          in1=x1.t[:, j, c * 512:(c + 1) * 512], op=ALU.add)
                pb.used(t_a)
                xs.append(t_a)
            x1_toks[j] = xs
            t_rs, ni = rmsnorm_tile(x1.t[:, j, :], gffn, None, xs)
            yield
            yield
            hb = h2b.next()
            t_h = P.I("dve", "scalar_tensor_tensor", waits=[t_rs, CONST] + xs + hb.take(), out=hb.t[:], in0=x1.t[:, j, :],
                      scalar=rs_all[:, ni:ni + 1], in1=gffn[:], op0=ALU.mult, op1=ALU.mult)
            last_h[0] = t_h
            tr = trc.next()
            wtr = tr.take()
            for kt in range(8):
                t_tr = P.I("pe", "transpose", waits=([t_h, CONST] + wtr) if kt == 0 else (), sig=(kt == 7), out=tr.t[:, kt, :],
                           in_=hb.t[:, kt * 128:(kt + 1) * 128], identity=ident[:])
            hb.used(t_tr)
            t_cp = P.I("act", "activation", waits=[t_tr] + (h2T_w if h2T_first[0] else []), out=h2T.t[:, :, j * 128:(j + 1) * 128], in_=tr.t[:], func=AF.Copy)
            h2T_first[0] = False
            tr.used(t_cp)
            h2T_toks.append(t_cp)

        run_tasks([c1_tile(j) for j in range(nt)], 4)
        t_h = last_h[0]
        nxt = c_order[c_order.index((kind, m)) + 1] if c_order.index((kind, m)) + 1 < len(c_order) else None
        if nxt is not None:
            issue_c_loads(*nxt)
        h2T_users = []
        if kind == "halo":
            t_hh = P.I("dve", "tensor_copy", waits=h2T_toks, out=h2T_halo[:], in_=h2T.t[:, :, 126:128])
            halo_tok[0] = t_hh
            h2T.used(t_hh)
            x1.used(t_h)
            return
        aT_w = aT.take()
        aT_toks = []
        for c2 in range(11):
            wb, t_w = load_wup(c2)
            for ci in range(2):
                c = 2 * c2 + ci
                ups = []
                for half in range(2):
                    cc = c + 22 * half
                    if m == 0:
                        pb = pz.next()
                        wpb = pb.take()
                        for kt in range(8):
                            t_mm = P.I("pe", "matmul", waits=([halo_tok[0], t_w] + wpb) if kt == 0 else (), sig=(kt == 7), out=pb.t[:, 0:2],
                                       lhsT=wb.t[:, kt, half, ci * 128:(ci + 1) * 128], rhs=h2T_halo[:, kt, :], start=(kt == 0), stop=(kt == 7))
                        carry_tok[cc] = P.I("dve", "tensor_scalar", waits=[t_mm, CONST], out=carry[:, cc, :], in0=pb.t[:, 0:2], scalar1=hflag[:, 0:1],
                                            scalar2=None, op0=ALU.mult)
                        pb.used(carry_tok[cc])
                    pb = pz.next()
                    wpb = pb.take()
                    for kt in range(8):
                        t_mm = P.I("pe", "matmul", waits=(h2T_toks + [t_w] + wpb) if kt == 0 else (), sig=(kt == 7), out=pb.t[:],
                                   lhsT=wb.t[:, kt, half, ci * 128:(ci + 1) * 128], rhs=h2T.t[:, kt, :], start=(kt == 0), stop=(kt == 7))
                    h2T_users.append(t_mm)
                    z = zb.next()
                    wz = z.take()
                    t_zc = P.I("dve", "tensor_copy", waits=[carry_tok[cc]] + wz, out=z.t[:, 0:2], in_=carry[:, cc, :])
                    t_z = P.I("act", "activation", waits=[t_mm] + wz, out=z.t[:, 2:514], in_=pb.t[:], func=AF.Copy)
                    v_ = cv.next()
                    t_c2 = P.I("act", "activation", waits=[t_mm, CONST] + v_.take(), out=v_.t[:], in_=pb.t[:], func=AF.Identity,
                               scale=convw[:, 2, cc:cc + 1], bias=convb[:, cc:cc + 1])
                    pb.used(t_z)
                    pb.used(t_c2)
                    carry_tok[cc] = P.I("dve", "tensor_copy", waits=[t_z, t_zc], out=carry[:, cc, :], in_=z.t[:, 512:514])
                    t_c1 = P.I("dve", "scalar_tensor_tensor", waits=[t_z, t_zc, t_c2], out=v_.t[:], in0=z.t[:, 1:513], scalar=convw[:, 1, cc:cc + 1],
                               in1=v_.t[:], op0=ALU.mult, op1=ALU.add)
                    t_c0 = P.I("dve", "scalar_tensor_tensor", waits=[t_c1], out=v_.t[:], in0=z.t[:, 0:512], scalar=convw[:, 0, cc:cc + 1],
                               in1=v_.t[:], op0=ALU.mult, op1=ALU.add)
                    z.used(t_c0)
                    z.used(carry_tok[cc])
                    ups.append((v_, t_c0))
                s_ = sg.next()
                t_si = P.I("act", "activation", waits=[ups[0][1]] + s_.take(), out=s_.t[:], in_=ups[0][0].t[:], func=AF.Silu)
                ups[0][0].used(t_si)
                t_a = P.I("pool", "tensor_tensor", waits=[t_si, ups[1][1]] + (aT_w if c == 0 else []), out=aT.t[:, c, :], in0=s_.t[:], in1=ups[1][0].t[:], op=ALU.mult)
                s_.used(t_a)
                ups[1][0].used(t_a)
                aT_toks.append(t_a)
            wb.used(t_mm)
        for t in h2T_users:
            h2T.used(t)
        aT_users = []
        for j in range(4):
            o = ob_.next()
            wo = o.take()
            for c2 in range(2):
                pb = pY.next()
                wpb = pb.take()
                for c in range(22):
                    t_mm = P.I("pe", "matmul", waits=(aT_toks + [W_DN] + wpb) if c == 0 else (), sig=(c == 21), out=pb.t[:],
                               lhsT=aT.t[:, c, j * 128:(j + 1) * 128], rhs=w_down_bf[:, c, c2 * 512:(c2 + 1) * 512], start=(c == 0), stop=(c == 21))
                aT_users.append(t_mm)
                t_o = P.I("dve", "tensor_tensor", waits=[t_mm] + x1_toks[j] + (wo if c2 == 0 else []), out=o.t[:, c2 * 512:(c2 + 1) * 512], in0=pb.t[:],
                          in1=x1.t[:, j, c2 * 512:(c2 + 1) * 512], op=ALU.add)
                pb.used(t_o)
            t_st = P.dma("sp", f"sty{(ob_.i - 1) % 2}", waits=[t_o], out=y[tok0 + j * 128: tok0 + (j + 1) * 128, :], in_=o.t[:])
            o.used(t_st)
            out_toks.append(t_st)
        x1.used(t_o)
        for t in aT_users:
            aT.used(t)

    c_order = [("halo", 0)] + [("own", m) for m in range(8)]
    for (kind_, m_) in c_order:
        phaseC_macro(kind_, m_)
    P.final_wait("sp", out_toks)

    with ExitStack() as es:
        sems = {k: es.enter_context(nc.semaphore(f"s_{k}")) for k in P.sem_keys()}
        block = es.enter_context(nc.Block())

        def replay(eng_name, E):
            for (waits, method, kw, inc) in P.q[eng_name]:
                for (k, v) in waits:
                    E.wait_ge(sems[k], v)
                if method is None:
                    continue
                ins = getattr(E, method)(**kw)
                if inc is not None:
                    ins.then_inc(sems[inc[0]], inc[1])

        @block.sync
        def _(E):
            replay("sp", E)

        @block.scalar
        def _(E):
            replay("act", E)

        @block.vector
        def _(E):
            replay("dve", E)

        @block.gpsimd
        def _(E):
            replay("pool", E)

        @block.tensor
        def _(E):
            replay("pe", E)
    esC.close()
    esW.close()
    es_all.close()
    return nc


DO_ATTN = True
_CACHE = {}


def _host_consts(half):
    c = {}
    ident = np.eye(128, dtype=np.float32).astype(ml_dtypes.bfloat16)
    kk = np.arange(128)[:, None]
    qq = np.arange(128)[None, :]
    cb = np.where(kk <= qq, 0.0, -BIG).astype(np.float32).astype(ml_dtypes.bfloat16)
    c["ident_bf"] = ident
    c["cbias_bf"] = cb
    hd = 8
    inv_freq = (np.float32(500000.0) ** (-np.arange(hd, dtype=np.float32) / np.float32(hd))).astype(np.float32)
    pos = np.concatenate([np.arange(4096), half * 4096 + np.arange(4096)]).astype(np.float32)
    ang = (pos[:, None] * inv_freq[None, :]).astype(np.float32)
    cos = np.cos(ang).astype(np.float32).reshape(64, 128, 8).transpose(1, 0, 2)
    sin = np.sin(ang).astype(np.float32).reshape(64, 128, 8).transpose(1, 0, 2)
    c["cos_t"] = np.ascontiguousarray(cos)
    c["sin_t"] = np.ascontiguousarray(sin)
    rb = np.full((33, 32), -BIG, dtype=np.float32)
    if half == 1:
        rb[0, 0:15] = 0.0
    for t in range(32):
        blk = t // 2
        if half == 1:
            rb[1 + t, 0:16] = 0.0
        rb[1 + t, 16:16 + blk] = 0.0
    c["rbias"] = np.ascontiguousarray(np.broadcast_to(rb[None], (128, 33, 32)))
    ic = np.zeros((4, 16), dtype=np.float32)
    for g, w in enumerate((2, 4, 8, 16)):
        tg = half * 4096 + np.arange(16)
        ic[g] = 1.0 / np.minimum(tg + 1.0, float(w))
    c["invcnt"] = np.ascontiguousarray(np.broadcast_to(ic[None], (128, 4, 16)))
    c["hflag"] = np.full((128, 1), float(half), dtype=np.float32)
    return c


def kernel(x, norm_mix_g, w_in, b_gate, q_norm_g, k_norm_g, w_pool, pool_scale,
           w_branch_attn, w_branch_pool, w_out, norm_ffn_g, w_up, conv_w, conv_b, w_down):
    f = lambda a: np.ascontiguousarray(np.asarray(a, dtype=np.float32))
    x = f(x)
    shared = {
        "w_in": f(w_in[0]), "w_pool": f(w_pool[0]), "w_ba": f(w_branch_attn[0]), "w_bp": f(w_branch_pool[0]),
        "w_out": f(w_out[0]), "w_up": f(w_up[0]), "w_down": f(w_down[0]),
        "gmix_b": f(np.broadcast_to(np.asarray(norm_mix_g[0])[None, :], (128, D))),
        "gffn_b": f(np.broadcast_to(np.asarray(norm_ffn_g[0])[None, :], (128, D))),
        "gq_b": f(np.broadcast_to(np.tile(np.asarray(q_norm_g[0]), 8)[None, :], (128, 512))),
        "gk_b": f(np.broadcast_to(np.tile(np.asarray(k_norm_g[0]), 8)[None, :], (128, 512))),
        "bgate_t": f(np.asarray(b_gate[0]).reshape(16, 128).T),
        "pscale_t": f(np.asarray(pool_scale[0]).reshape(4, 128).T),
        "convw_t": f(np.asarray(conv_w[0]).reshape(3, 44, 128).transpose(2, 0, 1)),
        "convb_t": f(np.asarray(conv_b[0]).reshape(44, 128).T),
    }
    if "nc" not in _CACHE:
        _CACHE["nc"] = build_program(DO_ATTN)
    nc = _CACHE["nc"]
    in_maps = []
    zeros = np.zeros((NTOK, D), dtype=np.float32)
    for c in range(8):
        b, half = c // 2, c % 2
        m = dict(shared)
        m["x_own"] = np.ascontiguousarray(x[b, half * NTOK:(half + 1) * NTOK])
        m["x_pre"] = np.ascontiguousarray(x[b, 0:NTOK]) if half == 1 else zeros
        m.update(_host_consts(half))
        in_maps.append(m)
    res = run_bass_kernel_spmd(nc, in_maps, core_ids=list(range(8)))
    out = np.empty((4, 2 * NTOK, D), dtype=np.float32)
    for c in range(8):
        b, half = c // 2, c % 2
        out[b, half * NTOK:(half + 1) * NTOK] = res.results[c]["y"]
    return out
```

```python
import numpy as np
import ml_dtypes
from contextlib import ExitStack
import concourse.bass as bass
import concourse.mybir as mybir
from concourse.bass_utils import run_bass_kernel_spmd

F32 = mybir.dt.float32
BF16 = mybir.dt.bfloat16
AF = mybir.ActivationFunctionType
ALU = mybir.AluOpType
AX = mybir.AxisListType

D = 1024
NTOK = 4096
NH = 8
HD = 64
DFF = 2816
EPS = 1e-6
BIG = 30000.0
NQT = NTOK + 128
ENGS = ["pe", "act", "dve", "pool", "sp"]


class Prog:
    def __init__(self):
        self.q = {e: [] for e in ENGS}
        self.cnt = {}
        self.seen = {e: {} for e in ENGS}
        self.pending = {e: [] for e in ENGS}

    def _filter(self, eng, waits):
        mx = {}
        for w in list(self.pending[eng]) + list(waits):
            if w is None:
                continue
            k, v = w
            if v > mx.get(k, 0):
                mx[k] = v
        out = []
        for k, v in mx.items():
            if self.seen[eng].get(k, 0) >= v:
                continue
            self.seen[eng][k] = v
            out.append((k, v))
        self.pending[eng] = []
        return out

    def I(self, eng, method, waits=(), sig=True, **kw):
        w = self._filter(eng, waits)
        inc = None
        tok = None
        if sig:
            self.cnt[eng] = self.cnt.get(eng, 0) + 1
            inc = (eng, 1)
            tok = (eng, self.cnt[eng])
        self.q[eng].append((w, method, kw, inc))
        return tok

    def dma(self, queue, semkey, waits=(), **kw):
        w = self._filter(queue, waits)
        self.cnt[semkey] = self.cnt.get(semkey, 0) + 16
        self.q[queue].append((w, "dma_start", kw, (semkey, 16)))
        return (semkey, self.cnt[semkey])

    def barrier(self):
        toks = [(k, v) for k, v in self.cnt.items()]
        for e in ENGS:
            self.pending[e] = list(toks)

    def final_wait(self, eng, toks):
        w = self._filter(eng, toks)
        self.q[eng].append((w, None, None, None))

    def sem_keys(self):
        return list(self.cnt.keys())


class Buf:
    def __init__(self, t):
        self.t = t
        self.free = []
        self.closed = True

    def acquire(self):
        assert self.closed, "buffer re-acquired while a previous user is still emitting"
        self.closed = False
        return self.take()

    def close(self):
        self.closed = True

    def take(self):
        f = self.free
        self.free = []
        return f

    def used(self, tok):
        if tok is not None:
            self.free.append(tok)


class Ring:
    def __init__(self, bufs):
        self.bufs = [Buf(b) for b in bufs]
        self.i = 0

    def next(self):
        b = self.bufs[self.i % len(self.bufs)]
        self.i += 1
        return b


def run_tasks(gens, width):
    active = []
    it = iter(gens)
    exhausted = False
    while True:
        if not exhausted and len(active) < width:
            g = next(it, None)
            if g is None:
                exhausted = True
            else:
                active.append(g)
        if not active and exhausted:
            break
        for g in list(active):
            try:
                next(g)
            except StopIteration:
                active.remove(g)


def build_program(do_attn=True):
    nc = bass.Bass("TRN2", target_bir_lowering=False)
    P = Prog()

    def din(name, shape, dt=F32):
        return nc.dram_tensor(name, list(shape), dt, kind="ExternalInput").ap()

    x_own = din("x_own", [NTOK, D])
    x_pre = din("x_pre", [NTOK, D])
    w_in = din("w_in", [D, 4096])
    w_pool = din("w_pool", [4, 128, 128])
    w_ba = din("w_ba", [512, D])
    w_bp = din("w_bp", [512, D])
    w_out = din("w_out", [D, D])
    w_up = din("w_up", [D, 2 * DFF])
    w_down = din("w_down", [DFF, D])
    gmix_b = din("gmix_b", [128, D])
    gffn_b = din("gffn_b", [128, D])
    gq_b = din("gq_b", [128, 512])
    gk_b = din("gk_b", [128, 512])
    bgate_t = din("bgate_t", [128, 16])
    pscale_t = din("pscale_t", [128, 4])
    convw_t = din("convw_t", [128, 3, 44])
    convb_t = din("convb_t", [128, 44])
    ident_d = din("ident_bf", [128, 128], BF16)
    cbias_d = din("cbias_bf", [128, 128], BF16)
    cos_d = din("cos_t", [128, 64, 16])
    sin_d = din("sin_t", [128, 64, 16])
    gmixt_d = din("gmix_t", [128, 8])
    rbias_d = din("rbias", [128, 33, 32])
    invcnt_d = din("invcnt", [128, 4, 16])
    hflag_d = din("hflag", [128, 1])
    y = nc.dram_tensor("y", [NTOK, D], F32, kind="ExternalOutput").ap()

    KT = nc.dram_tensor("KT_s", [NH, 96, 2 * NTOK], BF16).ap()
    VS = nc.dram_tensor("VS_s", [NH, 128, 64, 64], BF16).ap()
    QT = nc.dram_tensor("QT_s", [NH, 96, NQT], BF16).ap()
    G0 = nc.dram_tensor("G0_s", [8, 128, NQT], BF16).ap()
    GP = nc.dram_tensor("GP_s", [8, 128, NQT], BF16).ap()
    AT = nc.dram_tensor("AT_s", [4, 128, NQT], BF16).ap()

    es_all = ExitStack()
    esA = ExitStack()

    def sb(es, name, shape, dt):
        return es.enter_context(nc.sbuf_tensor("sb_" + name, list(shape), dt))

    def ps(es, name, shape, dt):
        return es.enter_context(nc.psum_tensor("ps_" + name, list(shape), dt))

    gffn = sb(es_all, "gffn", [128, D], F32)
    bgate = sb(es_all, "bgate", [128, 16], F32)
    pscale = sb(es_all, "pscale", [128, 4], F32)
    convw = sb(es_all, "convw", [128, 3, 44], F32)
    convb = sb(es_all, "convb", [128, 44], F32)
    ident = sb(es_all, "ident", [128, 128], BF16)
    cbias = sb(es_all, "cbias", [128, 128], BF16)
    hflag = sb(es_all, "hflag", [128, 1], F32)
    neghalf = sb(es_all, "neghalf", [128, 8], F32)
    ones_f = sb(es_all, "ones_f", [128, 64], F32)
    ssq_all = sb(es_all, "ssq_all", [128, 160], F32)
    rs_all = sb(es_all, "rs_all", [128, 160], F32)
    junk = sb(es_all, "junk", [128, D], F32)
    carry = sb(es_all, "carry", [128, 44, 2], F32)
    gmix = sb(esA, "gmix", [128, D], F32)
    gq = sb(esA, "gq", [128, 512], F32)
    gk = sb(esA, "gk", [128, 512], F32)
    cos_t = sb(esA, "cos_t", [128, 64, 16], F32)
    sin_t = sb(esA, "sin_t", [128, 64, 16], F32)
    gmix_t = sb(esA, "gmix_t", [128, 8], F32)
    rbias = sb(esA, "rbias", [128, 33, 32], F32)
    invcnt = sb(esA, "invcnt", [128, 4, 16], F32)

    c_toks = []
    for dst, src in [(gmix, gmix_b), (gffn, gffn_b), (gq, gq_b), (gk, gk_b), (bgate, bgate_t),
                     (pscale, pscale_t), (convw, convw_t), (convb, convb_t), (ident, ident_d),
                     (cbias, cbias_d), (cos_t, cos_d), (sin_t, sin_d), (rbias, rbias_d),
                     (invcnt, invcnt_d), (hflag, hflag_d), (gmix_t, gmixt_d)]:
        c_toks.append(P.dma("sp", "const", out=dst[:], in_=src))
    CONST = c_toks[-1]
    t_nh = P.I("pool", "memset", ap=neghalf[:], constant=-0.5)
    t_ones = P.I("pool", "memset", ap=ones_f[:], constant=1.0)
    norm_idx = [0]
    junk_tok = [None]

    def rmsnorm_tile(x_ap, g_t, out_bf, waits):
        i = norm_idx[0]
        norm_idx[0] += 1
        t1 = P.I("act", "activation", waits=list(waits) + [junk_tok[0]], out=junk[:], in_=x_ap, func=AF.Square,
                 accum_out=ssq_all[:, i:i + 1])
        junk_tok[0] = t1
        t2 = P.I("pool", "tensor_scalar", waits=[t1, t_nh], out=rs_all[:, i:i + 1], in0=ssq_all[:, i:i + 1],
                 scalar1=1.0 / D, scalar2=EPS, op0=ALU.mult, op1=ALU.add)
        t3 = P.I("pool", "tensor_tensor", waits=[t2], out=rs_all[:, i:i + 1], in0=rs_all[:, i:i + 1],
                 in1=neghalf[:, 0:1], op=ALU.pow)
        return t3, i

    spill_toks = []
    w_in_v = w_in.rearrange("(kt p) n -> p kt n", p=128)
    xr = Ring([sb(esA, f"xa{i}", [128, D], F32) for i in range(6)])
    hbr = Ring([sb(esA, f"hb{i}", [128, D], BF16) for i in range(3)])

    def x_chain(src, row0, trx_ring, get_dst):
        xb = xr.next()
        w = xb.acquire()
        t_ld = P.dma("sp", f"xa{(xr.i - 1) % len(xr.bufs)}", waits=w, out=xb.t[:], in_=src[row0:row0 + 128, :])
        yield
        i = norm_idx[0]
        norm_idx[0] += 1
        t1 = P.I("act", "activation", waits=[t_ld, junk_tok[0]], out=junk[:], in_=xb.t[:], func=AF.Square, accum_out=ssq_all[:, i:i + 1])
        junk_tok[0] = t1
        yield
        t2 = P.I("pool", "tensor_scalar", waits=[t1, t_nh], out=rs_all[:, i:i + 1], in0=ssq_all[:, i:i + 1],
                 scalar1=1.0 / D, scalar2=EPS, op0=ALU.mult, op1=ALU.add)
        t_rs = P.I("pool", "tensor_tensor", waits=[t2], out=rs_all[:, i:i + 1], in0=rs_all[:, i:i + 1],
                   in1=neghalf[:, 0:1], op=ALU.pow)
        hb = hbr.next()
        w = hb.acquire()
        t_h = P.I("pool", "tensor_scalar", waits=[t_rs, t_ld] + w, out=hb.t[:], in0=xb.t[:],
                  scalar1=rs_all[:, i:i + 1], scalar2=1.0, op0=ALU.mult, op1=ALU.mult)
        xb.used(t_h)
        xb.close()
        yield
        tr = trx_ring.next()
        wtr = tr.acquire()
        for kt in range(8):
            t_tr = P.I("pe", "transpose", waits=([t_h, CONST] + wtr) if kt == 0 else (), sig=(kt == 7),
                       out=tr.t[:, kt, :], in_=hb.t[:, kt * 128:(kt + 1) * 128], identity=ident[:])
        hb.used(t_tr)
        hb.close()
        dst_ap, dst_waits = get_dst()
        t_cp = P.I("act", "activation", waits=[t_tr] + dst_waits, out=dst_ap, in_=tr.t[:], func=AF.Copy)
        tr.used(t_cp)
        tr.close()
        return t_cp

    if do_attn:
        esA1 = ExitStack()
        w_qkv = sb(esA1, "w_qkv", [128, 8, 1536], BF16)
        hTt = Ring([sb(esA1, f"hTt{i}", [128, 8, 128], BF16) for i in range(3)])
        Kaug = Ring([sb(esA1, f"Kaug{i}", [128, 8, 96], BF16) for i in range(4)])
        Qaug = Ring([sb(esA1, f"Qaug{i}", [128, 8, 96], BF16) for i in range(7)])
        KTst = Ring([sb(esA1, f"KTst{i}", [96, 8, 512], BF16) for i in range(2)])
        QTst = Ring([sb(esA1, f"QTst{i}", [96, 8, 512], BF16) for i in range(2)])
        Vst = Ring([sb(esA1, f"Vst{i}", [128, 8, 4, 64], BF16) for i in range(2)])
        tq = Ring([sb(esA1, f"tq{i}", [128, 8, 64], F32) for i in range(10)])
        sqb = Ring([sb(esA1, f"sqb{i}", [128, 8, 64], F32) for i in range(6)])
        ssqh = sb(esA1, "ssqh", [128, 100, 8], F32)
        rp = Ring([sb(esA1, f"rp{i}", [128, 2, 8, 16], F32) for i in range(4)])
        kmT = sb(esA1, "kmT", [64, 8, 32], BF16)
        kms = sb(esA1, "kms", [64, 8], F32)
        sbr = Ring([sb(esA1, f"sbr{i}", [128, 8, 32], F32) for i in range(3)])
        m8 = Ring([sb(esA1, f"m8_{i}", [128, 8, 8], F32) for i in range(2)])
        thr = Ring([sb(esA1, f"thr{i}", [128, 8], F32) for i in range(2)])
        ger = Ring([sb(esA1, f"ge{i}", [128, 8, 32], F32) for i in range(2)])
        trx1 = Ring([ps(esA1, f"trx1_{i}", [128, 8, 128], BF16) for i in range(2)])
        pq = Ring([ps(esA1, f"pq{i}", [128, 512], F32) for i in range(3)])
        trk = Ring([ps(esA1, f"trk{i}", [128, 8, 128], BF16) for i in range(2)])
        prt = Ring([ps(esA1, "prt", [128, 8, 32], F32)])

        WK = None
        for (c0, c1) in [(512, 1536), (0, 512)]:
            for kt in range(8):
                tk = P.dma("pool", f"w{c0}", out=w_qkv[:, kt, c0:c1], in_=w_in_v[:, kt, c0:c1])
            if c0 == 512:
                WKV = tk
            else:
                WQ = tk
        for kt in range(8):
            t_f = P.I("dve", "tensor_scalar", waits=[WKV, WQ, CONST], out=w_qkv[:, kt, :], in0=w_qkv[:, kt, :],
                      scalar1=gmix_t[:, kt:kt + 1], scalar2=None, op0=ALU.mult)
        WKV = t_f
        WQ = t_f
        t_km0 = P.I("pool", "memset", ap=kmT[:], constant=0.0)
        kmT_tok = [t_km0]
        qk_idx = [0]
        kst_state, vst_state, qst_state = {}, {}, {}

        def qk_evac(pb, t_mm, g_t):
            pqv = pb.t[:].rearrange("p (h d) -> p h d", h=8)
            sq = sqb.next()
            t_sq = P.I("act", "activation", waits=[t_mm] + sq.acquire(), out=sq.t[:], in_=pqv, func=AF.Square)
            t = tq.next()
            t_g = P.I("dve", "tensor_tensor", waits=[t_mm, t_sq, CONST] + t.acquire(), out=t.t[:], in0=pqv,
                      in1=g_t[:].rearrange("p (h d) -> p h d", h=8), op=ALU.mult)
            pb.used(t_g)
            pb.used(t_sq)
            pb.close()
            return {"sq": sq, "t_sq": t_sq, "t": t, "t_g": t_g}

        def qk_reduce(st):
            i = qk_idx[0]
            qk_idx[0] += 1
            st["i"] = i
            t_red = P.I("dve", "tensor_reduce", waits=[st["t_sq"]], out=ssqh[:, i, :], in_=st["sq"].t[:], axis=AX.X, op=ALU.add)
            st["sq"].used(t_red)
            st["sq"].close()
            st["t_red"] = t_red

        def qk_pow(st):
            i = st["i"]
            t_r1 = P.I("pool", "tensor_scalar", waits=[st["t_red"], t_nh], out=ssqh[:, i, :], in0=ssqh[:, i, :],
                       scalar1=1.0 / HD, scalar2=EPS, op0=ALU.mult, op1=ALU.add)
            st["t_r2"] = P.I("pool", "tensor_tensor", waits=[t_r1], out=ssqh[:, i, :], in0=ssqh[:, i, :],
                             in1=neghalf[:], op=ALU.pow)

        def qk_rope(st, tile_idx, aug):
            t, t_g, t_r2, i = st["t"], st["t_g"], st["t_r2"], st["i"]
            r = rp.next()
            ccb = cos_t[:, tile_idx, :].unsqueeze(1).to_broadcast([128, 8, 16])
            ssb = sin_t[:, tile_idx, :].unsqueeze(1)
            x16 = t.t[:, :, 0:16]
            x1 = t.t[:, :, 0:8]
            x2 = t.t[:, :, 8:16]
            wr = r.acquire()
            ta = P.I("dve", "tensor_tensor", waits=[t_g] + wr, out=r.t[:, 0], in0=x16, in1=ccb, op=ALU.mult)
            tb = P.I("dve", "tensor_tensor", waits=[t_g], out=r.t[:, 1, :, 0:8], in0=x2,
                     in1=ssb[:, :, 0:8].to_broadcast([128, 8, 8]), op=ALU.mult)
            tc = P.I("dve", "tensor_tensor", waits=[t_g], out=r.t[:, 1, :, 8:16], in0=x1,
                     in1=ssb[:, :, 8:16].to_broadcast([128, 8, 8]), op=ALU.mult)
            te = P.I("dve", "tensor_tensor", waits=[ta, tb, tc], out=x16, in0=r.t[:, 0], in1=r.t[:, 1], op=ALU.add)
            tf = te
            r.used(te)
            r.close()
            t_fin = P.I("dve", "tensor_tensor", waits=[te, tf, t_r2] + aug.acquire(), out=aug.t[:, :, 0:64], in0=t.t[:],
                        in1=ssqh[:, i, :].unsqueeze(2).to_broadcast([128, 8, 64]), op=ALU.mult)
            t.used(t_fin)
            t.close()
            return t_fin

        def stage_get(state, key, ring):
            if key not in state:
                b = ring.next()
                state[key] = {"buf": b, "w": b.acquire(), "toks": [], "n": 0, "slot": (ring.i - 1) % len(ring.bufs)}
            return state[key]

        def taskA1(kind, idx):
            if kind == "pre":
                src, row0, ktile, groups, mkey, j = x_pre, idx * 128, idx, ["k", "v"], ("pre", idx // 4), idx % 4
            elif kind == "halo":
                src, row0, ktile, groups, mkey, j = x_pre, 3968, 31, ["q"], ("halo", 0), 0
            else:
                src, row0, ktile, groups, mkey, j = x_own, (idx - 32) * 128, idx, ["q", "k", "v"], ("own", (idx - 32) // 4), idx % 4
            nt_macro = 1 if kind == "halo" else 4
            qtile = 0 if kind == "halo" else 1 + (idx - 32)
            slot = ktile // 2
            hb_box = []

            def get_dst():
                hb_ = hTt.next()
                hb_box.append(hb_)
                return hb_.t[:], hb_.acquire()

            t_hT = yield from x_chain(src, row0, trx1, get_dst)
            hb_ = hb_box[0]
            yield
            col0 = {"q": 0, "k": 512, "v": 1024}
            st = {}
            for gname in groups:
                pb = pq.next()
                wpb = pb.acquire()
                for kt in range(8):
                    t_mm = P.I("pe", "matmul", waits=([t_hT, WKV if gname != "q" else WQ] + wpb) if kt == 0 else (), sig=(kt == 7),
                               out=pb.t[:], lhsT=hb_.t[:, kt, :], rhs=w_qkv[:, kt, col0[gname]:col0[gname] + 512],
                               start=(kt == 0), stop=(kt == 7))
                hb_.used(t_mm)
                if gname == "v":
                    vs = stage_get(vst_state, mkey, Vst)
                    t_v = P.I("act", "activation", waits=[t_mm] + vs["w"], out=vs["buf"].t[:, :, j, :],
                              in_=pb.t[:].rearrange("p (h d) -> p h d", h=8), func=AF.Copy)
                    vs["w"] = []
                    pb.used(t_v)
                    pb.close()
                    vs["toks"].append(t_v)
                    vs["n"] += 1
                    if vs["n"] == 4:
                        t_sv = P.dma("sp", f"vst{vs['slot']}", waits=vs["toks"], out=VS[:, :, ktile - 3:ktile + 1, :].rearrange("h p t c -> p h t c"),
                                     in_=vs["buf"].t[:])
                        vs["buf"].used(t_sv)
                        vs["buf"].close()
                        spill_toks.append(t_sv)
                else:
                    st[gname] = qk_evac(pb, t_mm, gk if gname == "k" else gq)
            hb_.close()
            yield
            for gname in st:
                qk_reduce(st[gname])
            yield
            for gname in st:
                qk_pow(st[gname])
            yield
            if "k" in groups:
                ka = Kaug.next()
                t_k = qk_rope(st["k"], ktile, ka)
            if "q" in groups:
                qa = Qaug.next()
                t_q = qk_rope(st["q"], ktile, qa)
            yield
            if "k" in groups:
                t_z = P.I("pool", "memset", waits=[t_k], ap=ka.t[:, :, 64:96], constant=0.0)
                t_o = P.I("pool", "memset", waits=[t_z], ap=ka.t[:, :, 64 + slot:65 + slot], constant=1.0)
            if "q" in groups:
                tkb = trk.next()
                wtk = tkb.acquire()
                for h in range(8):
                    t_tq = P.I("pe", "transpose", waits=([t_q] + wtk) if h == 0 else (), sig=(h == 7),
                               out=tkb.t[0:64, h, :], in_=qa.t[:, h, 0:64], identity=ident[:])
                qs = stage_get(qst_state, mkey, QTst)
                t_qc = P.I("act", "activation", waits=[t_tq] + qs["w"], out=qs["buf"].t[0:64, :, j * 128:(j + 1) * 128],
                           in_=tkb.t[0:64], func=AF.Copy)
                qs["w"] = []
                tkb.used(t_qc)
                tkb.close()
            yield
            if "k" in groups:
                tkb = trk.next()
                wtk = tkb.acquire()
                for h in range(8):
                    t_tk = P.I("pe", "transpose", waits=([t_k, t_o] + wtk) if h == 0 else (), sig=(h == 7),
                               out=tkb.t[0:96, h, :], in_=ka.t[:, h, :], identity=ident[:])
                ka.used(t_tk)
                ka.close()
                ks = stage_get(kst_state, mkey, KTst)
                t_kc = P.I("act", "activation", waits=[t_tk] + ks["w"], out=ks["buf"].t[:, :, j * 128:(j + 1) * 128],
                           in_=tkb.t[0:96], func=AF.Copy)
                ks["w"] = []
                tkb.used(t_kc)
                tkb.close()
                ks["toks"].append(t_kc)
                ks["n"] += 1
                k_last = (ks["n"] == 4)
            if "q" in groups:
                pr = prt.next()
                wpr = pr.acquire()
                for h in range(8):
                    t_rt = P.I("pe", "matmul", waits=([t_qc, kmT_tok[0]] + wpr) if h == 0 else (), sig=(h == 7),
                               out=pr.t[:, h, :], lhsT=qs["buf"].t[0:64, h, j * 128:(j + 1) * 128],
                               rhs=kmT[:, h, :], start=True, stop=True)
                sbb = sbr.next()
                t_sb = P.I("dve", "tensor_tensor", waits=[t_rt, CONST] + sbb.acquire(), out=sbb.t[:], in0=pr.t[:],
                           in1=rbias[:, qtile, :].unsqueeze(1).to_broadcast([128, 8, 32]), op=ALU.add)
                pr.used(t_sb)
                pr.close()
            yield
            if "k" in groups:
                if ktile % 2 == 1:
                    jb = j - 1
                    t_ks = P.I("dve", "tensor_reduce", waits=[ks["toks"][-1], ks["toks"][-2], kmT_tok[0]], out=kms[:],
                               in_=ks["buf"].t[0:64, :, jb * 128:(jb + 2) * 128], axis=AX.X, op=ALU.add)
                    t_km = P.I("dve", "tensor_scalar", waits=[t_ks], out=kmT[:, :, slot], in0=kms[:],
                               scalar1=1.0 / 256.0, scalar2=None, op0=ALU.mult)
                    kmT_tok[0] = t_km
                    ks["buf"].used(t_km)
                if k_last:
                    kbase = (ktile - 3) * 128
                    t_st = P.dma("sp", f"kst{ks['slot']}", waits=ks["toks"], out=KT[:, :, kbase:kbase + 512].rearrange("h r t -> r h t"),
                                 in_=ks["buf"].t[:])
                    ks["buf"].used(t_st)
                    ks["buf"].close()
                    spill_toks.append(t_st)
            if "q" not in groups:
                return
            mb = m8.next()
            wm = mb.acquire()
            for h in range(8):
                t_m8 = P.I("dve", "max", waits=([t_sb] + wm) if h == 0 else (), sig=(h == 7),
                           out=mb.t[:, h, :], in_=sbb.t[:, h, :])
            th = thr.next()
            t_th = P.I("dve", "tensor_scalar", waits=[t_m8] + th.acquire(), out=th.t[:], in0=mb.t[:, :, 2],
                       scalar1=-1.0e4, scalar2=None, op0=ALU.max)
            mb.used(t_th)
            mb.close()
            gb = ger.next()
            t_ge = P.I("dve", "tensor_tensor", waits=[t_th, t_sb] + gb.acquire(), out=gb.t[:], in0=sbb.t[:],
                       in1=th.t[:].unsqueeze(2).to_broadcast([128, 8, 32]), op=ALU.is_ge)
            sbb.used(t_ge)
            sbb.close()
            th.used(t_ge)
            th.close()
            t_mk = P.I("dve", "tensor_scalar", waits=[t_ge, t_q, t_tq], out=qa.t[:, :, 64:96], in0=gb.t[:],
                       scalar1=-1.0, scalar2=BIG, op0=ALU.add, op1=ALU.mult)
            gb.used(t_mk)
            gb.close()
            t_mk = P.I("dve", "memset", waits=[t_mk], ap=qa.t[:, :, 64 + slot:65 + slot], constant=0.0)
            yield
            tkb = trk.next()
            wtk = tkb.acquire()
            for h in range(8):
                t_tm = P.I("pe", "transpose", waits=([t_mk] + wtk) if h == 0 else (), sig=(h == 7),
                           out=tkb.t[0:32, h, :], in_=qa.t[:, h, 64:96], identity=ident[:])
            qa.used(t_tm)
            qa.close()
            t_mc = P.I("dve", "tensor_copy", waits=[t_tm, t_rt], out=qs["buf"].t[64:96, :, j * 128:(j + 1) * 128],
                       in_=tkb.t[0:32])
            tkb.used(t_mc)
            tkb.close()
            qs["toks"].append(t_mc)
            qs["n"] += 1
            if qs["n"] == nt_macro:
                N = nt_macro * 128
                qoff = 0 if kind == "halo" else 128 + (idx - 32 - 3) * 128
                t_sq = P.dma("sp", f"qst{qs['slot']}", waits=qs["toks"], out=QT[:, :, qoff:qoff + N].rearrange("h r t -> r h t"),
                             in_=qs["buf"].t[:, :, 0:N])
                qs["buf"].used(t_sq)
                qs["buf"].close()
                spill_toks.append(t_sq)

        gensA1 = [taskA1("pre", i) for i in range(32)] + [taskA1("halo", 0)] + [taskA1("own", 32 + i) for i in range(32)]
        run_tasks(gensA1, 16)
        P.barrier()
        esA1.close()

    esA2 = ExitStack()
    w_pg = sb(esA2, "w_pg", [128, 8, 2560], BF16)
    w_bp_bf = sb(esA2, "w_bp_bf", [128, 4, D], BF16)
    w_pool_bf = sb(esA2, "w_pool_bf", [128, 4, 128], BF16)
    hTm = Ring([sb(esA2, f"hTm{i}", [128, 8, 512], BF16) for i in range(2)])
    G0st = Ring([sb(esA2, f"G0st{i}", [128, 8, 512], BF16) for i in range(2)])
    GPst = Ring([sb(esA2, f"GPst{i}", [128, 8, 512], BF16) for i in range(2)])
    ubs = [[sb(esA2, f"ub{g}_{i}", [128, 16 + 512], F32) for i in range(3)] for g in range(4)]
    ucarry = sb(esA2, "ucarry", [128, 4, 16], F32)
    pooled = Ring([sb(esA2, f"pooled{i}", [128, 512], BF16) for i in range(4)])
    pmr = Ring([sb(esA2, f"pm{i}", [128, 4, 512], BF16) for i in range(2)])
    g1r = Ring([sb(esA2, f"g1_{i}", [128, 512], F32) for i in range(5)])
    tmp16 = sb(esA2, "tmp16", [128, 4, 16], F32)
    trx2 = Ring([ps(esA2, f"trx2_{i}", [128, 8, 128], BF16) for i in range(2)])
    pf = Ring([ps(esA2, f"pf{i}", [128, 512], F32) for i in range(6)])

    for kt in range(8):
        WP = P.dma("pool", "wpg0", out=w_pg[:, kt, 0:512], in_=w_in_v[:, kt, 1536:2048])
    for kt in range(8):
        WG = P.dma("pool", "wpg1", out=w_pg[:, kt, 512:2560], in_=w_in_v[:, kt, 2048:4096])
    for kt in range(8):
        t_f = P.I("dve", "tensor_scalar", waits=[WP, CONST], out=w_pg[:, kt, 0:512], in0=w_pg[:, kt, 0:512],
                  scalar1=gmix_t[:, kt:kt + 1], scalar2=None, op0=ALU.mult)
    WP = t_f
    for kt in range(8):
        t_f = P.I("dve", "tensor_scalar", waits=[WG, CONST], out=w_pg[:, kt, 512:2560], in0=w_pg[:, kt, 512:2560],
                  scalar1=gmix_t[:, kt:kt + 1], scalar2=None, op0=ALU.mult)
    WG = t_f
    W_BP = P.dma("pool", "wbp", out=w_bp_bf[:], in_=w_bp.rearrange("(g p) n -> p g n", p=128))
    W_POOL = P.dma("pool", "wpool", out=w_pool_bf[:], in_=w_pool.rearrange("g c d -> c g d"))
    t_uc0 = P.I("pool", "memset", ap=ucarry[:], constant=0.0)
    ucarry_tok = [t_uc0] * 4
    ub_free = [[] for _ in range(4)]
    for g_ in range(4):
        for i_ in (1, 2):
            ub_free[g_].append(P.I("pool", "memset", ap=ubs[g_][i_][:], constant=0.0))
    macros = [("halo", 0)] + [("own", m) for m in range(8)]
    mstate = {}

    def mget(mi):
        if mi not in mstate:
            hb_ = hTm.next()
            mstate[mi] = {"hT": hb_, "hT_w": hb_.acquire(), "hT_toks": [], "hT_users": 0,
                          "pm": None, "pm_toks": {}, "pm_users": 0, "g0": None, "gp": None, "g0_toks": [], "gp_toks": []}
        return mstate[mi]

    def hT_user_done(ms, tok, total):
        ms["hT"].used(tok)
        ms["hT_users"] += 1
        if ms["hT_users"] == total:
            ms["hT"].close()

    def taskX(mi, j):
        kind, m = macros[mi]
        src = x_pre if kind == "halo" else x_own
        row0 = 3968 if kind == "halo" else m * 512 + j * 128

        def get_dst():
            ms = mget(mi)
            w = ms["hT_w"]
            ms["hT_w"] = []
            return ms["hT"].t[:, :, j * 128:(j + 1) * 128], w

        t = yield from x_chain(src, row0, trx2, get_dst)
        mget(mi)["hT_toks"].append(t)

    def taskPool(mi, g):
        kind, m = macros[mi]
        N = 128 if kind == "halo" else 512
        ms = mget(mi)
        while len(ms["hT_toks"]) < N // 128:
            yield
        ub = ubs[g]
        w = 2 << g
        pb = pf.next()
        wpb = pb.acquire()
        for kt in range(8):
            t_mm = P.I("pe", "matmul", waits=(ms["hT_toks"] + [WP] + wpb) if kt == 0 else (), sig=(kt == 7),
                       out=pb.t[:, 0:N], lhsT=w_pg[:, kt, g * 128:(g + 1) * 128],
                       rhs=ms["hT"].t[:, kt, 0:N], start=(kt == 0), stop=(kt == 7))
        hT_user_done(ms, t_mm, 20)
        wub = ub_free[g]
        t_c0 = P.I("pool", "tensor_copy", waits=wub + [ucarry_tok[g]], out=ub[0][:, 0:16], in_=ucarry[:, g, :])
        t_u = P.I("act", "activation", waits=[t_mm] + wub, out=ub[0][:, 16:16 + N], in_=pb.t[:, 0:N], func=AF.Copy)
        pb.used(t_u)
        pb.close()
        if kind == "halo":
            ucarry_tok[g] = P.I("pool", "tensor_scalar", waits=[t_u, t_c0, CONST], out=ucarry[:, g, :], in0=ub[0][:, N:N + 16],
                                scalar1=hflag[:, 0:1], scalar2=None, op0=ALU.mult)
        else:
            ucarry_tok[g] = P.I("pool", "tensor_copy", waits=[t_u, t_c0], out=ucarry[:, g, :], in_=ub[0][:, N:N + 16])
        L = 16 + N
        src_i, last = 0, [t_u, t_c0]
        sh = 1
        for step in range(g + 1):
            dst_i = 1 if src_i != 1 else 2
            t_a = P.I("pool", "tensor_tensor", waits=last + wub, out=ub[dst_i][:, sh:L], in0=ub[src_i][:, sh:L],
                      in1=ub[src_i][:, 0:L - sh], op=ALU.add)
            last = [t_a]
            src_i = dst_i
            sh *= 2
        po = pooled.next()
        wpo = po.acquire()
        t_p = P.I("dve", "scalar_tensor_tensor", waits=last + [t_u, ucarry_tok[g]] + wpo, out=po.t[:, 0:N], in0=ub[src_i][:, 16:16 + N],
                  scalar=1.0 / w, in1=ub[0][:, 16:16 + N], op0=ALU.mult, op1=ALU.subtract)
        if kind == "own" and m == 0:
            t_p1 = P.I("dve", "tensor_tensor", waits=[t_p, CONST], out=tmp16[:, g, :], in0=ub[src_i][:, 16:32],
                       in1=invcnt[:, g, :], op=ALU.mult)
            t_p = P.I("dve", "tensor_tensor", waits=[t_p1], out=po.t[:, 0:16], in0=tmp16[:, g, :], in1=ub[0][:, 16:32],
                      op=ALU.subtract)
        ub_free[g] = [t_p]
        yield
        yield
        yield
        if ms["pm"] is None:
            ms["pm"] = pmr.next()
            ms["pm_w"] = ms["pm"].acquire()
        pb2 = pf.next()
        t_pm = P.I("pe", "matmul", waits=[t_p, W_POOL] + pb2.acquire(), out=pb2.t[:, 0:N], lhsT=w_pool_bf[:, g, :],
                   rhs=po.t[:, 0:N], start=True, stop=True)
        po.used(t_pm)
        po.close()
        t_ps = P.I("act", "activation", waits=[t_pm, CONST] + ms["pm_w"], out=ms["pm"].t[:, g, 0:N], in_=pb2.t[:, 0:N],
                   func=AF.Identity, scale=pscale[:, g:g + 1])
        ms["pm_w"] = []
        pb2.used(t_ps)
        pb2.close()
        ms["pm_toks"][g] = t_ps

    def taskGate(mi, f):
        kind, m = macros[mi]
        N = 128 if kind == "halo" else 512
        qoff = 0 if kind == "halo" else 128 + m * 512
        ms = mget(mi)
        while len(ms["hT_toks"]) < N // 128:
            yield
        if ms["g0"] is None:
            ms["g0"] = G0st.next()
            ms["gslot"] = (G0st.i - 1) % len(G0st.bufs)
            ms["g0_w"] = ms["g0"].acquire()
            ms["gp"] = GPst.next()
            ms["gp_w"] = ms["gp"].acquire()
        pb = pf.next()
        wpb = pb.acquire()
        for kt in range(8):
            t_mm = P.I("pe", "matmul", waits=(ms["hT_toks"] + [WG] + wpb) if kt == 0 else (), sig=(kt == 7),
                       out=pb.t[:, 0:N], lhsT=w_pg[:, kt, 512 + f * 128:512 + (f + 1) * 128],
                       rhs=ms["hT"].t[:, kt, 0:N], start=(kt == 0), stop=(kt == 7))
        hT_user_done(ms, t_mm, 20)
        t_s0 = P.I("act", "activation", waits=[t_mm, CONST] + ms["g0_w"], out=ms["g0"].t[:, f, 0:N], in_=pb.t[:, 0:N],
                   func=AF.Sigmoid, bias=bgate[:, f:f + 1])
        ms["g0_w"] = []
        pb.used(t_s0)
        pb.close()
        ms["g0_toks"].append(t_s0)
        pb = pf.next()
        wpb = pb.acquire()
        for kt in range(8):
            t_mm = P.I("pe", "matmul", waits=(ms["hT_toks"] + wpb) if kt == 0 else (), sig=(kt == 7),
                       out=pb.t[:, 0:N], lhsT=w_pg[:, kt, 1536 + f * 128:1536 + (f + 1) * 128],
                       rhs=ms["hT"].t[:, kt, 0:N], start=(kt == 0), stop=(kt == 7))
        hT_user_done(ms, t_mm, 20)
        g1 = g1r.next()
        t_s1 = P.I("act", "activation", waits=[t_mm] + g1.acquire(), out=g1.t[:, 0:N], in_=pb.t[:, 0:N],
                   func=AF.Sigmoid, bias=bgate[:, 8 + f:9 + f])
        pb.used(t_s1)
        pb.close()
        yield
        yield
        while len(ms["pm_toks"]) < 4:
            yield
        pb = pf.next()
        wpb = pb.acquire()
        for g in range(4):
            t_mm = P.I("pe", "matmul", waits=(list(ms["pm_toks"].values()) + [W_BP] + wpb) if g == 0 else (), sig=(g == 3),
                       out=pb.t[:, 0:N], lhsT=w_bp_bf[:, g, f * 128:(f + 1) * 128], rhs=ms["pm"].t[:, g, 0:N],
                       start=(g == 0), stop=(g == 3))
        ms["pm"].used(t_mm)
        ms["pm_users"] += 1
        if ms["pm_users"] == 8:
            ms["pm"].close()
        t_gp = P.I("dve", "tensor_tensor", waits=[t_mm, t_s1] + ms["gp_w"], out=ms["gp"].t[:, f, 0:N], in0=pb.t[:, 0:N],
                   in1=g1.t[:, 0:N], op=ALU.mult)
        ms["gp_w"] = []
        pb.used(t_gp)
        pb.close()
        g1.used(t_gp)
        g1.close()
        ms["gp_toks"].append(t_gp)
        if len(ms["gp_toks"]) == 8:
            t_s = P.dma("sp", f"g0st{ms['gslot']}", waits=ms["g0_toks"], out=G0[:, :, qoff:qoff + N].rearrange("f p t -> p f t"), in_=ms["g0"].t[:, :, 0:N])
            ms["g0"].used(t_s)
            ms["g0"].close()
            spill_toks.append(t_s)
            t_s = P.dma("sp", f"gpst{ms['gslot']}", waits=ms["gp_toks"], out=GP[:, :, qoff:qoff + N].rearrange("f p t -> p f t"), in_=ms["gp"].t[:, :, 0:N])
            ms["gp"].used(t_s)
            ms["gp"].close()
            spill_toks.append(t_s)

    gensA2 = []
    nx = lambda mi: (1 if macros[mi][0] == "halo" else 4)
    gensA2 += [taskX(0, 0)]
    for mi in range(len(macros)):
        fm = [taskPool(mi, g) for g in range(4)] + [taskGate(mi, f) for f in range(8)]
        nxt = [taskX(mi + 1, j) for j in range(nx(mi + 1))] if mi + 1 < len(macros) else []
        seq = []
        k = 0
        for i, t in enumerate(fm):
            seq.append(t)
            if i % 3 == 2 and k < len(nxt):
                seq.append(nxt[k])
                k += 1
        seq += nxt[k:]
        gensA2 += seq
    run_tasks(gensA2, 12)
    P.barrier()
    esA2.close()
    esA.close()

    esW = ExitStack()
    w_out_bf = sb(esW, "w_out_bf", [128, 8, D], BF16)
    w_down_bf = sb(esW, "w_down_bf", [128, 22, D], BF16)
    h2T_halo = sb(esW, "h2T_halo", [128, 8, 2], BF16)
    for kt in range(8):
        W_OUT = P.dma("pool", "wc1", out=w_out_bf[:, kt, :], in_=w_out[kt * 128:(kt + 1) * 128, :])
    for c in range(22):
        W_DN = P.dma("pool", "wc3", out=w_down_bf[:, c, :], in_=w_down[c * 128:(c + 1) * 128, :])
    esB = ExitStack()
    attnT = sb(esB, "attnT", [128, 4, NQT], BF16)
    if not do_attn:
        t_at = P.I("pool", "memset", ap=attnT[:], constant=0.0)
        at_toks = [t_at]
    if do_attn:
        at_toks = []
        KTh = Ring([sb(esB, f"KTh{i}", [96, 2 * NTOK], BF16) for i in range(2)])
        QTh = Ring([sb(esB, f"QTh{i}", [96, NQT], BF16) for i in range(2)])
        Vh = Ring([sb(esB, f"Vh{i}", [128, 64, 128], BF16) for i in range(2)])
        Pb = Ring([sb(esB, f"Pb{i}", [128, 3, 512], BF16) for i in range(4)])
        rden = Ring([sb(esB, f"rden{i}", [128, 512], F32) for i in range(2)])
        Sps = Ring([ps(esB, f"Sps{i}", [128, 3, 512], F32) for i in range(2)])
        Ops = Ring([ps(esB, f"Ops{i}", [128, 512], F32) for i in range(2)])
        for r in Vh.bufs:
            r.used(P.I("pool", "memset", ap=r.t[:, :, 64:128], constant=1.0))

        def group_items(gi):
            items = []
            if gi == 0:
                for s in range(15):
                    items += [(2 * s, 0, 128, False, False), (2 * s + 1, 0, 128, False, False)]
                items += [(30, 0, 128, False, True), (31, 0, 128, True, True)]
                return 0, 128, items
            i0 = 2 * (gi - 1)
            for s in range(16 + i0):
                items += [(2 * s, 0, 512, False, False), (2 * s + 1, 0, 512, False, False)]
            s0 = 16 + i0
            s1 = s0 + 1
            items += [(2 * s0, 256, 256, False, False), (2 * s0 + 1, 256, 256, False, False)]
            items += [(2 * s0, 0, 128, True, True), (2 * s0, 128, 128, False, True), (2 * s0 + 1, 128, 128, True, True)]
            items += [(2 * s1, 256, 128, True, True), (2 * s1, 384, 128, False, True), (2 * s1 + 1, 384, 128, True, True)]
            return 128 + i0 * 256, 512, items

        for h in range(NH):
            kb = KTh.next()
            qb = QTh.next()
            vb = Vh.next()
            ld_w = spill_toks if h < 2 else []
            t_lk = P.dma("sp", f"ldk{h % 2}", waits=ld_w + kb.take(), out=kb.t[:], in_=KT[h])
            t_lq = P.dma("sp", f"ldq{h % 2}", waits=qb.take(), out=qb.t[:], in_=QT[h])
            t_lv = P.dma("sp", f"ldv{h % 2}", waits=vb.take(), out=vb.t[:, :, 0:64], in_=VS[h])
            users = []
            for gi in range(9):
                q0, nq, items = group_items(gi)
                batches = []
                for it in items:
                    if batches and len(batches[-1]) < 3 and batches[-1][0][1:3] == it[1:3] and batches[-1][0][4] == it[4]:
                        batches[-1].append(it)
                    else:
                        batches.append([it])
                ob = Ops.next()
                ob_w = ob.take()
                nb = len(batches)
                exp_tok = [None] * nb
                pbuf = [None] * nb
                first_pv = [True]

                def emit_pv(bi):
                    for k, (ktile, qc0, qn, causal, own) in enumerate(batches[bi]):
                        is_last = (bi == nb - 1) and (k == len(batches[bi]) - 1)
                        t = P.I("pe", "matmul", waits=[exp_tok[bi], t_lv] + (ob_w if first_pv[0] else []), sig=True,
                                out=ob.t[:, qc0:qc0 + qn], lhsT=vb.t[:, ktile, :], rhs=pbuf[bi].t[:, k, 0:qn],
                                start=first_pv[0], stop=is_last, skip_group_check=True)
                        first_pv[0] = False
                    pbuf[bi].used(t)
                    return t

                t_pv = None
                for bi, batch in enumerate(batches):
                    sbuf_ = Sps.next()
                    ws = sbuf_.take()
                    for k, (ktile, qc0, qn, causal, own) in enumerate(batch):
                        nr = 96
                        t_qk = P.I("pe", "matmul", waits=([t_lk, t_lq] + ws) if k == 0 else (), sig=True,
                                   out=sbuf_.t[:, k, 0:qn], lhsT=kb.t[0:nr, ktile * 128:(ktile + 1) * 128],
                                   rhs=qb.t[0:nr, q0 + qc0:q0 + qc0 + qn], start=True, stop=not causal)
                        if causal:
                            t_qk = P.I("pe", "matmul", waits=[CONST], sig=True, out=sbuf_.t[:, k, 0:qn], lhsT=ident[:],
                                       rhs=cbias[:, 0:qn], start=False, stop=True)
                    qn = batch[0][2]
                    pb_ = Pb.next()
                    pbuf[bi] = pb_
                    exp_tok[bi] = P.I("act", "activation", waits=[t_qk] + pb_.take(), out=pb_.t[:, 0:len(batch), 0:qn],
                                      in_=sbuf_.t[:, 0:len(batch), 0:qn], func=AF.Exp, scale=HD ** -0.5)
                    sbuf_.used(exp_tok[bi])
                    if bi >= 1:
                        t_pv = emit_pv(bi - 1)
                t_pv = emit_pv(nb - 1)
                users.append(t_pv)
                rd = rden.next()
                t_rc = P.I("dve", "reciprocal", waits=[t_pv] + rd.take(), out=rd.t[64:128, 0:nq], in_=ob.t[64:128, 0:nq])
                po = (h % 2) * 64
                t_at = P.I("dve", "tensor_tensor", waits=[t_rc], out=attnT[po:po + 64, h // 2, q0:q0 + nq],
                           in0=ob.t[0:64, 0:nq], in1=rd.t[64:128, 0:nq], op=ALU.mult)
                ob.used(t_at)
                rd.used(t_at)
                at_toks.append(t_at)
            for u in users:
                kb.used(u)
                qb.used(u)
                vb.used(u)
    t_ats = P.dma("sp", "ats", waits=at_toks, out=AT.rearrange("g p t -> p g t"), in_=attnT[:])
    spill_toks.append(t_ats)
    P.barrier()
    esB.close()

    esC = ExitStack()
    w_ba_bf = sb(esC, "w_ba_bf", [128, 4, D], BF16)
    wup = Ring([sb(esC, f"wup{i}", [128, 8, 2, 256], BF16) for i in range(3)])
    G0l = Buf(sb(esC, "G0l", [128, 8, 512], BF16))
    GPl = Buf(sb(esC, "GPl", [128, 8, 512], BF16))
    attl = Buf(sb(esC, "attl", [128, 4, 512], BF16))
    tmpm = Ring([sb(esC, f"tmpm{i}", [128, 512], F32) for i in range(2)])
    x1 = Buf(sb(esC, "x1", [128, 4, D], F32))
    h2b = Ring([sb(esC, f"h2b{i}", [128, D], BF16) for i in range(2)])
    h2T = Buf(sb(esC, "h2T", [128, 8, 512], BF16))
    zb = Ring([sb(esC, f"zb{i}", [128, 2 + 512], F32) for i in range(2)])
    cv = Ring([sb(esC, f"cv{i}", [128, 512], F32) for i in range(3)])
    sg = Ring([sb(esC, f"sg{i}", [128, 512], F32) for i in range(2)])
    aT = Buf(sb(esC, "aT", [128, 22, 512], BF16))
    ob_ = Ring([sb(esC, f"ob{i}", [128, D], F32) for i in range(2)])
    pA = Ring([ps(esC, f"pA{i}", [128, 512], F32) for i in range(2)])
    pY = Ring([ps(esC, f"pY{i}", [128, 512], F32) for i in range(2)])
    trc = Ring([ps(esC, f"trc{i}", [128, 8, 128], BF16) for i in range(1)])
    pz = Ring([ps(esC, f"pz{i}", [128, 512], F32) for i in range(3)])

    W_BA = P.dma("pool", "wc0", out=w_ba_bf[:], in_=w_ba.rearrange("(g p) n -> p g n", p=128))
    w_up_v = w_up.rearrange("(kt p) n -> p kt n", p=128)

    chunk_tab = {}
    N_CHUNKS = 8 * 11
    chunk_ctr = [0]

    def issue_chunk(idx):
        if idx in chunk_tab or idx >= N_CHUNKS:
            return
        c2 = idx % 11
        wb = wup.next()
        key = f"wup{(wup.i - 1) % 3}"
        ww = wb.take()
        P.dma("pool", key, waits=ww, out=wb.t[:, :, 0, :], in_=w_up_v[:, :, c2 * 256:(c2 + 1) * 256])
        tk = P.dma("pool", key, out=wb.t[:, :, 1, :], in_=w_up_v[:, :, DFF + c2 * 256:DFF + (c2 + 1) * 256])
        chunk_tab[idx] = (wb, tk)

    def load_wup(c2):
        idx = chunk_ctr[0]
        chunk_ctr[0] += 1
        assert idx % 11 == c2
        issue_chunk(idx)
        issue_chunk(idx + 1)
        issue_chunk(idx + 2)
        return chunk_tab[idx]
    carry_tok = [None] * 44
    out_toks = []

    c_loads = {}
    halo_tok = [None]

    def issue_c_loads(kind, m):
        N = 128 if kind == "halo" else 512
        qoff = 0 if kind == "halo" else 128 + m * 512
        t_g0 = P.dma("sp", "ldg0", waits=spill_toks + G0l.take(), out=G0l.t[:, :, 0:N], in_=G0[:, :, qoff:qoff + N].rearrange("f p t -> p f t"))
        t_gp = P.dma("sp", "ldgp", waits=GPl.take(), out=GPl.t[:, :, 0:N], in_=GP[:, :, qoff:qoff + N].rearrange("f p t -> p f t"))
        t_al = P.dma("sp", "ldat", waits=attl.take(), out=attl.t[:, :, 0:N], in_=AT[:, :, qoff:qoff + N].rearrange("g p t -> p g t"))
        c_loads[(kind, m)] = (t_g0, t_gp, t_al)

    def phaseC_macro(kind, m):
        nt = 1 if kind == "halo" else 4
        N = nt * 128
        qoff = 0 if kind == "halo" else 128 + m * 512
        src = x_pre if kind == "halo" else x_own
        tok0 = 3968 if kind == "halo" else m * 512
        if (kind, m) not in c_loads:
            issue_c_loads(kind, m)
        t_g0, t_gp, t_al = c_loads[(kind, m)]
        t_x = P.dma("sp", "ldx1", waits=x1.take(), out=x1.t[:, 0:nt, :], in_=src[tok0:tok0 + N, :].rearrange("(j p) d -> p j d", p=128))
        mix_toks = []
        for f in range(8):
            pb = pA.next()
            wpb = pb.take()
            for g in range(4):
                t_mm = P.I("pe", "matmul", waits=([W_BA, t_al] + wpb) if g == 0 else (), sig=(g == 3), out=pb.t[:, 0:N],
                           lhsT=w_ba_bf[:, g, f * 128:(f + 1) * 128], rhs=attl.t[:, g, 0:N], start=(g == 0), stop=(g == 3))
            tm = tmpm.next()
            t_1 = P.I("dve", "tensor_tensor", waits=[t_mm, t_g0] + tm.take(), out=tm.t[:, 0:N], in0=pb.t[:, 0:N], in1=G0l.t[:, f, 0:N], op=ALU.mult)
            pb.used(t_1)
            t_2 = P.I("dve", "tensor_tensor", waits=[t_1, t_gp], out=GPl.t[:, f, 0:N], in0=tm.t[:, 0:N],
                      in1=GPl.t[:, f, 0:N], op=ALU.add)
            tm.used(t_2)
            mix_toks.append(t_2)
        attl.used(t_mm)
        G0l.used(mix_toks[-1])
        h2T_w = h2T.take()
        h2T_toks = []
        x1_toks = [None] * nt
        h2T_first = [True]
        last_h = [None]

        def c1_tile(j):
            xs = []
            for c in range(2):
                pb = pY.next()
                wpb = pb.take()
                for kt in range(8):
                    t_mm = P.I("pe", "matmul", waits=(mix_toks + [W_OUT] + wpb) if kt == 0 else (), sig=(kt == 7), out=pb.t[:],
                               lhsT=GPl.t[:, kt, j * 128:(j + 1) * 128], rhs=w_out_bf[:, kt, c * 512:(c + 1) * 512],
                               start=(kt == 0), stop=(kt == 7))
                GPl.used(t_mm)
                t_a = P.I("dve", "tensor_tensor", waits=[t_mm, t_x], out=x1.t[:, j, c * 512:(c + 1) * 512], in0=pb.t[:],
                          in1=x1.t[:, j, c * 512:(c + 1) * 512], op=ALU.add)
                pb.used(t_a)
                xs.append(t_a)
            x1_toks[j] = xs
            t_rs, ni = rmsnorm_tile(x1.t[:, j, :], gffn, None, xs)
            yield
            yield
            hb = h2b.next()
            t_h = P.I("dve", "scalar_tensor_tensor", waits=[t_rs, CONST] + xs + hb.take(), out=hb.t[:], in0=x1.t[:, j, :],
                      scalar=rs_all[:, ni:ni + 1], in1=gffn[:], op0=ALU.mult, op1=ALU.mult)
            last_h[0] = t_h
            tr = trc.next()
            wtr = tr.take()
            for kt in range(8):
                t_tr = P.I("pe", "transpose", waits=([t_h, CONST] + wtr) if kt == 0 else (), sig=(kt == 7), out=tr.t[:, kt, :],
                           in_=hb.t[:, kt * 128:(kt + 1) * 128], identity=ident[:])
            hb.used(t_tr)
            t_cp = P.I("act", "activation", waits=[t_tr] + (h2T_w if h2T_first[0] else []), out=h2T.t[:, :, j * 128:(j + 1) * 128], in_=tr.t[:], func=AF.Copy)
            h2T_first[0] = False
            tr.used(t_cp)
            h2T_toks.append(t_cp)

        run_tasks([c1_tile(j) for j in range(nt)], 4)
        t_h = last_h[0]
        nxt = c_order[c_order.index((kind, m)) + 1] if c_order.index((kind, m)) + 1 < len(c_order) else None
        if nxt is not None:
            issue_c_loads(*nxt)
        h2T_users = []
        if kind == "halo":
            t_hh = P.I("dve", "tensor_copy", waits=h2T_toks, out=h2T_halo[:], in_=h2T.t[:, :, 126:128])
            halo_tok[0] = t_hh
            h2T.used(t_hh)
            x1.used(t_h)
            return
        aT_w = aT.take()
        aT_toks = []
        for c2 in range(11):
            wb, t_w = load_wup(c2)
            for ci in range(2):
                c = 2 * c2 + ci
                ups = []
                for half in range(2):
                    cc = c + 22 * half
                    if m == 0:
                        pb = pz.next()
                        wpb = pb.take()
                        for kt in range(8):
                            t_mm = P.I("pe", "matmul", waits=([halo_tok[0], t_w] + wpb) if kt == 0 else (), sig=(kt == 7), out=pb.t[:, 0:2],
                                       lhsT=wb.t[:, kt, half, ci * 128:(ci + 1) * 128], rhs=h2T_halo[:, kt, :], start=(kt == 0), stop=(kt == 7))
                        carry_tok[cc] = P.I("dve", "tensor_scalar", waits=[t_mm, CONST], out=carry[:, cc, :], in0=pb.t[:, 0:2], scalar1=hflag[:, 0:1],
                                            scalar2=None, op0=ALU.mult)
                        pb.used(carry_tok[cc])
                    pb = pz.next()
                    wpb = pb.take()
                    for kt in range(8):
                        t_mm = P.I("pe", "matmul", waits=(h2T_toks + [t_w] + wpb) if kt == 0 else (), sig=(kt == 7), out=pb.t[:],
                                   lhsT=wb.t[:, kt, half, ci * 128:(ci + 1) * 128], rhs=h2T.t[:, kt, :], start=(kt == 0), stop=(kt == 7))
                    h2T_users.append(t_mm)
                    z = zb.next()
                    wz = z.take()
                    t_zc = P.I("dve", "tensor_copy", waits=[carry_tok[cc]] + wz, out=z.t[:, 0:2], in_=carry[:, cc, :])
                    t_z = P.I("act", "activation", waits=[t_mm] + wz, out=z.t[:, 2:514], in_=pb.t[:], func=AF.Copy)
                    v_ = cv.next()
                    t_c2 = P.I("act", "activation", waits=[t_mm, CONST] + v_.take(), out=v_.t[:], in_=pb.t[:], func=AF.Identity,
                               scale=convw[:, 2, cc:cc + 1], bias=convb[:, cc:cc + 1])
                    pb.used(t_z)
                    pb.used(t_c2)
                    carry_tok[cc] = P.I("dve", "tensor_copy", waits=[t_z, t_zc], out=carry[:, cc, :], in_=z.t[:, 512:514])
                    t_c1 = P.I("dve", "scalar_tensor_tensor", waits=[t_z, t_zc, t_c2], out=v_.t[:], in0=z.t[:, 1:513], scalar=convw[:, 1, cc:cc + 1],
                               in1=v_.t[:], op0=ALU.mult, op1=ALU.add)
                    t_c0 = P.I("dve", "scalar_tensor_tensor", waits=[t_c1], out=v_.t[:], in0=z.t[:, 0:512], scalar=convw[:, 0, cc:cc + 1],
                               in1=v_.t[:], op0=ALU.mult, op1=ALU.add)
                    z.used(t_c0)
                    z.used(carry_tok[cc])
                    ups.append((v_, t_c0))
                s_ = sg.next()
                t_si = P.I("act", "activation", waits=[ups[0][1]] + s_.take(), out=s_.t[:], in_=ups[0][0].t[:], func=AF.Silu)
                ups[0][0].used(t_si)
                t_a = P.I("pool", "tensor_tensor", waits=[t_si, ups[1][1]] + (aT_w if c == 0 else []), out=aT.t[:, c, :], in0=s_.t[:], in1=ups[1][0].t[:], op=ALU.mult)
                s_.used(t_a)
                ups[1][0].used(t_a)
                aT_toks.append(t_a)
            wb.used(t_mm)
        for t in h2T_users:
            h2T.used(t)
        aT_users = []
        for j in range(4):
            o = ob_.next()
            wo = o.take()
            for c2 in range(2):
                pb = pY.next()
                wpb = pb.take()
                for c in range(22):
                    t_mm = P.I("pe", "matmul", waits=(aT_toks + [W_DN] + wpb) if c == 0 else (), sig=(c == 21), out=pb.t[:],
                               lhsT=aT.t[:, c, j * 128:(j + 1) * 128], rhs=w_down_bf[:, c, c2 * 512:(c2 + 1) * 512], start=(c == 0), stop=(c == 21))
                aT_users.append(t_mm)
                t_o = P.I("dve", "tensor_tensor", waits=[t_mm] + x1_toks[j] + (wo if c2 == 0 else []), out=o.t[:, c2 * 512:(c2 + 1) * 512], in0=pb.t[:],
                          in1=x1.t[:, j, c2 * 512:(c2 + 1) * 512], op=ALU.add)
                pb.used(t_o)
            t_st = P.dma("sp", f"sty{(ob_.i - 1) % 2}", waits=[t_o], out=y[tok0 + j * 128: tok0 + (j + 1) * 128, :], in_=o.t[:])
            o.used(t_st)
            out_toks.append(t_st)
        x1.used(t_o)
        for t in aT_users:
            aT.used(t)

    c_order = [("halo", 0)] + [("own", m) for m in range(8)]
    for (kind_, m_) in c_order:
        phaseC_macro(kind_, m_)
    P.final_wait("sp", out_toks)

    with ExitStack() as es:
        sems = {k: es.enter_context(nc.semaphore(f"s_{k}")) for k in P.sem_keys()}
        block = es.enter_context(nc.Block())

        def replay(eng_name, E):
            for (waits, method, kw, inc) in P.q[eng_name]:
                for (k, v) in waits:
                    E.wait_ge(sems[k], v)
                if method is None:
                    continue
                ins = getattr(E, method)(**kw)
                if inc is not None:
                    ins.then_inc(sems[inc[0]], inc[1])

        @block.sync
        def _(E):
            replay("sp", E)

        @block.scalar
        def _(E):
            replay("act", E)

        @block.vector
        def _(E):
            replay("dve", E)

        @block.gpsimd
        def _(E):
            replay("pool", E)

        @block.tensor
        def _(E):
            replay("pe", E)
    esC.close()
    esW.close()
    es_all.close()
    return nc


DO_ATTN = True
_CACHE = {}


def _host_consts(half):
    c = {}
    ident = np.eye(128, dtype=np.float32).astype(ml_dtypes.bfloat16)
    kk = np.arange(128)[:, None]
    qq = np.arange(128)[None, :]
    cb = np.where(kk <= qq, 0.0, -BIG).astype(np.float32).astype(ml_dtypes.bfloat16)
    c["ident_bf"] = ident
    c["cbias_bf"] = cb
    hd = 8
    inv_freq = (np.float32(500000.0) ** (-np.arange(hd, dtype=np.float32) / np.float32(hd))).astype(np.float32)
    pos = np.concatenate([np.arange(4096), half * 4096 + np.arange(4096)]).astype(np.float32)
    ang = (pos[:, None] * inv_freq[None, :]).astype(np.float32)
    cos = np.cos(ang).astype(np.float32).reshape(64, 128, 8).transpose(1, 0, 2)
    sin = np.sin(ang).astype(np.float32).reshape(64, 128, 8).transpose(1, 0, 2)
    c["cos_t"] = np.ascontiguousarray(np.concatenate([cos, cos], axis=2))
    c["sin_t"] = np.ascontiguousarray(np.concatenate([-sin, sin], axis=2))
    rb = np.full((33, 32), -BIG, dtype=np.float32)
    if half == 1:
        rb[0, 0:15] = 0.0
    for t in range(32):
        blk = t // 2
        if half == 1:
            rb[1 + t, 0:16] = 0.0
        rb[1 + t, 16:16 + blk] = 0.0
    c["rbias"] = np.ascontiguousarray(np.broadcast_to(rb[None], (128, 33, 32)))
    ic = np.zeros((4, 16), dtype=np.float32)
    for g, w in enumerate((2, 4, 8, 16)):
        tg = half * 4096 + np.arange(16)
        ic[g] = 1.0 / np.minimum(tg + 1.0, float(w))
    c["invcnt"] = np.ascontiguousarray(np.broadcast_to(ic[None], (128, 4, 16)))
    c["hflag"] = np.full((128, 1), float(half), dtype=np.float32)
    return c


def kernel(x, norm_mix_g, w_in, b_gate, q_norm_g, k_norm_g, w_pool, pool_scale,
           w_branch_attn, w_branch_pool, w_out, norm_ffn_g, w_up, conv_w, conv_b, w_down):
    f = lambda a: np.ascontiguousarray(np.asarray(a, dtype=np.float32))
    x = f(x)
    shared = {
        "w_in": f(w_in[0]), "w_pool": f(w_pool[0]), "w_ba": f(w_branch_attn[0]), "w_bp": f(w_branch_pool[0]),
        "w_out": f(w_out[0]), "w_up": f(w_up[0]), "w_down": f(w_down[0]),
        "gmix_b": f(np.broadcast_to(np.asarray(norm_mix_g[0])[None, :], (128, D))),
        "gffn_b": f(np.broadcast_to(np.asarray(norm_ffn_g[0])[None, :], (128, D))),
        "gq_b": f(np.broadcast_to(np.tile(np.asarray(q_norm_g[0]), 8)[None, :], (128, 512))),
        "gk_b": f(np.broadcast_to(np.tile(np.asarray(k_norm_g[0]), 8)[None, :], (128, 512))),
        "bgate_t": f(np.asarray(b_gate[0]).reshape(16, 128).T),
        "gmix_t": f(np.asarray(norm_mix_g[0]).reshape(8, 128).T),
        "pscale_t": f(np.asarray(pool_scale[0]).reshape(4, 128).T),
        "convw_t": f(np.asarray(conv_w[0]).reshape(3, 44, 128).transpose(2, 0, 1)),
        "convb_t": f(np.asarray(conv_b[0]).reshape(44, 128).T),
    }
    if "nc" not in _CACHE:
        _CACHE["nc"] = build_program(DO_ATTN)
    nc = _CACHE["nc"]
    in_maps = []
    zeros = np.zeros((NTOK, D), dtype=np.float32)
    for c in range(8):
        b, half = c // 2, c % 2
        m = dict(shared)
        m["x_own"] = np.ascontiguousarray(x[b, half * NTOK:(half + 1) * NTOK])
        m["x_pre"] = np.ascontiguousarray(x[b, 0:NTOK]) if half == 1 else zeros
        m.update(_host_consts(half))
        in_maps.append(m)
    res = run_bass_kernel_spmd(nc, in_maps, core_ids=list(range(8)))
    out = np.empty((4, 2 * NTOK, D), dtype=np.float32)
    for c in range(8):
        b, half = c // 2, c % 2
        out[b, half * NTOK:(half + 1) * NTOK] = res.results[c]["y"]
    return out
```

```python
import numpy as np
import ml_dtypes
from contextlib import ExitStack
import concourse.bass as bass
import concourse.mybir as mybir
from concourse.bass_utils import run_bass_kernel_spmd

F32 = mybir.dt.float32
BF16 = mybir.dt.bfloat16
AF = mybir.ActivationFunctionType
ALU = mybir.AluOpType
AX = mybir.AxisListType

D = 1024
NTOK = 4096
NH = 8
HD = 64
DFF = 2816
EPS = 1e-6
BIG = 30000.0
NQT = NTOK + 128
ENGS = ["pe", "act", "dve", "pool", "sp"]


class Prog:
    def __init__(self):
        self.q = {e: [] for e in ENGS}
        self.cnt = {}
        self.seen = {e: {} for e in ENGS}
        self.pending = {e: [] for e in ENGS}

    def _filter(self, eng, waits):
        mx = {}
        for w in list(self.pending[eng]) + list(waits):
            if w is None:
                continue
            k, v = w
            if v > mx.get(k, 0):
                mx[k] = v
        out = []
        for k, v in mx.items():
            if self.seen[eng].get(k, 0) >= v:
                continue
            self.seen[eng][k] = v
            out.append((k, v))
        self.pending[eng] = []
        return out

    def I(self, eng, method, waits=(), sig=True, **kw):
        w = self._filter(eng, waits)
        inc = None
        tok = None
        if sig:
            self.cnt[eng] = self.cnt.get(eng, 0) + 1
            inc = (eng, 1)
            tok = (eng, self.cnt[eng])
        self.q[eng].append((w, method, kw, inc))
        return tok

    def dma(self, queue, semkey, waits=(), **kw):
        w = self._filter(queue, waits)
        self.cnt[semkey] = self.cnt.get(semkey, 0) + 16
        self.q[queue].append((w, "dma_start", kw, (semkey, 16)))
        return (semkey, self.cnt[semkey])

    def barrier(self):
        toks = [(k, v) for k, v in self.cnt.items()]
        for e in ENGS:
            self.pending[e] = list(toks)

    def final_wait(self, eng, toks):
        w = self._filter(eng, toks)
        self.q[eng].append((w, None, None, None))

    def sem_keys(self):
        return list(self.cnt.keys())


class Buf:
    def __init__(self, t):
        self.t = t
        self.free = []
        self.closed = True

    def acquire(self):
        assert self.closed, "buffer re-acquired while a previous user is still emitting"
        self.closed = False
        return self.take()

    def close(self):
        self.closed = True

    def take(self):
        f = self.free
        self.free = []
        return f

    def used(self, tok):
        if tok is not None:
            self.free.append(tok)


class Ring:
    def __init__(self, bufs):
        self.bufs = [Buf(b) for b in bufs]
        self.i = 0

    def next(self):
        b = self.bufs[self.i % len(self.bufs)]
        self.i += 1
        return b


def run_tasks(gens, width):
    active = []
    it = iter(gens)
    exhausted = False
    while True:
        if not exhausted and len(active) < width:
            g = next(it, None)
            if g is None:
                exhausted = True
            else:
                active.append(g)
        if not active and exhausted:
            break
        for g in list(active):
            try:
                next(g)
            except StopIteration:
                active.remove(g)


def build_program(do_attn=True):
    nc = bass.Bass("TRN2", target_bir_lowering=False)
    P = Prog()

    def din(name, shape, dt=F32):
        return nc.dram_tensor(name, list(shape), dt, kind="ExternalInput").ap()

    x_own = din("x_own", [NTOK, D])
    x_pre = din("x_pre", [NTOK, D])
    w_in = din("w_in", [D, 4096])
    w_pool = din("w_pool", [4, 128, 128])
    w_ba = din("w_ba", [512, D])
    w_bp = din("w_bp", [512, D])
    w_out = din("w_out", [D, D])
    w_up = din("w_up", [D, 2 * DFF])
    w_down = din("w_down", [DFF, D])
    gmix_b = din("gmix_b", [128, D])
    gffn_b = din("gffn_b", [128, D])
    gq_b = din("gq_b", [128, 512])
    gk_b = din("gk_b", [128, 512])
    bgate_t = din("bgate_t", [128, 16])
    pscale_t = din("pscale_t", [128, 4])
    convw_t = din("convw_t", [128, 3, 44])
    convb_t = din("convb_t", [128, 44])
    ident_d = din("ident_bf", [128, 128], BF16)
    cbias_d = din("cbias_bf", [128, 128], BF16)
    cos_d = din("cos_t", [128, 64, 16])
    sin_d = din("sin_t", [128, 64, 16])
    gmixt_d = din("gmix_t", [128, 8])
    rbias_d = din("rbias", [128, 33, 32])
    invcnt_d = din("invcnt", [128, 4, 16])
    hflag_d = din("hflag", [128, 1])
    y = nc.dram_tensor("y", [NTOK, D], F32, kind="ExternalOutput").ap()

    KT = nc.dram_tensor("KT_s", [NH, 96, 2 * NTOK], BF16).ap()
    VS = nc.dram_tensor("VS_s", [NH, 128, 64, 64], BF16).ap()
    QT = nc.dram_tensor("QT_s", [NH, 96, NQT], BF16).ap()
    G0 = nc.dram_tensor("G0_s", [8, 128, NQT], BF16).ap()
    GP = nc.dram_tensor("GP_s", [8, 128, NQT], BF16).ap()
    AT = nc.dram_tensor("AT_s", [4, 128, NQT], BF16).ap()

    es_all = ExitStack()
    esA = ExitStack()

    def sb(es, name, shape, dt):
        return es.enter_context(nc.sbuf_tensor("sb_" + name, list(shape), dt))

    def ps(es, name, shape, dt):
        return es.enter_context(nc.psum_tensor("ps_" + name, list(shape), dt))

    gffn = sb(es_all, "gffn", [128, D], F32)
    bgate = sb(es_all, "bgate", [128, 16], F32)
    pscale = sb(es_all, "pscale", [128, 4], F32)
    convw = sb(es_all, "convw", [128, 3, 44], F32)
    convb = sb(es_all, "convb", [128, 44], F32)
    ident = sb(es_all, "ident", [128, 128], BF16)
    cbias = sb(es_all, "cbias", [128, 128], BF16)
    hflag = sb(es_all, "hflag", [128, 1], F32)
    neghalf = sb(es_all, "neghalf", [128, 8], F32)
    ones_f = sb(es_all, "ones_f", [128, 64], F32)
    ssq_all = sb(es_all, "ssq_all", [128, 160], F32)
    rs_all = sb(es_all, "rs_all", [128, 160], F32)
    junk = sb(es_all, "junk", [128, D], F32)
    carry = sb(es_all, "carry", [128, 44, 2], F32)
    gmix = sb(esA, "gmix", [128, D], F32)
    gq = sb(esA, "gq", [128, 512], F32)
    gk = sb(esA, "gk", [128, 512], F32)
    cos_t = sb(esA, "cos_t", [128, 64, 16], F32)
    sin_t = sb(esA, "sin_t", [128, 64, 16], F32)
    gmix_t = sb(esA, "gmix_t", [128, 8], F32)
    rbias = sb(esA, "rbias", [128, 33, 32], F32)
    invcnt = sb(esA, "invcnt", [128, 4, 16], F32)

    c_toks = []
    for dst, src in [(gmix, gmix_b), (gffn, gffn_b), (gq, gq_b), (gk, gk_b), (bgate, bgate_t),
                     (pscale, pscale_t), (convw, convw_t), (convb, convb_t), (ident, ident_d),
                     (cbias, cbias_d), (cos_t, cos_d), (sin_t, sin_d), (rbias, rbias_d),
                     (invcnt, invcnt_d), (hflag, hflag_d), (gmix_t, gmixt_d)]:
        c_toks.append(P.dma("sp", "const", out=dst[:], in_=src))
    CONST = c_toks[-1]
    t_nh = P.I("pool", "memset", ap=neghalf[:], constant=-0.5)
    t_ones = P.I("pool", "memset", ap=ones_f[:], constant=1.0)
    norm_idx = [0]
    junk_tok = [None]

    def rmsnorm_tile(x_ap, g_t, out_bf, waits):
        i = norm_idx[0]
        norm_idx[0] += 1
        t1 = P.I("act", "activation", waits=list(waits) + [junk_tok[0]], out=junk[:], in_=x_ap, func=AF.Square,
                 accum_out=ssq_all[:, i:i + 1])
        junk_tok[0] = t1
        t2 = P.I("pool", "tensor_scalar", waits=[t1, t_nh], out=rs_all[:, i:i + 1], in0=ssq_all[:, i:i + 1],
                 scalar1=1.0 / D, scalar2=EPS, op0=ALU.mult, op1=ALU.add)
        t3 = P.I("pool", "tensor_tensor", waits=[t2], out=rs_all[:, i:i + 1], in0=rs_all[:, i:i + 1],
                 in1=neghalf[:, 0:1], op=ALU.pow)
        return t3, i

    spill_toks = []
    w_in_v = w_in.rearrange("(kt p) n -> p kt n", p=128)
    xr = Ring([sb(esA, f"xa{i}", [128, D], F32) for i in range(6)])
    hbr = Ring([sb(esA, f"hb{i}", [128, D], BF16) for i in range(3)])

    def x_chain(src, row0, trx_ring, get_dst):
        xb = xr.next()
        w = xb.acquire()
        t_ld = P.dma("sp", f"xa{(xr.i - 1) % len(xr.bufs)}", waits=w, out=xb.t[:], in_=src[row0:row0 + 128, :])
        yield
        i = norm_idx[0]
        norm_idx[0] += 1
        t1 = P.I("act", "activation", waits=[t_ld, junk_tok[0]], out=junk[:], in_=xb.t[:], func=AF.Square, accum_out=ssq_all[:, i:i + 1])
        junk_tok[0] = t1
        yield
        t2 = P.I("pool", "tensor_scalar", waits=[t1, t_nh], out=rs_all[:, i:i + 1], in0=ssq_all[:, i:i + 1],
                 scalar1=1.0 / D, scalar2=EPS, op0=ALU.mult, op1=ALU.add)
        t_rs = P.I("pool", "tensor_tensor", waits=[t2], out=rs_all[:, i:i + 1], in0=rs_all[:, i:i + 1],
                   in1=neghalf[:, 0:1], op=ALU.pow)
        hb = hbr.next()
        w = hb.acquire()
        t_h = P.I("pool", "tensor_scalar", waits=[t_rs, t_ld] + w, out=hb.t[:], in0=xb.t[:],
                  scalar1=rs_all[:, i:i + 1], scalar2=1.0, op0=ALU.mult, op1=ALU.mult)
        xb.used(t_h)
        xb.close()
        yield
        tr = trx_ring.next()
        wtr = tr.acquire()
        for kt in range(8):
            t_tr = P.I("pe", "transpose", waits=([t_h, CONST] + wtr) if kt == 0 else (), sig=(kt == 7),
                       out=tr.t[:, kt, :], in_=hb.t[:, kt * 128:(kt + 1) * 128], identity=ident[:])
        hb.used(t_tr)
        hb.close()
        dst_ap, dst_waits = get_dst()
        t_cp = P.I("act", "activation", waits=[t_tr] + dst_waits, out=dst_ap, in_=tr.t[:], func=AF.Copy)
        tr.used(t_cp)
        tr.close()
        return t_cp

    if do_attn:
        esA1 = ExitStack()
        w_qkv = sb(esA1, "w_qkv", [128, 8, 1536], BF16)
        hTt = Ring([sb(esA1, f"hTt{i}", [128, 8, 128], BF16) for i in range(3)])
        Kaug = Ring([sb(esA1, f"Kaug{i}", [128, 8, 96], BF16) for i in range(4)])
        Qaug = Ring([sb(esA1, f"Qaug{i}", [128, 8, 96], BF16) for i in range(7)])
        KTst = Ring([sb(esA1, f"KTst{i}", [96, 8, 512], BF16) for i in range(2)])
        QTst = Ring([sb(esA1, f"QTst{i}", [96, 8, 512], BF16) for i in range(2)])
        Vst = Ring([sb(esA1, f"Vst{i}", [128, 8, 4, 64], BF16) for i in range(2)])
        tq = Ring([sb(esA1, f"tq{i}", [128, 8, 64], F32) for i in range(10)])
        sqb = Ring([sb(esA1, f"sqb{i}", [128, 8, 64], F32) for i in range(6)])
        ssqh = sb(esA1, "ssqh", [128, 100, 8], F32)
        rp = Ring([sb(esA1, f"rp{i}", [128, 2, 8, 16], F32) for i in range(4)])
        kmT = sb(esA1, "kmT", [64, 8, 32], BF16)
        kms = sb(esA1, "kms", [64, 8], F32)
        sbr = Ring([sb(esA1, f"sbr{i}", [128, 8, 32], F32) for i in range(3)])
        m8 = Ring([sb(esA1, f"m8_{i}", [128, 8, 8], F32) for i in range(2)])
        thr = Ring([sb(esA1, f"thr{i}", [128, 8], F32) for i in range(2)])
        ger = Ring([sb(esA1, f"ge{i}", [128, 8, 32], F32) for i in range(2)])
        trx1 = Ring([ps(esA1, f"trx1_{i}", [128, 8, 128], BF16) for i in range(2)])
        pq = Ring([ps(esA1, f"pq{i}", [128, 512], F32) for i in range(3)])
        trk = Ring([ps(esA1, f"trk{i}", [128, 8, 128], BF16) for i in range(2)])
        prt = Ring([ps(esA1, "prt", [128, 8, 32], F32)])

        WK = None
        for (c0, c1) in [(512, 1536), (0, 512)]:
            for kt in range(8):
                tk = P.dma("pool", f"w{c0}", out=w_qkv[:, kt, c0:c1], in_=w_in_v[:, kt, c0:c1])
            if c0 == 512:
                WKV = tk
            else:
                WQ = tk
        for kt in range(8):
            t_f = P.I("dve", "tensor_scalar", waits=[WKV, WQ, CONST], out=w_qkv[:, kt, :], in0=w_qkv[:, kt, :],
                      scalar1=gmix_t[:, kt:kt + 1], scalar2=None, op0=ALU.mult)
        WKV = t_f
        WQ = t_f
        t_km0 = P.I("pool", "memset", ap=kmT[:], constant=0.0)
        kmT_tok = [t_km0]
        qk_idx = [0]
        kst_state, vst_state, qst_state = {}, {}, {}

        def qk_evac(pb, t_mm, g_t):
            pqv = pb.t[:].rearrange("p (h d) -> p h d", h=8)
            sq = sqb.next()
            t_sq = P.I("act", "activation", waits=[t_mm] + sq.acquire(), out=sq.t[:], in_=pqv, func=AF.Square)
            t = tq.next()
            t_g = P.I("dve", "tensor_tensor", waits=[t_mm, t_sq, CONST] + t.acquire(), out=t.t[:], in0=pqv,
                      in1=g_t[:].rearrange("p (h d) -> p h d", h=8), op=ALU.mult)
            pb.used(t_g)
            pb.used(t_sq)
            pb.close()
            return {"sq": sq, "t_sq": t_sq, "t": t, "t_g": t_g}

        def qk_reduce(st):
            i = qk_idx[0]
            qk_idx[0] += 1
            st["i"] = i
            t_red = P.I("dve", "tensor_reduce", waits=[st["t_sq"]], out=ssqh[:, i, :], in_=st["sq"].t[:], axis=AX.X, op=ALU.add)
            st["sq"].used(t_red)
            st["sq"].close()
            st["t_red"] = t_red

        def qk_pow(st):
            i = st["i"]
            t_r1 = P.I("pool", "tensor_scalar", waits=[st["t_red"], t_nh], out=ssqh[:, i, :], in0=ssqh[:, i, :],
                       scalar1=1.0 / HD, scalar2=EPS, op0=ALU.mult, op1=ALU.add)
            st["t_r2"] = P.I("pool", "tensor_tensor", waits=[t_r1], out=ssqh[:, i, :], in0=ssqh[:, i, :],
                             in1=neghalf[:], op=ALU.pow)

        def qk_rope(st, tile_idx, aug):
            t, t_g, t_r2, i = st["t"], st["t_g"], st["t_r2"], st["i"]
            r = rp.next()
            ccb = cos_t[:, tile_idx, :].unsqueeze(1).to_broadcast([128, 8, 16])
            ssb = sin_t[:, tile_idx, :].unsqueeze(1)
            x16 = t.t[:, :, 0:16]
            x1 = t.t[:, :, 0:8]
            x2 = t.t[:, :, 8:16]
            wr = r.acquire()
            ta = P.I("dve", "tensor_tensor", waits=[t_g] + wr, out=r.t[:, 0], in0=x16, in1=ccb, op=ALU.mult)
            tb = P.I("dve", "tensor_tensor", waits=[t_g], out=r.t[:, 1, :, 0:8], in0=x2,
                     in1=ssb[:, :, 0:8].to_broadcast([128, 8, 8]), op=ALU.mult)
            tc = P.I("dve", "tensor_tensor", waits=[t_g], out=r.t[:, 1, :, 8:16], in0=x1,
                     in1=ssb[:, :, 8:16].to_broadcast([128, 8, 8]), op=ALU.mult)
            te = P.I("dve", "tensor_tensor", waits=[ta, tb, tc], out=x16, in0=r.t[:, 0], in1=r.t[:, 1], op=ALU.add)
            tf = te
            r.used(te)
            r.close()
            t_fin = P.I("dve", "tensor_tensor", waits=[te, tf, t_r2] + aug.acquire(), out=aug.t[:, :, 0:64], in0=t.t[:],
                        in1=ssqh[:, i, :].unsqueeze(2).to_broadcast([128, 8, 64]), op=ALU.mult)
            t.used(t_fin)
            t.close()
            return t_fin

        def stage_get(state, key, ring):
            if key not in state:
                b = ring.next()
                state[key] = {"buf": b, "w": b.acquire(), "toks": [], "n": 0, "slot": (ring.i - 1) % len(ring.bufs)}
            return state[key]

        def taskA1(kind, idx):
            if kind == "pre":
                src, row0, ktile, groups, mkey, j = x_pre, idx * 128, idx, ["k", "v"], ("pre", idx // 4), idx % 4
            elif kind == "halo":
                src, row0, ktile, groups, mkey, j = x_pre, 3968, 31, ["q"], ("halo", 0), 0
            else:
                src, row0, ktile, groups, mkey, j = x_own, (idx - 32) * 128, idx, ["q", "k", "v"], ("own", (idx - 32) // 4), idx % 4
            nt_macro = 1 if kind == "halo" else 4
            qtile = 0 if kind == "halo" else 1 + (idx - 32)
            slot = ktile // 2
            hb_box = []

            def get_dst():
                hb_ = hTt.next()
                hb_box.append(hb_)
                return hb_.t[:], hb_.acquire()

            t_hT = yield from x_chain(src, row0, trx1, get_dst)
            hb_ = hb_box[0]
            yield
            col0 = {"q": 0, "k": 512, "v": 1024}
            st = {}
            for gname in groups:
                pb = pq.next()
                wpb = pb.acquire()
                for kt in range(8):
                    t_mm = P.I("pe", "matmul", waits=([t_hT, WKV if gname != "q" else WQ] + wpb) if kt == 0 else (), sig=(kt == 7),
                               out=pb.t[:], lhsT=hb_.t[:, kt, :], rhs=w_qkv[:, kt, col0[gname]:col0[gname] + 512],
                               start=(kt == 0), stop=(kt == 7))
                hb_.used(t_mm)
                if gname == "v":
                    vs = stage_get(vst_state, mkey, Vst)
                    t_v = P.I("act", "activation", waits=[t_mm] + vs["w"], out=vs["buf"].t[:, :, j, :],
                              in_=pb.t[:].rearrange("p (h d) -> p h d", h=8), func=AF.Copy)
                    vs["w"] = []
                    pb.used(t_v)
                    pb.close()
                    vs["toks"].append(t_v)
                    vs["n"] += 1
                    if vs["n"] == 4:
                        t_sv = P.dma("sp", f"vst{vs['slot']}", waits=vs["toks"], out=VS[:, :, ktile - 3:ktile + 1, :].rearrange("h p t c -> p h t c"),
                                     in_=vs["buf"].t[:])
                        vs["buf"].used(t_sv)
                        vs["buf"].close()
                        spill_toks.append(t_sv)
                else:
                    st[gname] = qk_evac(pb, t_mm, gk if gname == "k" else gq)
            hb_.close()
            yield
            for gname in st:
                qk_reduce(st[gname])
            yield
            for gname in st:
                qk_pow(st[gname])
            yield
            if "k" in groups:
                ka = Kaug.next()
                t_k = qk_rope(st["k"], ktile, ka)
            if "q" in groups:
                qa = Qaug.next()
                t_q = qk_rope(st["q"], ktile, qa)
            yield
            if "k" in groups:
                t_z = P.I("pool", "memset", waits=[t_k], ap=ka.t[:, :, 64:96], constant=0.0)
                t_o = P.I("pool", "memset", waits=[t_z], ap=ka.t[:, :, 64 + slot:65 + slot], constant=1.0)
            if "q" in groups:
                tkb = trk.next()
                wtk = tkb.acquire()
                for h in range(8):
                    t_tq = P.I("pe", "transpose", waits=([t_q] + wtk) if h == 0 else (), sig=(h == 7),
                               out=tkb.t[0:64, h, :], in_=qa.t[:, h, 0:64], identity=ident[:])
                qs = stage_get(qst_state, mkey, QTst)
                t_qc = P.I("act", "activation", waits=[t_tq] + qs["w"], out=qs["buf"].t[0:64, :, j * 128:(j + 1) * 128],
                           in_=tkb.t[0:64], func=AF.Copy)
                qs["w"] = []
                tkb.used(t_qc)
                tkb.close()
            yield
            if "k" in groups:
                tkb = trk.next()
                wtk = tkb.acquire()
                for h in range(8):
                    t_tk = P.I("pe", "transpose", waits=([t_k, t_o] + wtk) if h == 0 else (), sig=(h == 7),
                               out=tkb.t[0:96, h, :], in_=ka.t[:, h, :], identity=ident[:])
                ka.used(t_tk)
                ka.close()
                ks = stage_get(kst_state, mkey, KTst)
                t_kc = P.I("act", "activation", waits=[t_tk] + ks["w"], out=ks["buf"].t[:, :, j * 128:(j + 1) * 128],
                           in_=tkb.t[0:96], func=AF.Copy)
                ks["w"] = []
                tkb.used(t_kc)
                tkb.close()
                ks["toks"].append(t_kc)
                ks["n"] += 1
                k_last = (ks["n"] == 4)
            if "q" in groups:
                pr = prt.next()
                wpr = pr.acquire()
                for h in range(8):
                    t_rt = P.I("pe", "matmul", waits=([t_qc, kmT_tok[0]] + wpr) if h == 0 else (), sig=(h == 7),
                               out=pr.t[:, h, :], lhsT=qs["buf"].t[0:64, h, j * 128:(j + 1) * 128],
                               rhs=kmT[:, h, :], start=True, stop=True)
                sbb = sbr.next()
                t_sb = P.I("dve", "tensor_tensor", waits=[t_rt, CONST] + sbb.acquire(), out=sbb.t[:], in0=pr.t[:],
                           in1=rbias[:, qtile, :].unsqueeze(1).to_broadcast([128, 8, 32]), op=ALU.add)
                pr.used(t_sb)
                pr.close()
            yield
            if "k" in groups:
                if ktile % 2 == 1:
                    jb = j - 1
                    t_ks = P.I("dve", "tensor_reduce", waits=[ks["toks"][-1], ks["toks"][-2], kmT_tok[0]], out=kms[:],
                               in_=ks["buf"].t[0:64, :, jb * 128:(jb + 2) * 128], axis=AX.X, op=ALU.add)
                    t_km = P.I("dve", "tensor_scalar", waits=[t_ks], out=kmT[:, :, slot], in0=kms[:],
                               scalar1=1.0 / 256.0, scalar2=None, op0=ALU.mult)
                    kmT_tok[0] = t_km
                    ks["buf"].used(t_km)
                if k_last:
                    kbase = (ktile - 3) * 128
                    t_st = P.dma("sp", f"kst{ks['slot']}", waits=ks["toks"], out=KT[:, :, kbase:kbase + 512].rearrange("h r t -> r h t"),
                                 in_=ks["buf"].t[:])
                    ks["buf"].used(t_st)
                    ks["buf"].close()
                    spill_toks.append(t_st)
            if "q" not in groups:
                return
            mb = m8.next()
            wm = mb.acquire()
            for h in range(8):
                t_m8 = P.I("dve", "max", waits=([t_sb] + wm) if h == 0 else (), sig=(h == 7),
                           out=mb.t[:, h, :], in_=sbb.t[:, h, :])
            th = thr.next()
            t_th = P.I("dve", "tensor_scalar", waits=[t_m8] + th.acquire(), out=th.t[:], in0=mb.t[:, :, 2],
                       scalar1=-1.0e4, scalar2=None, op0=ALU.max)
            mb.used(t_th)
            mb.close()
            gb = ger.next()
            t_ge = P.I("dve", "tensor_tensor", waits=[t_th, t_sb] + gb.acquire(), out=gb.t[:], in0=sbb.t[:],
                       in1=th.t[:].unsqueeze(2).to_broadcast([128, 8, 32]), op=ALU.is_ge)
            sbb.used(t_ge)
            sbb.close()
            th.used(t_ge)
            th.close()
            t_mk = P.I("dve", "tensor_scalar", waits=[t_ge, t_q, t_tq], out=qa.t[:, :, 64:96], in0=gb.t[:],
                       scalar1=-1.0, scalar2=BIG, op0=ALU.add, op1=ALU.mult)
            gb.used(t_mk)
            gb.close()
            t_mk = P.I("dve", "memset", waits=[t_mk], ap=qa.t[:, :, 64 + slot:65 + slot], constant=0.0)
            yield
            tkb = trk.next()
            wtk = tkb.acquire()
            for h in range(8):
                t_tm = P.I("pe", "transpose", waits=([t_mk] + wtk) if h == 0 else (), sig=(h == 7),
                           out=tkb.t[0:32, h, :], in_=qa.t[:, h, 64:96], identity=ident[:])
            qa.used(t_tm)
            qa.close()
            t_mc = P.I("dve", "tensor_copy", waits=[t_tm, t_rt], out=qs["buf"].t[64:96, :, j * 128:(j + 1) * 128],
                       in_=tkb.t[0:32])
            tkb.used(t_mc)
            tkb.close()
            qs["toks"].append(t_mc)
            qs["n"] += 1
            if qs["n"] == nt_macro:
                N = nt_macro * 128
                qoff = 0 if kind == "halo" else 128 + (idx - 32 - 3) * 128
                t_sq = P.dma("sp", f"qst{qs['slot']}", waits=qs["toks"], out=QT[:, :, qoff:qoff + N].rearrange("h r t -> r h t"),
                             in_=qs["buf"].t[:, :, 0:N])
                qs["buf"].used(t_sq)
                qs["buf"].close()
                spill_toks.append(t_sq)

        gensA1 = [taskA1("pre", i) for i in range(32)] + [taskA1("halo", 0)] + [taskA1("own", 32 + i) for i in range(32)]
        run_tasks(gensA1, 16)
        P.barrier()
        esA1.close()

    esA2 = ExitStack()
    w_pg = sb(esA2, "w_pg", [128, 8, 2560], BF16)
    w_bp_bf = sb(esA2, "w_bp_bf", [128, 4, D], BF16)
    w_pool_bf = sb(esA2, "w_pool_bf", [128, 4, 128], BF16)
    hTm = Ring([sb(esA2, f"hTm{i}", [128, 8, 512], BF16) for i in range(2)])
    G0st = Ring([sb(esA2, f"G0st{i}", [128, 8, 512], BF16) for i in range(2)])
    GPst = Ring([sb(esA2, f"GPst{i}", [128, 8, 512], BF16) for i in range(2)])
    ubs = [[sb(esA2, f"ub{g}_{i}", [128, 16 + 512], F32) for i in range(3)] for g in range(4)]
    ucarry = sb(esA2, "ucarry", [128, 4, 16], F32)
    pooled = Ring([sb(esA2, f"pooled{i}", [128, 512], BF16) for i in range(4)])
    pmr = Ring([sb(esA2, f"pm{i}", [128, 4, 512], BF16) for i in range(2)])
    g1r = Ring([sb(esA2, f"g1_{i}", [128, 512], F32) for i in range(5)])
    tmp16 = sb(esA2, "tmp16", [128, 4, 16], F32)
    trx2 = Ring([ps(esA2, f"trx2_{i}", [128, 8, 128], BF16) for i in range(2)])
    pf = Ring([ps(esA2, f"pf{i}", [128, 512], F32) for i in range(6)])

    for kt in range(8):
        WP = P.dma("pool", "wpg0", out=w_pg[:, kt, 0:512], in_=w_in_v[:, kt, 1536:2048])
    for kt in range(8):
        WG = P.dma("pool", "wpg1", out=w_pg[:, kt, 512:2560], in_=w_in_v[:, kt, 2048:4096])
    for kt in range(8):
        t_f = P.I("dve", "tensor_scalar", waits=[WP, CONST], out=w_pg[:, kt, 0:512], in0=w_pg[:, kt, 0:512],
                  scalar1=gmix_t[:, kt:kt + 1], scalar2=None, op0=ALU.mult)
    WP = t_f
    for kt in range(8):
        t_f = P.I("dve", "tensor_scalar", waits=[WG, CONST], out=w_pg[:, kt, 512:2560], in0=w_pg[:, kt, 512:2560],
                  scalar1=gmix_t[:, kt:kt + 1], scalar2=None, op0=ALU.mult)
    WG = t_f
    W_BP = P.dma("pool", "wbp", out=w_bp_bf[:], in_=w_bp.rearrange("(g p) n -> p g n", p=128))
    W_POOL = P.dma("pool", "wpool", out=w_pool_bf[:], in_=w_pool.rearrange("g c d -> c g d"))
    t_uc0 = P.I("pool", "memset", ap=ucarry[:], constant=0.0)
    ucarry_tok = [t_uc0] * 4
    ub_free = [[] for _ in range(4)]
    for g_ in range(4):
        for i_ in (1, 2):
            ub_free[g_].append(P.I("pool", "memset", ap=ubs[g_][i_][:], constant=0.0))
    macros = [("halo", 0)] + [("own", m) for m in range(8)]
    mstate = {}

    def mget(mi):
        if mi not in mstate:
            hb_ = hTm.next()
            mstate[mi] = {"hT": hb_, "hT_w": hb_.acquire(), "hT_toks": [], "hT_users": 0,
                          "pm": None, "pm_toks": {}, "pm_users": 0, "g0": None, "gp": None, "g0_toks": [], "gp_toks": []}
        return mstate[mi]

    def hT_user_done(ms, tok, total):
        ms["hT"].used(tok)
        ms["hT_users"] += 1
        if ms["hT_users"] == total:
            ms["hT"].close()

    def taskX(mi, j):
        kind, m = macros[mi]
        src = x_pre if kind == "halo" else x_own
        row0 = 3968 if kind == "halo" else m * 512 + j * 128

        def get_dst():
            ms = mget(mi)
            w = ms["hT_w"]
            ms["hT_w"] = []
            return ms["hT"].t[:, :, j * 128:(j + 1) * 128], w

        t = yield from x_chain(src, row0, trx2, get_dst)
        mget(mi)["hT_toks"].append(t)

    def taskPool(mi, g):
        kind, m = macros[mi]
        N = 128 if kind == "halo" else 512
        ms = mget(mi)
        while len(ms["hT_toks"]) < N // 128:
            yield
        ub = ubs[g]
        w = 2 << g
        pb = pf.next()
        wpb = pb.acquire()
        for kt in range(8):
            t_mm = P.I("pe", "matmul", waits=(ms["hT_toks"] + [WP] + wpb) if kt == 0 else (), sig=(kt == 7),
                       out=pb.t[:, 0:N], lhsT=w_pg[:, kt, g * 128:(g + 1) * 128],
                       rhs=ms["hT"].t[:, kt, 0:N], start=(kt == 0), stop=(kt == 7))
        hT_user_done(ms, t_mm, 20)
        wub = ub_free[g]
        t_c0 = P.I("pool", "tensor_copy", waits=wub + [ucarry_tok[g]], out=ub[0][:, 0:16], in_=ucarry[:, g, :])
        t_u = P.I("act", "activation", waits=[t_mm] + wub, out=ub[0][:, 16:16 + N], in_=pb.t[:, 0:N], func=AF.Copy)
        pb.used(t_u)
        pb.close()
        if kind == "halo":
            ucarry_tok[g] = P.I("pool", "tensor_scalar", waits=[t_u, t_c0, CONST], out=ucarry[:, g, :], in0=ub[0][:, N:N + 16],
                                scalar1=hflag[:, 0:1], scalar2=None, op0=ALU.mult)
        else:
            ucarry_tok[g] = P.I("pool", "tensor_copy", waits=[t_u, t_c0], out=ucarry[:, g, :], in_=ub[0][:, N:N + 16])
        L = 16 + N
        src_i, last = 0, [t_u, t_c0]
        sh = 1
        for step in range(g + 1):
            dst_i = 1 if src_i != 1 else 2
            t_a = P.I("pool", "tensor_tensor", waits=last + wub, out=ub[dst_i][:, sh:L], in0=ub[src_i][:, sh:L],
                      in1=ub[src_i][:, 0:L - sh], op=ALU.add)
            last = [t_a]
            src_i = dst_i
            sh *= 2
        po = pooled.next()
        wpo = po.acquire()
        t_p = P.I("dve", "scalar_tensor_tensor", waits=last + [t_u, ucarry_tok[g]] + wpo, out=po.t[:, 0:N], in0=ub[src_i][:, 16:16 + N],
                  scalar=1.0 / w, in1=ub[0][:, 16:16 + N], op0=ALU.mult, op1=ALU.subtract)
        if kind == "own" and m == 0:
            t_p1 = P.I("dve", "tensor_tensor", waits=[t_p, CONST], out=tmp16[:, g, :], in0=ub[src_i][:, 16:32],
                       in1=invcnt[:, g, :], op=ALU.mult)
            t_p = P.I("dve", "tensor_tensor", waits=[t_p1], out=po.t[:, 0:16], in0=tmp16[:, g, :], in1=ub[0][:, 16:32],
                      op=ALU.subtract)
        ub_free[g] = [t_p]
        yield
        yield
        yield
        if ms["pm"] is None:
            ms["pm"] = pmr.next()
            ms["pm_w"] = ms["pm"].acquire()
        pb2 = pf.next()
        t_pm = P.I("pe", "matmul", waits=[t_p, W_POOL] + pb2.acquire(), out=pb2.t[:, 0:N], lhsT=w_pool_bf[:, g, :],
                   rhs=po.t[:, 0:N], start=True, stop=True)
        po.used(t_pm)
        po.close()
        t_ps = P.I("act", "activation", waits=[t_pm, CONST] + ms["pm_w"], out=ms["pm"].t[:, g, 0:N], in_=pb2.t[:, 0:N],
                   func=AF.Identity, scale=pscale[:, g:g + 1])
        ms["pm_w"] = []
        pb2.used(t_ps)
        pb2.close()
        ms["pm_toks"][g] = t_ps

    def taskGate(mi, f):
        kind, m = macros[mi]
        N = 128 if kind == "halo" else 512
        qoff = 0 if kind == "halo" else 128 + m * 512
        ms = mget(mi)
        while len(ms["hT_toks"]) < N // 128:
            yield
        if ms["g0"] is None:
            ms["g0"] = G0st.next()
            ms["gslot"] = (G0st.i - 1) % len(G0st.bufs)
            ms["g0_w"] = ms["g0"].acquire()
            ms["gp"] = GPst.next()
            ms["gp_w"] = ms["gp"].acquire()
        pb = pf.next()
        wpb = pb.acquire()
        for kt in range(8):
            t_mm = P.I("pe", "matmul", waits=(ms["hT_toks"] + [WG] + wpb) if kt == 0 else (), sig=(kt == 7),
                       out=pb.t[:, 0:N], lhsT=w_pg[:, kt, 512 + f * 128:512 + (f + 1) * 128],
                       rhs=ms["hT"].t[:, kt, 0:N], start=(kt == 0), stop=(kt == 7))
        hT_user_done(ms, t_mm, 20)
        t_s0 = P.I("act", "activation", waits=[t_mm, CONST] + ms["g0_w"], out=ms["g0"].t[:, f, 0:N], in_=pb.t[:, 0:N],
                   func=AF.Sigmoid, bias=bgate[:, f:f + 1])
        ms["g0_w"] = []
        pb.used(t_s0)
        pb.close()
        ms["g0_toks"].append(t_s0)
        pb = pf.next()
        wpb = pb.acquire()
        for kt in range(8):
            t_mm = P.I("pe", "matmul", waits=(ms["hT_toks"] + wpb) if kt == 0 else (), sig=(kt == 7),
                       out=pb.t[:, 0:N], lhsT=w_pg[:, kt, 1536 + f * 128:1536 + (f + 1) * 128],
                       rhs=ms["hT"].t[:, kt, 0:N], start=(kt == 0), stop=(kt == 7))
        hT_user_done(ms, t_mm, 20)
        g1 = g1r.next()
        t_s1 = P.I("act", "activation", waits=[t_mm] + g1.acquire(), out=g1.t[:, 0:N], in_=pb.t[:, 0:N],
                   func=AF.Sigmoid, bias=bgate[:, 8 + f:9 + f])
        pb.used(t_s1)
        pb.close()
        yield
        yield
        while len(ms["pm_toks"]) < 4:
            yield
        pb = pf.next()
        wpb = pb.acquire()
        for g in range(4):
            t_mm = P.I("pe", "matmul", waits=(list(ms["pm_toks"].values()) + [W_BP] + wpb) if g == 0 else (), sig=(g == 3),
                       out=pb.t[:, 0:N], lhsT=w_bp_bf[:, g, f * 128:(f + 1) * 128], rhs=ms["pm"].t[:, g, 0:N],
                       start=(g == 0), stop=(g == 3))
        ms["pm"].used(t_mm)
        ms["pm_users"] += 1
        if ms["pm_users"] == 8:
            ms["pm"].close()
        t_gp = P.I("dve", "tensor_tensor", waits=[t_mm, t_s1] + ms["gp_w"], out=ms["gp"].t[:, f, 0:N], in0=pb.t[:, 0:N],
                   in1=g1.t[:, 0:N], op=ALU.mult)
        ms["gp_w"] = []
        pb.used(t_gp)
        pb.close()
        g1.used(t_gp)
        g1.close()
        ms["gp_toks"].append(t_gp)
        if len(ms["gp_toks"]) == 8:
            t_s = P.dma("sp", f"g0st{ms['gslot']}", waits=ms["g0_toks"], out=G0[:, :, qoff:qoff + N].rearrange("f p t -> p f t"), in_=ms["g0"].t[:, :, 0:N])
            ms["g0"].used(t_s)
            ms["g0"].close()
            spill_toks.append(t_s)
            t_s = P.dma("sp", f"gpst{ms['gslot']}", waits=ms["gp_toks"], out=GP[:, :, qoff:qoff + N].rearrange("f p t -> p f t"), in_=ms["gp"].t[:, :, 0:N])
            ms["gp"].used(t_s)
            ms["gp"].close()
            spill_toks.append(t_s)

    gensA2 = []
    nx = lambda mi: (1 if macros[mi][0] == "halo" else 4)
    gensA2 += [taskX(0, 0)]
    for mi in range(len(macros)):
        fm = [taskPool(mi, g) for g in range(4)] + [taskGate(mi, f) for f in range(8)]
        nxt = [taskX(mi + 1, j) for j in range(nx(mi + 1))] if mi + 1 < len(macros) else []
        seq = []
        k = 0
        for i, t in enumerate(fm):
            seq.append(t)
            if i % 3 == 2 and k < len(nxt):
                seq.append(nxt[k])
                k += 1
        seq += nxt[k:]
        gensA2 += seq
    run_tasks(gensA2, 12)
    P.barrier()
    esA2.close()
    esA.close()

    esW = ExitStack()
    w_out_bf = sb(esW, "w_out_bf", [128, 8, D], BF16)
    w_down_bf = sb(esW, "w_down_bf", [128, 22, D], BF16)
    h2T_halo = sb(esW, "h2T_halo", [128, 8, 2], BF16)
    for kt in range(8):
        W_OUT = P.dma("pool", "wc1", out=w_out_bf[:, kt, :], in_=w_out[kt * 128:(kt + 1) * 128, :])
    for c in range(22):
        W_DN = P.dma("pool", "wc3", out=w_down_bf[:, c, :], in_=w_down[c * 128:(c + 1) * 128, :])
    esB = ExitStack()
    attnT = sb(esB, "attnT", [128, 4, NQT], BF16)
    if not do_attn:
        t_at = P.I("pool", "memset", ap=attnT[:], constant=0.0)
        at_toks = [t_at]
    if do_attn:
        at_toks = []
        KTh = Ring([sb(esB, f"KTh{i}", [96, 2 * NTOK], BF16) for i in range(2)])
        QTh = Ring([sb(esB, f"QTh{i}", [96, NQT], BF16) for i in range(2)])
        Vh = Ring([sb(esB, f"Vh{i}", [128, 64, 128], BF16) for i in range(2)])
        Pb = Ring([sb(esB, f"Pb{i}", [128, 3, 512], BF16) for i in range(4)])
        rden = Ring([sb(esB, f"rden{i}", [128, 512], F32) for i in range(2)])
        Sps = Ring([ps(esB, f"Sps{i}", [128, 3, 512], F32) for i in range(2)])
        Ops = Ring([ps(esB, f"Ops{i}", [128, 512], F32) for i in range(2)])
        for r in Vh.bufs:
            r.used(P.I("pool", "memset", ap=r.t[:, :, 64:128], constant=1.0))

        def group_items(gi):
            items = []
            if gi == 0:
                for s in range(15):
                    items += [(2 * s, 0, 128, False, False), (2 * s + 1, 0, 128, False, False)]
                items += [(30, 0, 128, False, True), (31, 0, 128, True, True)]
                return 0, 128, items
            i0 = 2 * (gi - 1)
            for s in range(16 + i0):
                items += [(2 * s, 0, 512, False, False), (2 * s + 1, 0, 512, False, False)]
            s0 = 16 + i0
            s1 = s0 + 1
            items += [(2 * s0, 256, 256, False, False), (2 * s0 + 1, 256, 256, False, False)]
            items += [(2 * s0, 0, 128, True, True), (2 * s0, 128, 128, False, True), (2 * s0 + 1, 128, 128, True, True)]
            items += [(2 * s1, 256, 128, True, True), (2 * s1, 384, 128, False, True), (2 * s1 + 1, 384, 128, True, True)]
            return 128 + i0 * 256, 512, items

        for h in range(NH):
            kb = KTh.next()
            qb = QTh.next()
            vb = Vh.next()
            ld_w = spill_toks if h < 2 else []
            t_lk = P.dma("sp", f"ldk{h % 2}", waits=ld_w + kb.take(), out=kb.t[:], in_=KT[h])
            t_lq = P.dma("sp", f"ldq{h % 2}", waits=qb.take(), out=qb.t[:], in_=QT[h])
            t_lv = P.dma("sp", f"ldv{h % 2}", waits=vb.take(), out=vb.t[:, :, 0:64], in_=VS[h])
            users = []
            for gi in range(9):
                q0, nq, items = group_items(gi)
                batches = []
                for it in items:
                    if batches and len(batches[-1]) < 3 and batches[-1][0][1:3] == it[1:3] and batches[-1][0][4] == it[4]:
                        batches[-1].append(it)
                    else:
                        batches.append([it])
                ob = Ops.next()
                ob_w = ob.take()
                nb = len(batches)
                exp_tok = [None] * nb
                pbuf = [None] * nb
                first_pv = [True]

                def emit_pv(bi):
                    for k, (ktile, qc0, qn, causal, own) in enumerate(batches[bi]):
                        is_last = (bi == nb - 1) and (k == len(batches[bi]) - 1)
                        t = P.I("pe", "matmul", waits=[exp_tok[bi], t_lv] + (ob_w if first_pv[0] else []), sig=True,
                                out=ob.t[:, qc0:qc0 + qn], lhsT=vb.t[:, ktile, :], rhs=pbuf[bi].t[:, k, 0:qn],
                                start=first_pv[0], stop=is_last, skip_group_check=True)
                        first_pv[0] = False
                    pbuf[bi].used(t)
                    return t

                t_pv = None
                for bi, batch in enumerate(batches):
                    sbuf_ = Sps.next()
                    ws = sbuf_.take()
                    for k, (ktile, qc0, qn, causal, own) in enumerate(batch):
                        nr = 96
                        t_qk = P.I("pe", "matmul", waits=([t_lk, t_lq] + ws) if k == 0 else (), sig=True,
                                   out=sbuf_.t[:, k, 0:qn], lhsT=kb.t[0:nr, ktile * 128:(ktile + 1) * 128],
                                   rhs=qb.t[0:nr, q0 + qc0:q0 + qc0 + qn], start=True, stop=not causal)
                        if causal:
                            t_qk = P.I("pe", "matmul", waits=[CONST], sig=True, out=sbuf_.t[:, k, 0:qn], lhsT=ident[:],
                                       rhs=cbias[:, 0:qn], start=False, stop=True)
                    qn = batch[0][2]
                    pb_ = Pb.next()
                    pbuf[bi] = pb_
                    exp_tok[bi] = P.I("act", "activation", waits=[t_qk] + pb_.take(), out=pb_.t[:, 0:len(batch), 0:qn],
                                      in_=sbuf_.t[:, 0:len(batch), 0:qn], func=AF.Exp, scale=HD ** -0.5)
                    sbuf_.used(exp_tok[bi])
                    if bi >= 1:
                        t_pv = emit_pv(bi - 1)
                t_pv = emit_pv(nb - 1)
                users.append(t_pv)
                rd = rden.next()
                t_rc = P.I("dve", "reciprocal", waits=[t_pv] + rd.take(), out=rd.t[64:128, 0:nq], in_=ob.t[64:128, 0:nq])
                po = (h % 2) * 64
                t_at = P.I("dve", "tensor_tensor", waits=[t_rc], out=attnT[po:po + 64, h // 2, q0:q0 + nq],
                           in0=ob.t[0:64, 0:nq], in1=rd.t[64:128, 0:nq], op=ALU.mult)
                ob.used(t_at)
                rd.used(t_at)
                at_toks.append(t_at)
            for u in users:
                kb.used(u)
                qb.used(u)
                vb.used(u)
    t_ats = P.dma("sp", "ats", waits=at_toks, out=AT.rearrange("g p t -> p g t"), in_=attnT[:])
    spill_toks.append(t_ats)
    P.barrier()
    esB.close()

    esC = ExitStack()
    w_ba_bf = sb(esC, "w_ba_bf", [128, 4, D], BF16)
    wup = Ring([sb(esC, f"wup{i}", [128, 8, 2, 256], BF16) for i in range(3)])
    G0l = Buf(sb(esC, "G0l", [128, 8, 512], BF16))
    GPl = Buf(sb(esC, "GPl", [128, 8, 512], BF16))
    attl = Buf(sb(esC, "attl", [128, 4, 512], BF16))
    tmpm = Ring([sb(esC, f"tmpm{i}", [128, 512], F32) for i in range(2)])
    x1 = Buf(sb(esC, "x1", [128, 4, D], F32))
    h2b = Ring([sb(esC, f"h2b{i}", [128, D], BF16) for i in range(2)])
    h2T = Buf(sb(esC, "h2T", [128, 8, 512], BF16))
    zb = Ring([sb(esC, f"zb{i}", [128, 2 + 512], F32) for i in range(2)])
    cv = Ring([sb(esC, f"cv{i}", [128, 512], F32) for i in range(3)])
    sg = Ring([sb(esC, f"sg{i}", [128, 512], F32) for i in range(2)])
    aT = Buf(sb(esC, "aT", [128, 22, 512], BF16))
    ob_ = Ring([sb(esC, f"ob{i}", [128, D], F32) for i in range(2)])
    pA = Ring([ps(esC, f"pA{i}", [128, 512], F32) for i in range(2)])
    pY = Ring([ps(esC, f"pY{i}", [128, 512], F32) for i in range(2)])
    trc = Ring([ps(esC, f"trc{i}", [128, 8, 128], BF16) for i in range(1)])
    pz = Ring([ps(esC, f"pz{i}", [128, 512], F32) for i in range(3)])

    W_BA = P.dma("pool", "wc0", out=w_ba_bf[:], in_=w_ba.rearrange("(g p) n -> p g n", p=128))
    w_up_v = w_up.rearrange("(kt p) n -> p kt n", p=128)

    chunk_tab = {}
    N_CHUNKS = 8 * 11
    chunk_ctr = [0]

    def issue_chunk(idx):
        if idx in chunk_tab or idx >= N_CHUNKS:
            return
        c2 = idx % 11
        wb = wup.next()
        key = f"wup{(wup.i - 1) % 3}"
        ww = wb.take()
        P.dma("pool", key, waits=ww, out=wb.t[:, :, 0, :], in_=w_up_v[:, :, c2 * 256:(c2 + 1) * 256])
        tk = P.dma("pool", key, out=wb.t[:, :, 1, :], in_=w_up_v[:, :, DFF + c2 * 256:DFF + (c2 + 1) * 256])
        chunk_tab[idx] = (wb, tk)

    def load_wup(c2):
        idx = chunk_ctr[0]
        chunk_ctr[0] += 1
        assert idx % 11 == c2
        issue_chunk(idx)
        issue_chunk(idx + 1)
        issue_chunk(idx + 2)
        return chunk_tab[idx]
    carry_tok = [None] * 44
    out_toks = []

    c_loads = {}
    halo_tok = [None]

    def issue_c_loads(kind, m):
        N = 128 if kind == "halo" else 512
        qoff = 0 if kind == "halo" else 128 + m * 512
        t_g0 = P.dma("sp", "ldg0", waits=spill_toks + G0l.take(), out=G0l.t[:, :, 0:N], in_=G0[:, :, qoff:qoff + N].rearrange("f p t -> p f t"))
        t_gp = P.dma("sp", "ldgp", waits=GPl.take(), out=GPl.t[:, :, 0:N], in_=GP[:, :, qoff:qoff + N].rearrange("f p t -> p f t"))
        t_al = P.dma("sp", "ldat", waits=attl.take(), out=attl.t[:, :, 0:N], in_=AT[:, :, qoff:qoff + N].rearrange("g p t -> p g t"))
        c_loads[(kind, m)] = (t_g0, t_gp, t_al)

    def phaseC_macro(kind, m):
        nt = 1 if kind == "halo" else 4
        N = nt * 128
        qoff = 0 if kind == "halo" else 128 + m * 512
        src = x_pre if kind == "halo" else x_own
        tok0 = 3968 if kind == "halo" else m * 512
        if (kind, m) not in c_loads:
            issue_c_loads(kind, m)
        t_g0, t_gp, t_al = c_loads[(kind, m)]
        t_x = P.dma("sp", "ldx1", waits=x1.take(), out=x1.t[:, 0:nt, :], in_=src[tok0:tok0 + N, :].rearrange("(j p) d -> p j d", p=128))
        mix_toks = []
        for f in range(8):
            pb = pA.next()
            wpb = pb.take()
            for g in range(4):
                t_mm = P.I("pe", "matmul", waits=([W_BA, t_al] + wpb) if g == 0 else (), sig=(g == 3), out=pb.t[:, 0:N],
                           lhsT=w_ba_bf[:, g, f * 128:(f + 1) * 128], rhs=attl.t[:, g, 0:N], start=(g == 0), stop=(g == 3))
            tm = tmpm.next()
            t_1 = P.I("dve", "tensor_tensor", waits=[t_mm, t_g0] + tm.take(), out=tm.t[:, 0:N], in0=pb.t[:, 0:N], in1=G0l.t[:, f, 0:N], op=ALU.mult)
            pb.used(t_1)
            t_2 = P.I("dve", "tensor_tensor", waits=[t_1, t_gp], out=GPl.t[:, f, 0:N], in0=tm.t[:, 0:N],
                      in1=GPl.t[:, f, 0:N], op=ALU.add)
            tm.used(t_2)
            mix_toks.append(t_2)
        attl.used(t_mm)
        G0l.used(mix_toks[-1])
        h2T_w = h2T.take()
        h2T_toks = []
        x1_toks = [None] * nt
        h2T_first = [True]
        last_h = [None]

        def c1_tile(j):
            xs = []
            for c in range(2):
                pb = pY.next()
                wpb = pb.take()
                for kt in range(8):
                    t_mm = P.I("pe", "matmul", waits=(mix_toks + [W_OUT] + wpb) if kt == 0 else (), sig=(kt == 7), out=pb.t[:],
                               lhsT=GPl.t[:, kt, j * 128:(j + 1) * 128], rhs=w_out_bf[:, kt, c * 512:(c + 1) * 512],
                               start=(kt == 0), stop=(kt == 7))
                GPl.used(t_mm)
                t_a = P.I("dve", "tensor_tensor", waits=[t_mm, t_x], out=x1.t[:, j, c * 512:(c + 1) * 512], in0=pb.t[:],
                          in1=x1.t[:, j, c * 512:(c + 1) * 512], op=ALU.add)
                pb.used(t_a)
                xs.append(t_a)
            x1_toks[j] = xs
            t_rs, ni = rmsnorm_tile(x1.t[:, j, :], gffn, None, xs)
            yield
            yield
            hb = h2b.next()
            t_h = P.I("dve", "scalar_tensor_tensor", waits=[t_rs, CONST] + xs + hb.take(), out=hb.t[:], in0=x1.t[:, j, :],
                      scalar=rs_all[:, ni:ni + 1], in1=gffn[:], op0=ALU.mult, op1=ALU.mult)
            last_h[0] = t_h
            tr = trc.next()
            wtr = tr.take()
            for kt in range(8):
                t_tr = P.I("pe", "transpose", waits=([t_h, CONST] + wtr) if kt == 0 else (), sig=(kt == 7), out=tr.t[:, kt, :],
                           in_=hb.t[:, kt * 128:(kt + 1) * 128], identity=ident[:])
            hb.used(t_tr)
            t_cp = P.I("act", "activation", waits=[t_tr] + (h2T_w if h2T_first[0] else []), out=h2T.t[:, :, j * 128:(j + 1) * 128], in_=tr.t[:], func=AF.Copy)
            h2T_first[0] = False
            tr.used(t_cp)
            h2T_toks.append(t_cp)

        run_tasks([c1_tile(j) for j in range(nt)], 4)
        t_h = last_h[0]
        nxt = c_order[c_order.index((kind, m)) + 1] if c_order.index((kind, m)) + 1 < len(c_order) else None
        if nxt is not None:
            issue_c_loads(*nxt)
        h2T_users = []
        if kind == "halo":
            t_hh = P.I("dve", "tensor_copy", waits=h2T_toks, out=h2T_halo[:], in_=h2T.t[:, :, 126:128])
            halo_tok[0] = t_hh
            h2T.used(t_hh)
            x1.used(t_h)
            return
        aT_w = aT.take()
        aT_toks = []
        for c2 in range(11):
            wb, t_w = load_wup(c2)
            for ci in range(2):
                c = 2 * c2 + ci
                ups = []
                for half in range(2):
                    cc = c + 22 * half
                    if m == 0:
                        pb = pz.next()
                        wpb = pb.take()
                        for kt in range(8):
                            t_mm = P.I("pe", "matmul", waits=([halo_tok[0], t_w] + wpb) if kt == 0 else (), sig=(kt == 7), out=pb.t[:, 0:2],
                                       lhsT=wb.t[:, kt, half, ci * 128:(ci + 1) * 128], rhs=h2T_halo[:, kt, :], start=(kt == 0), stop=(kt == 7))
                        carry_tok[cc] = P.I("dve", "tensor_scalar", waits=[t_mm, CONST], out=carry[:, cc, :], in0=pb.t[:, 0:2], scalar1=hflag[:, 0:1],
                                            scalar2=None, op0=ALU.mult)
                        pb.used(carry_tok[cc])
                    pb = pz.next()
                    wpb = pb.take()
                    for kt in range(8):
                        t_mm = P.I("pe", "matmul", waits=(h2T_toks + [t_w] + wpb) if kt == 0 else (), sig=(kt == 7), out=pb.t[:],
                                   lhsT=wb.t[:, kt, half, ci * 128:(ci + 1) * 128], rhs=h2T.t[:, kt, :], start=(kt == 0), stop=(kt == 7))
                    h2T_users.append(t_mm)
                    z = zb.next()
                    wz = z.take()
                    t_zc = P.I("act", "activation", waits=[carry_tok[cc]] + wz, out=z.t[:, 0:2], in_=carry[:, cc, :], func=AF.Copy)
                    t_z = P.I("act", "activation", waits=[t_mm] + wz, out=z.t[:, 2:514], in_=pb.t[:], func=AF.Copy)
                    v_ = cv.next()
                    t_c2 = P.I("act", "activation", waits=[t_mm, CONST] + v_.take(), out=v_.t[:], in_=pb.t[:], func=AF.Identity,
                               scale=convw[:, 2, cc:cc + 1], bias=convb[:, cc:cc + 1])
                    pb.used(t_z)
                    pb.used(t_c2)
                    carry_tok[cc] = P.I("pool", "tensor_copy", waits=[t_z, t_zc], out=carry[:, cc, :], in_=z.t[:, 512:514])
                    t_c1 = P.I("dve", "scalar_tensor_tensor", waits=[t_z, t_zc, t_c2], out=v_.t[:], in0=z.t[:, 1:513], scalar=convw[:, 1, cc:cc + 1],
                               in1=v_.t[:], op0=ALU.mult, op1=ALU.add)
                    t_c0 = P.I("dve", "scalar_tensor_tensor", waits=[t_c1], out=v_.t[:], in0=z.t[:, 0:512], scalar=convw[:, 0, cc:cc + 1],
                               in1=v_.t[:], op0=ALU.mult, op1=ALU.add)
                    z.used(t_c0)
                    z.used(carry_tok[cc])
                    ups.append((v_, t_c0))
                s_ = sg.next()
                t_si = P.I("act", "activation", waits=[ups[0][1]] + s_.take(), out=s_.t[:], in_=ups[0][0].t[:], func=AF.Silu)
                ups[0][0].used(t_si)
                t_a = P.I("pool", "tensor_tensor", waits=[t_si, ups[1][1]] + (aT_w if c == 0 else []), out=aT.t[:, c, :], in0=s_.t[:], in1=ups[1][0].t[:], op=ALU.mult)
                s_.used(t_a)
                ups[1][0].used(t_a)
                aT_toks.append(t_a)
            wb.used(t_mm)
        for t in h2T_users:
            h2T.used(t)
        aT_users = []
        for j in range(4):
            o = ob_.next()
            wo = o.take()
            for c2 in range(2):
                pb = pY.next()
                wpb = pb.take()
                for c in range(22):
                    t_mm = P.I("pe", "matmul", waits=(aT_toks + [W_DN] + wpb) if c == 0 else (), sig=(c == 21), out=pb.t[:],
                               lhsT=aT.t[:, c, j * 128:(j + 1) * 128], rhs=w_down_bf[:, c, c2 * 512:(c2 + 1) * 512], start=(c == 0), stop=(c == 21))
                aT_users.append(t_mm)
                t_o = P.I("dve", "tensor_tensor", waits=[t_mm] + x1_toks[j] + (wo if c2 == 0 else []), out=o.t[:, c2 * 512:(c2 + 1) * 512], in0=pb.t[:],
                          in1=x1.t[:, j, c2 * 512:(c2 + 1) * 512], op=ALU.add)
                pb.used(t_o)
            t_st = P.dma("sp", f"sty{(ob_.i - 1) % 2}", waits=[t_o], out=y[tok0 + j * 128: tok0 + (j + 1) * 128, :], in_=o.t[:])
            o.used(t_st)
            out_toks.append(t_st)
        x1.used(t_o)
        for t in aT_users:
            aT.used(t)

    c_order = [("halo", 0)] + [("own", m) for m in range(8)]
    for (kind_, m_) in c_order:
        phaseC_macro(kind_, m_)
    P.final_wait("sp", out_toks)

    with ExitStack() as es:
        sems = {k: es.enter_context(nc.semaphore(f"s_{k}")) for k in P.sem_keys()}
        block = es.enter_context(nc.Block())

        def replay(eng_name, E):
            for (waits, method, kw, inc) in P.q[eng_name]:
                for (k, v) in waits:
                    E.wait_ge(sems[k], v)
                if method is None:
                    continue
                ins = getattr(E, method)(**kw)
                if inc is not None:
                    ins.then_inc(sems[inc[0]], inc[1])

        @block.sync
        def _(E):
            replay("sp", E)

        @block.scalar
        def _(E):
            replay("act", E)

        @block.vector
        def _(E):
            replay("dve", E)

        @block.gpsimd
        def _(E):
            replay("pool", E)

        @block.tensor
        def _(E):
            replay("pe", E)
    esC.close()
    esW.close()
    es_all.close()
    return nc


DO_ATTN = True
_CACHE = {}


def _host_consts(half):
    c = {}
    ident = np.eye(128, dtype=np.float32).astype(ml_dtypes.bfloat16)
    kk = np.arange(128)[:, None]
    qq = np.arange(128)[None, :]
    cb = np.where(kk <= qq, 0.0, -BIG).astype(np.float32).astype(ml_dtypes.bfloat16)
    c["ident_bf"] = ident
    c["cbias_bf"] = cb
    hd = 8
    inv_freq = (np.float32(500000.0) ** (-np.arange(hd, dtype=np.float32) / np.float32(hd))).astype(np.float32)
    pos = np.concatenate([np.arange(4096), half * 4096 + np.arange(4096)]).astype(np.float32)
    ang = (pos[:, None] * inv_freq[None, :]).astype(np.float32)
    cos = np.cos(ang).astype(np.float32).reshape(64, 128, 8).transpose(1, 0, 2)
    sin = np.sin(ang).astype(np.float32).reshape(64, 128, 8).transpose(1, 0, 2)
    c["cos_t"] = np.ascontiguousarray(np.concatenate([cos, cos], axis=2))
    c["sin_t"] = np.ascontiguousarray(np.concatenate([-sin, sin], axis=2))
    rb = np.full((33, 32), -BIG, dtype=np.float32)
    if half == 1:
        rb[0, 0:15] = 0.0
    for t in range(32):
        blk = t // 2
        if half == 1:
            rb[1 + t, 0:16] = 0.0
        rb[1 + t, 16:16 + blk] = 0.0
    c["rbias"] = np.ascontiguousarray(np.broadcast_to(rb[None], (128, 33, 32)))
    ic = np.zeros((4, 16), dtype=np.float32)
    for g, w in enumerate((2, 4, 8, 16)):
        tg = half * 4096 + np.arange(16)
        ic[g] = 1.0 / np.minimum(tg + 1.0, float(w))
    c["invcnt"] = np.ascontiguousarray(np.broadcast_to(ic[None], (128, 4, 16)))
    c["hflag"] = np.full((128, 1), float(half), dtype=np.float32)
    return c


def kernel(x, norm_mix_g, w_in, b_gate, q_norm_g, k_norm_g, w_pool, pool_scale,
           w_branch_attn, w_branch_pool, w_out, norm_ffn_g, w_up, conv_w, conv_b, w_down):
    f = lambda a: np.ascontiguousarray(np.asarray(a, dtype=np.float32))
    x = f(x)
    shared = {
        "w_in": f(w_in[0]), "w_pool": f(w_pool[0]), "w_ba": f(w_branch_attn[0]), "w_bp": f(w_branch_pool[0]),
        "w_out": f(w_out[0]), "w_up": f(w_up[0]), "w_down": f(w_down[0]),
        "gmix_b": f(np.broadcast_to(np.asarray(norm_mix_g[0])[None, :], (128, D))),
        "gffn_b": f(np.broadcast_to(np.asarray(norm_ffn_g[0])[None, :], (128, D))),
        "gq_b": f(np.broadcast_to(np.tile(np.asarray(q_norm_g[0]), 8)[None, :], (128, 512))),
        "gk_b": f(np.broadcast_to(np.tile(np.asarray(k_norm_g[0]), 8)[None, :], (128, 512))),
        "bgate_t": f(np.asarray(b_gate[0]).reshape(16, 128).T),
        "gmix_t": f(np.asarray(norm_mix_g[0]).reshape(8, 128).T),
        "pscale_t": f(np.asarray(pool_scale[0]).reshape(4, 128).T),
        "convw_t": f(np.asarray(conv_w[0]).reshape(3, 44, 128).transpose(2, 0, 1)),
        "convb_t": f(np.asarray(conv_b[0]).reshape(44, 128).T),
    }
    if "nc" not in _CACHE:
        _CACHE["nc"] = build_program(DO_ATTN)
    nc = _CACHE["nc"]
    in_maps = []
    zeros = np.zeros((NTOK, D), dtype=np.float32)
    for c in range(8):
        b, half = c // 2, c % 2
        m = dict(shared)
        m["x_own"] = np.ascontiguousarray(x[b, half * NTOK:(half + 1) * NTOK])
        m["x_pre"] = np.ascontiguousarray(x[b, 0:NTOK]) if half == 1 else zeros
        m.update(_host_consts(half))
        in_maps.append(m)
    res = run_bass_kernel_spmd(nc, in_maps, core_ids=list(range(8)))
    out = np.empty((4, 2 * NTOK, D), dtype=np.float32)
    for c in range(8):
        b, half = c // 2, c % 2
        out[b, half * NTOK:(half + 1) * NTOK] = res.results[c]["y"]
    return out
```

```python
import numpy as np
import ml_dtypes
from contextlib import ExitStack
import concourse.bass as bass
import concourse.mybir as mybir
from concourse.bass_utils import run_bass_kernel_spmd

F32 = mybir.dt.float32
BF16 = mybir.dt.bfloat16
AF = mybir.ActivationFunctionType
ALU = mybir.AluOpType
AX = mybir.AxisListType

D = 1024
NTOK = 4096
NH = 8
HD = 64
DFF = 2816
EPS = 1e-6
BIG = 30000.0
NQT = NTOK + 128
ENGS = ["pe", "act", "dve", "pool", "sp"]


class Prog:
    def __init__(self):
        self.q = {e: [] for e in ENGS}
        self.cnt = {}
        self.seen = {e: {} for e in ENGS}
        self.pending = {e: [] for e in ENGS}

    def _filter(self, eng, waits):
        mx = {}
        for w in list(self.pending[eng]) + list(waits):
            if w is None:
                continue
            k, v = w
            if v > mx.get(k, 0):
                mx[k] = v
        out = []
        for k, v in mx.items():
            if self.seen[eng].get(k, 0) >= v:
                continue
            self.seen[eng][k] = v
            out.append((k, v))
        self.pending[eng] = []
        return out

    def I(self, eng, method, waits=(), sig=True, **kw):
        w = self._filter(eng, waits)
        inc = None
        tok = None
        if sig:
            self.cnt[eng] = self.cnt.get(eng, 0) + 1
            inc = (eng, 1)
            tok = (eng, self.cnt[eng])
        self.q[eng].append((w, method, kw, inc))
        return tok

    def dma(self, queue, semkey, waits=(), **kw):
        w = self._filter(queue, waits)
        self.cnt[semkey] = self.cnt.get(semkey, 0) + 16
        self.q[queue].append((w, "dma_start", kw, (semkey, 16)))
        return (semkey, self.cnt[semkey])

    def barrier(self):
        toks = [(k, v) for k, v in self.cnt.items()]
        for e in ENGS:
            self.pending[e] = list(toks)

    def final_wait(self, eng, toks):
        w = self._filter(eng, toks)
        self.q[eng].append((w, None, None, None))

    def sem_keys(self):
        return list(self.cnt.keys())


class Buf:
    def __init__(self, t):
        self.t = t
        self.free = []
        self.closed = True

    def acquire(self):
        assert self.closed, "buffer re-acquired while a previous user is still emitting"
        self.closed = False
        return self.take()

    def close(self):
        self.closed = True

    def take(self):
        f = self.free
        self.free = []
        return f

    def used(self, tok):
        if tok is not None:
            self.free.append(tok)


class Ring:
    def __init__(self, bufs):
        self.bufs = [Buf(b) for b in bufs]
        self.i = 0

    def next(self):
        b = self.bufs[self.i % len(self.bufs)]
        self.i += 1
        return b


def run_tasks(gens, width):
    active = []
    it = iter(gens)
    exhausted = False
    while True:
        if not exhausted and len(active) < width:
            g = next(it, None)
            if g is None:
                exhausted = True
            else:
                active.append(g)
        if not active and exhausted:
            break
        for g in list(active):
            try:
                next(g)
            except StopIteration:
                active.remove(g)


def build_program(do_attn=True):
    nc = bass.Bass("TRN2", target_bir_lowering=False)
    P = Prog()

    def din(name, shape, dt=F32):
        return nc.dram_tensor(name, list(shape), dt, kind="ExternalInput").ap()

    x_own = din("x_own", [NTOK, D])
    x_pre = din("x_pre", [NTOK, D])
    w_in = din("w_in", [D, 4096])
    w_pool = din("w_pool", [4, 128, 128])
    w_ba = din("w_ba", [512, D])
    w_bp = din("w_bp", [512, D])
    w_out = din("w_out", [D, D])
    w_up = din("w_up", [D, 2 * DFF])
    w_down = din("w_down", [DFF, D])
    gmix_b = din("gmix_b", [128, D])
    gffn_b = din("gffn_b", [128, D])
    gq_b = din("gq_b", [128, 512])
    gk_b = din("gk_b", [128, 512])
    bgate_t = din("bgate_t", [128, 16])
    pscale_t = din("pscale_t", [128, 4])
    convw_t = din("convw_t", [128, 3, 44])
    convb_t = din("convb_t", [128, 44])
    ident_d = din("ident_bf", [128, 128], BF16)
    cbias_d = din("cbias_bf", [128, 128], BF16)
    cos_d = din("cos_t", [128, 64, 16])
    sin_d = din("sin_t", [128, 64, 16])
    gmixt_d = din("gmix_t", [128, 8])
    rbias_d = din("rbias", [128, 33, 32])
    invcnt_d = din("invcnt", [128, 4, 16])
    hflag_d = din("hflag", [128, 1])
    y = nc.dram_tensor("y", [NTOK, D], F32, kind="ExternalOutput").ap()

    KT = nc.dram_tensor("KT_s", [NH, 96, 2 * NTOK], BF16).ap()
    VS = nc.dram_tensor("VS_s", [NH, 128, 64, 64], BF16).ap()
    QT = nc.dram_tensor("QT_s", [NH, 96, NQT], BF16).ap()
    G0 = nc.dram_tensor("G0_s", [8, 128, NQT], BF16).ap()
    GP = nc.dram_tensor("GP_s", [8, 128, NQT], BF16).ap()
    AT = nc.dram_tensor("AT_s", [4, 128, NQT], BF16).ap()

    es_all = ExitStack()
    esA = ExitStack()

    def sb(es, name, shape, dt):
        return es.enter_context(nc.sbuf_tensor("sb_" + name, list(shape), dt))

    def ps(es, name, shape, dt):
        return es.enter_context(nc.psum_tensor("ps_" + name, list(shape), dt))

    gffn = sb(es_all, "gffn", [128, D], F32)
    bgate = sb(es_all, "bgate", [128, 16], F32)
    pscale = sb(es_all, "pscale", [128, 4], F32)
    convw = sb(es_all, "convw", [128, 3, 44], F32)
    convb = sb(es_all, "convb", [128, 44], F32)
    ident = sb(es_all, "ident", [128, 128], BF16)
    cbias = sb(es_all, "cbias", [128, 128], BF16)
    hflag = sb(es_all, "hflag", [128, 1], F32)
    neghalf = sb(es_all, "neghalf", [128, 8], F32)
    ones_f = sb(es_all, "ones_f", [128, 64], F32)
    ssq_all = sb(es_all, "ssq_all", [128, 160], F32)
    rs_all = sb(es_all, "rs_all", [128, 160], F32)
    junk = sb(es_all, "junk", [128, D], F32)
    carry = sb(es_all, "carry", [128, 44, 2], F32)
    gmix = sb(esA, "gmix", [128, D], F32)
    gq = sb(esA, "gq", [128, 512], F32)
    gk = sb(esA, "gk", [128, 512], F32)
    cos_t = sb(esA, "cos_t", [128, 64, 16], F32)
    sin_t = sb(esA, "sin_t", [128, 64, 16], F32)
    gmix_t = sb(esA, "gmix_t", [128, 8], F32)
    rbias = sb(esA, "rbias", [128, 33, 32], F32)
    invcnt = sb(esA, "invcnt", [128, 4, 16], F32)

    c_toks = []
    for dst, src in [(gmix, gmix_b), (gffn, gffn_b), (gq, gq_b), (gk, gk_b), (bgate, bgate_t),
                     (pscale, pscale_t), (convw, convw_t), (convb, convb_t), (ident, ident_d),
                     (cbias, cbias_d), (cos_t, cos_d), (sin_t, sin_d), (rbias, rbias_d),
                     (invcnt, invcnt_d), (hflag, hflag_d), (gmix_t, gmixt_d)]:
        c_toks.append(P.dma("sp", "const", out=dst[:], in_=src))
    CONST = c_toks[-1]
    t_nh = P.I("pool", "memset", ap=neghalf[:], constant=-0.5)
    t_ones = P.I("pool", "memset", ap=ones_f[:], constant=1.0)
    norm_idx = [0]
    junk_tok = [None]

    def rmsnorm_tile(x_ap, g_t, out_bf, waits):
        i = norm_idx[0]
        norm_idx[0] += 1
        t1 = P.I("act", "activation", waits=list(waits) + [junk_tok[0]], out=junk[:], in_=x_ap, func=AF.Square,
                 accum_out=ssq_all[:, i:i + 1])
        junk_tok[0] = t1
        t2 = P.I("pool", "tensor_scalar", waits=[t1, t_nh], out=rs_all[:, i:i + 1], in0=ssq_all[:, i:i + 1],
                 scalar1=1.0 / D, scalar2=EPS, op0=ALU.mult, op1=ALU.add)
        t3 = P.I("pool", "tensor_tensor", waits=[t2], out=rs_all[:, i:i + 1], in0=rs_all[:, i:i + 1],
                 in1=neghalf[:, 0:1], op=ALU.pow)
        return t3, i

    spill_toks = []
    w_in_v = w_in.rearrange("(kt p) n -> p kt n", p=128)
    xr = Ring([sb(esA, f"xa{i}", [128, D], F32) for i in range(6)])
    hbr = Ring([sb(esA, f"hb{i}", [128, D], BF16) for i in range(3)])

    def x_chain(src, row0, trx_ring, get_dst):
        xb = xr.next()
        w = xb.acquire()
        t_ld = P.dma("sp", f"xa{(xr.i - 1) % len(xr.bufs)}", waits=w, out=xb.t[:], in_=src[row0:row0 + 128, :])
        yield
        i = norm_idx[0]
        norm_idx[0] += 1
        t1 = P.I("act", "activation", waits=[t_ld, junk_tok[0]], out=junk[:], in_=xb.t[:], func=AF.Square, accum_out=ssq_all[:, i:i + 1])
        junk_tok[0] = t1
        yield
        t2 = P.I("pool", "tensor_scalar", waits=[t1, t_nh], out=rs_all[:, i:i + 1], in0=ssq_all[:, i:i + 1],
                 scalar1=1.0 / D, scalar2=EPS, op0=ALU.mult, op1=ALU.add)
        t_rs = P.I("pool", "tensor_tensor", waits=[t2], out=rs_all[:, i:i + 1], in0=rs_all[:, i:i + 1],
                   in1=neghalf[:, 0:1], op=ALU.pow)
        hb = hbr.next()
        w = hb.acquire()
        t_h = P.I("pool", "tensor_scalar", waits=[t_rs, t_ld] + w, out=hb.t[:], in0=xb.t[:],
                  scalar1=rs_all[:, i:i + 1], scalar2=1.0, op0=ALU.mult, op1=ALU.mult)
        xb.used(t_h)
        xb.close()
        yield
        tr = trx_ring.next()
        wtr = tr.acquire()
        for kt in range(8):
            t_tr = P.I("pe", "transpose", waits=([t_h, CONST] + wtr) if kt == 0 else (), sig=(kt == 7),
                       out=tr.t[:, kt, :], in_=hb.t[:, kt * 128:(kt + 1) * 128], identity=ident[:])
        hb.used(t_tr)
        hb.close()
        dst_ap, dst_waits = get_dst()
        t_cp = P.I("act", "activation", waits=[t_tr] + dst_waits, out=dst_ap, in_=tr.t[:], func=AF.Copy)
        tr.used(t_cp)
        tr.close()
        return t_cp

    if do_attn:
        esA1 = ExitStack()
        w_qkv = sb(esA1, "w_qkv", [128, 8, 1536], BF16)
        hTt = Ring([sb(esA1, f"hTt{i}", [128, 8, 128], BF16) for i in range(3)])
        Kaug = Ring([sb(esA1, f"Kaug{i}", [128, 8, 96], BF16) for i in range(4)])
        Qaug = Ring([sb(esA1, f"Qaug{i}", [128, 8, 96], BF16) for i in range(7)])
        KTst = Ring([sb(esA1, f"KTst{i}", [96, 8, 512], BF16) for i in range(2)])
        QTst = Ring([sb(esA1, f"QTst{i}", [96, 8, 512], BF16) for i in range(2)])
        Vst = Ring([sb(esA1, f"Vst{i}", [128, 8, 4, 64], BF16) for i in range(2)])
        tq = Ring([sb(esA1, f"tq{i}", [128, 8, 64], F32) for i in range(10)])
        sqb = Ring([sb(esA1, f"sqb{i}", [128, 8, 64], F32) for i in range(6)])
        ssqh = sb(esA1, "ssqh", [128, 100, 8], F32)
        rp = Ring([sb(esA1, f"rp{i}", [128, 2, 8, 16], F32) for i in range(4)])
        kmT = sb(esA1, "kmT", [64, 8, 32], BF16)
        kms = sb(esA1, "kms", [64, 8], F32)
        sbr = Ring([sb(esA1, f"sbr{i}", [128, 8, 32], F32) for i in range(3)])
        m8 = Ring([sb(esA1, f"m8_{i}", [128, 8, 8], F32) for i in range(2)])
        thr = Ring([sb(esA1, f"thr{i}", [128, 8], F32) for i in range(2)])
        ger = Ring([sb(esA1, f"ge{i}", [128, 8, 32], F32) for i in range(2)])
        trx1 = Ring([ps(esA1, f"trx1_{i}", [128, 8, 128], BF16) for i in range(2)])
        pq = Ring([ps(esA1, f"pq{i}", [128, 512], F32) for i in range(3)])
        trk = Ring([ps(esA1, f"trk{i}", [128, 8, 128], BF16) for i in range(2)])
        prt = Ring([ps(esA1, "prt", [128, 8, 32], F32)])

        WK = None
        for (c0, c1) in [(512, 1536), (0, 512)]:
            for kt in range(8):
                tk = P.dma("pool", f"w{c0}", out=w_qkv[:, kt, c0:c1], in_=w_in_v[:, kt, c0:c1])
            if c0 == 512:
                WKV = tk
            else:
                WQ = tk
        for kt in range(8):
            t_f = P.I("dve", "tensor_scalar", waits=[WKV, WQ, CONST], out=w_qkv[:, kt, :], in0=w_qkv[:, kt, :],
                      scalar1=gmix_t[:, kt:kt + 1], scalar2=None, op0=ALU.mult)
        WKV = t_f
        WQ = t_f
        t_km0 = P.I("pool", "memset", ap=kmT[:], constant=0.0)
        kmT_tok = [t_km0]
        qk_idx = [0]
        kst_state, vst_state, qst_state = {}, {}, {}

        def qk_evac(pb, t_mm, g_t):
            pqv = pb.t[:].rearrange("p (h d) -> p h d", h=8)
            sq = sqb.next()
            t_sq = P.I("act", "activation", waits=[t_mm] + sq.acquire(), out=sq.t[:], in_=pqv, func=AF.Square)
            t = tq.next()
            t_g = P.I("dve", "tensor_tensor", waits=[t_mm, t_sq, CONST] + t.acquire(), out=t.t[:], in0=pqv,
                      in1=g_t[:].rearrange("p (h d) -> p h d", h=8), op=ALU.mult)
            pb.used(t_g)
            pb.used(t_sq)
            pb.close()
            return {"sq": sq, "t_sq": t_sq, "t": t, "t_g": t_g}

        def qk_reduce(st):
            i = qk_idx[0]
            qk_idx[0] += 1
            st["i"] = i
            t_red = P.I("dve", "tensor_reduce", waits=[st["t_sq"]], out=ssqh[:, i, :], in_=st["sq"].t[:], axis=AX.X, op=ALU.add)
            st["sq"].used(t_red)
            st["sq"].close()
            st["t_red"] = t_red

        def qk_pow(st):
            i = st["i"]
            t_r1 = P.I("pool", "tensor_scalar", waits=[st["t_red"], t_nh], out=ssqh[:, i, :], in0=ssqh[:, i, :],
                       scalar1=1.0 / HD, scalar2=EPS, op0=ALU.mult, op1=ALU.add)
            st["t_r2"] = P.I("pool", "tensor_tensor", waits=[t_r1], out=ssqh[:, i, :], in0=ssqh[:, i, :],
                             in1=neghalf[:], op=ALU.pow)

        def qk_rope(st, tile_idx, aug):
            t, t_g, t_r2, i = st["t"], st["t_g"], st["t_r2"], st["i"]
            r = rp.next()
            ccb = cos_t[:, tile_idx, :].unsqueeze(1).to_broadcast([128, 8, 16])
            ssb = sin_t[:, tile_idx, :].unsqueeze(1)
            x16 = t.t[:, :, 0:16]
            x1 = t.t[:, :, 0:8]
            x2 = t.t[:, :, 8:16]
            wr = r.acquire()
            ta = P.I("dve", "tensor_tensor", waits=[t_g] + wr, out=r.t[:, 0], in0=x16, in1=ccb, op=ALU.mult)
            tb = P.I("dve", "tensor_tensor", waits=[t_g], out=r.t[:, 1, :, 0:8], in0=x2,
                     in1=ssb[:, :, 0:8].to_broadcast([128, 8, 8]), op=ALU.mult)
            tc = P.I("dve", "tensor_tensor", waits=[t_g], out=r.t[:, 1, :, 8:16], in0=x1,
                     in1=ssb[:, :, 8:16].to_broadcast([128, 8, 8]), op=ALU.mult)
            te = P.I("dve", "tensor_tensor", waits=[ta, tb, tc], out=x16, in0=r.t[:, 0], in1=r.t[:, 1], op=ALU.add)
            tf = te
            r.used(te)
            r.close()
            t_fin = P.I("dve", "tensor_tensor", waits=[te, tf, t_r2] + aug.acquire(), out=aug.t[:, :, 0:64], in0=t.t[:],
                        in1=ssqh[:, i, :].unsqueeze(2).to_broadcast([128, 8, 64]), op=ALU.mult)
            t.used(t_fin)
            t.close()
            return t_fin

        def stage_get(state, key, ring):
            if key not in state:
                b = ring.next()
                state[key] = {"buf": b, "w": b.acquire(), "toks": [], "n": 0, "slot": (ring.i - 1) % len(ring.bufs)}
            return state[key]

        def taskA1(kind, idx):
            if kind == "pre":
                src, row0, ktile, groups, mkey, j = x_pre, idx * 128, idx, ["k", "v"], ("pre", idx // 4), idx % 4
            elif kind == "halo":
                src, row0, ktile, groups, mkey, j = x_pre, 3968, 31, ["q"], ("halo", 0), 0
            else:
                src, row0, ktile, groups, mkey, j = x_own, (idx - 32) * 128, idx, ["q", "k", "v"], ("own", (idx - 32) // 4), idx % 4
            nt_macro = 1 if kind == "halo" else 4
            qtile = 0 if kind == "halo" else 1 + (idx - 32)
            slot = ktile // 2
            hb_box = []

            def get_dst():
                hb_ = hTt.next()
                hb_box.append(hb_)
                return hb_.t[:], hb_.acquire()

            t_hT = yield from x_chain(src, row0, trx1, get_dst)
            hb_ = hb_box[0]
            yield
            col0 = {"q": 0, "k": 512, "v": 1024}
            st = {}
            for gname in groups:
                pb = pq.next()
                wpb = pb.acquire()
                for kt in range(8):
                    t_mm = P.I("pe", "matmul", waits=([t_hT, WKV if gname != "q" else WQ] + wpb) if kt == 0 else (), sig=(kt == 7),
                               out=pb.t[:], lhsT=hb_.t[:, kt, :], rhs=w_qkv[:, kt, col0[gname]:col0[gname] + 512],
                               start=(kt == 0), stop=(kt == 7))
                hb_.used(t_mm)
                if gname == "v":
                    vs = stage_get(vst_state, mkey, Vst)
                    t_v = P.I("act", "activation", waits=[t_mm] + vs["w"], out=vs["buf"].t[:, :, j, :],
                              in_=pb.t[:].rearrange("p (h d) -> p h d", h=8), func=AF.Copy)
                    vs["w"] = []
                    pb.used(t_v)
                    pb.close()
                    vs["toks"].append(t_v)
                    vs["n"] += 1
                    if vs["n"] == 4:
                        t_sv = P.dma("sp", f"vst{vs['slot']}", waits=vs["toks"], out=VS[:, :, ktile - 3:ktile + 1, :].rearrange("h p t c -> p h t c"),
                                     in_=vs["buf"].t[:])
                        vs["buf"].used(t_sv)
                        vs["buf"].close()
                        spill_toks.append(t_sv)
                else:
                    st[gname] = qk_evac(pb, t_mm, gk if gname == "k" else gq)
            hb_.close()
            yield
            for gname in st:
                qk_reduce(st[gname])
            yield
            for gname in st:
                qk_pow(st[gname])
            yield
            if "k" in groups:
                ka = Kaug.next()
                t_k = qk_rope(st["k"], ktile, ka)
            if "q" in groups:
                qa = Qaug.next()
                t_q = qk_rope(st["q"], ktile, qa)
            yield
            if "k" in groups:
                t_z = P.I("pool", "memset", waits=[t_k], ap=ka.t[:, :, 64:96], constant=0.0)
                t_o = P.I("pool", "memset", waits=[t_z], ap=ka.t[:, :, 64 + slot:65 + slot], constant=1.0)
            if "q" in groups:
                tkb = trk.next()
                wtk = tkb.acquire()
                for h in range(8):
                    t_tq = P.I("pe", "transpose", waits=([t_q] + wtk) if h == 0 else (), sig=(h == 7),
                               out=tkb.t[0:64, h, :], in_=qa.t[:, h, 0:64], identity=ident[:])
                qs = stage_get(qst_state, mkey, QTst)
                t_qc = P.I("act", "activation", waits=[t_tq] + qs["w"], out=qs["buf"].t[0:64, :, j * 128:(j + 1) * 128],
                           in_=tkb.t[0:64], func=AF.Copy)
                qs["w"] = []
                tkb.used(t_qc)
                tkb.close()
            yield
            if "k" in groups:
                tkb = trk.next()
                wtk = tkb.acquire()
                for h in range(8):
                    t_tk = P.I("pe", "transpose", waits=([t_k, t_o] + wtk) if h == 0 else (), sig=(h == 7),
                               out=tkb.t[0:96, h, :], in_=ka.t[:, h, :], identity=ident[:])
                ka.used(t_tk)
                ka.close()
                ks = stage_get(kst_state, mkey, KTst)
                t_kc = P.I("act", "activation", waits=[t_tk] + ks["w"], out=ks["buf"].t[:, :, j * 128:(j + 1) * 128],
                           in_=tkb.t[0:96], func=AF.Copy)
                ks["w"] = []
                tkb.used(t_kc)
                tkb.close()
                ks["toks"].append(t_kc)
                ks["n"] += 1
                k_last = (ks["n"] == 4)
            if "q" in groups:
                pr = prt.next()
                wpr = pr.acquire()
                for h in range(8):
                    t_rt = P.I("pe", "matmul", waits=([t_qc, kmT_tok[0]] + wpr) if h == 0 else (), sig=(h == 7),
                               out=pr.t[:, h, :], lhsT=qs["buf"].t[0:64, h, j * 128:(j + 1) * 128],
                               rhs=kmT[:, h, :], start=True, stop=True)
                sbb = sbr.next()
                t_sb = P.I("dve", "tensor_tensor", waits=[t_rt, CONST] + sbb.acquire(), out=sbb.t[:], in0=pr.t[:],
                           in1=rbias[:, qtile, :].unsqueeze(1).to_broadcast([128, 8, 32]), op=ALU.add)
                pr.used(t_sb)
                pr.close()
            yield
            if "k" in groups:
                if ktile % 2 == 1:
                    jb = j - 1
                    t_ks = P.I("dve", "tensor_reduce", waits=[ks["toks"][-1], ks["toks"][-2], kmT_tok[0]], out=kms[:],
                               in_=ks["buf"].t[0:64, :, jb * 128:(jb + 2) * 128], axis=AX.X, op=ALU.add)
                    t_km = P.I("dve", "tensor_scalar", waits=[t_ks], out=kmT[:, :, slot], in0=kms[:],
                               scalar1=1.0 / 256.0, scalar2=None, op0=ALU.mult)
                    kmT_tok[0] = t_km
                    ks["buf"].used(t_km)
                if k_last:
                    kbase = (ktile - 3) * 128
                    t_st = P.dma("sp", f"kst{ks['slot']}", waits=ks["toks"], out=KT[:, :, kbase:kbase + 512].rearrange("h r t -> r h t"),
                                 in_=ks["buf"].t[:])
                    ks["buf"].used(t_st)
                    ks["buf"].close()
                    spill_toks.append(t_st)
            if "q" not in groups:
                return
            mb = m8.next()
            wm = mb.acquire()
            for h in range(8):
                t_m8 = P.I("dve", "max", waits=([t_sb] + wm) if h == 0 else (), sig=(h == 7),
                           out=mb.t[:, h, :], in_=sbb.t[:, h, :])
            th = thr.next()
            t_th = P.I("dve", "tensor_scalar", waits=[t_m8] + th.acquire(), out=th.t[:], in0=mb.t[:, :, 2],
                       scalar1=-1.0e4, scalar2=None, op0=ALU.max)
            mb.used(t_th)
            mb.close()
            gb = ger.next()
            t_ge = P.I("dve", "tensor_tensor", waits=[t_th, t_sb] + gb.acquire(), out=gb.t[:], in0=sbb.t[:],
                       in1=th.t[:].unsqueeze(2).to_broadcast([128, 8, 32]), op=ALU.is_ge)
            sbb.used(t_ge)
            sbb.close()
            th.used(t_ge)
            th.close()
            t_mk = P.I("pool", "tensor_scalar", waits=[t_ge, t_q, t_tq], out=qa.t[:, :, 64:96], in0=gb.t[:],
                       scalar1=-1.0, scalar2=BIG, op0=ALU.add, op1=ALU.mult)
            gb.used(t_mk)
            gb.close()
            t_mk = P.I("pool", "memset", waits=[t_mk], ap=qa.t[:, :, 64 + slot:65 + slot], constant=0.0)
            yield
            tkb = trk.next()
            wtk = tkb.acquire()
            for h in range(8):
                t_tm = P.I("pe", "transpose", waits=([t_mk] + wtk) if h == 0 else (), sig=(h == 7),
                           out=tkb.t[0:32, h, :], in_=qa.t[:, h, 64:96], identity=ident[:])
            qa.used(t_tm)
            qa.close()
            t_mc = P.I("dve", "tensor_copy", waits=[t_tm, t_rt], out=qs["buf"].t[64:96, :, j * 128:(j + 1) * 128],
                       in_=tkb.t[0:32])
            tkb.used(t_mc)
            tkb.close()
            qs["toks"].append(t_mc)
            qs["n"] += 1
            if qs["n"] == nt_macro:
                N = nt_macro * 128
                qoff = 0 if kind == "halo" else 128 + (idx - 32 - 3) * 128
                t_sq = P.dma("sp", f"qst{qs['slot']}", waits=qs["toks"], out=QT[:, :, qoff:qoff + N].rearrange("h r t -> r h t"),
                             in_=qs["buf"].t[:, :, 0:N])
                qs["buf"].used(t_sq)
                qs["buf"].close()
                spill_toks.append(t_sq)

        gensA1 = [taskA1("pre", i) for i in range(32)] + [taskA1("halo", 0)] + [taskA1("own", 32 + i) for i in range(32)]
        run_tasks(gensA1, 16)
        P.barrier()
        esA1.close()

    esA2 = ExitStack()
    w_pg = sb(esA2, "w_pg", [128, 8, 2560], BF16)
    w_bp_bf = sb(esA2, "w_bp_bf", [128, 4, D], BF16)
    w_pool_bf = sb(esA2, "w_pool_bf", [128, 4, 128], BF16)
    hTm = Ring([sb(esA2, f"hTm{i}", [128, 8, 512], BF16) for i in range(2)])
    G0st = Ring([sb(esA2, f"G0st{i}", [128, 8, 512], BF16) for i in range(2)])
    GPst = Ring([sb(esA2, f"GPst{i}", [128, 8, 512], BF16) for i in range(2)])
    ubs = [[sb(esA2, f"ub{g}_{i}", [128, 16 + 512], F32) for i in range(3)] for g in range(4)]
    ucarry = sb(esA2, "ucarry", [128, 4, 16], F32)
    pooled = Ring([sb(esA2, f"pooled{i}", [128, 512], BF16) for i in range(4)])
    pmr = Ring([sb(esA2, f"pm{i}", [128, 4, 512], BF16) for i in range(2)])
    g1r = Ring([sb(esA2, f"g1_{i}", [128, 512], F32) for i in range(5)])
    tmp16 = sb(esA2, "tmp16", [128, 4, 16], F32)
    trx2 = Ring([ps(esA2, f"trx2_{i}", [128, 8, 128], BF16) for i in range(2)])
    pf = Ring([ps(esA2, f"pf{i}", [128, 512], F32) for i in range(6)])

    for kt in range(8):
        WP = P.dma("pool", "wpg0", out=w_pg[:, kt, 0:512], in_=w_in_v[:, kt, 1536:2048])
    for kt in range(8):
        WG = P.dma("pool", "wpg1", out=w_pg[:, kt, 512:2560], in_=w_in_v[:, kt, 2048:4096])
    for kt in range(8):
        t_f = P.I("dve", "tensor_scalar", waits=[WP, CONST], out=w_pg[:, kt, 0:512], in0=w_pg[:, kt, 0:512],
                  scalar1=gmix_t[:, kt:kt + 1], scalar2=None, op0=ALU.mult)
    WP = t_f
    for kt in range(8):
        t_f = P.I("dve", "tensor_scalar", waits=[WG, CONST], out=w_pg[:, kt, 512:2560], in0=w_pg[:, kt, 512:2560],
                  scalar1=gmix_t[:, kt:kt + 1], scalar2=None, op0=ALU.mult)
    WG = t_f
    W_BP = P.dma("pool", "wbp", out=w_bp_bf[:], in_=w_bp.rearrange("(g p) n -> p g n", p=128))
    W_POOL = P.dma("pool", "wpool", out=w_pool_bf[:], in_=w_pool.rearrange("g c d -> c g d"))
    t_uc0 = P.I("pool", "memset", ap=ucarry[:], constant=0.0)
    ucarry_tok = [t_uc0] * 4
    ub_free = [[] for _ in range(4)]
    for g_ in range(4):
        for i_ in (1, 2):
            ub_free[g_].append(P.I("pool", "memset", ap=ubs[g_][i_][:], constant=0.0))
    macros = [("halo", 0)] + [("own", m) for m in range(8)]
    mstate = {}

    def mget(mi):
        if mi not in mstate:
            hb_ = hTm.next()
            mstate[mi] = {"hT": hb_, "hT_w": hb_.acquire(), "hT_toks": [], "hT_users": 0,
                          "pm": None, "pm_toks": {}, "pm_users": 0, "g0": None, "gp": None, "g0_toks": [], "gp_toks": []}
        return mstate[mi]

    def hT_user_done(ms, tok, total):
        ms["hT"].used(tok)
        ms["hT_users"] += 1
        if ms["hT_users"] == total:
            ms["hT"].close()

    def taskX(mi, j):
        kind, m = macros[mi]
        src = x_pre if kind == "halo" else x_own
        row0 = 3968 if kind == "halo" else m * 512 + j * 128

        def get_dst():
            ms = mget(mi)
            w = ms["hT_w"]
            ms["hT_w"] = []
            return ms["hT"].t[:, :, j * 128:(j + 1) * 128], w

        t = yield from x_chain(src, row0, trx2, get_dst)
        mget(mi)["hT_toks"].append(t)

    def taskPool(mi, g):
        kind, m = macros[mi]
        N = 128 if kind == "halo" else 512
        ms = mget(mi)
        while len(ms["hT_toks"]) < N // 128:
            yield
        ub = ubs[g]
        w = 2 << g
        pb = pf.next()
        wpb = pb.acquire()
        for kt in range(8):
            t_mm = P.I("pe", "matmul", waits=(ms["hT_toks"] + [WP] + wpb) if kt == 0 else (), sig=(kt == 7),
                       out=pb.t[:, 0:N], lhsT=w_pg[:, kt, g * 128:(g + 1) * 128],
                       rhs=ms["hT"].t[:, kt, 0:N], start=(kt == 0), stop=(kt == 7))
        hT_user_done(ms, t_mm, 20)
        wub = ub_free[g]
        t_c0 = P.I("pool", "tensor_copy", waits=wub + [ucarry_tok[g]], out=ub[0][:, 0:16], in_=ucarry[:, g, :])
        t_u = P.I("act", "activation", waits=[t_mm] + wub, out=ub[0][:, 16:16 + N], in_=pb.t[:, 0:N], func=AF.Copy)
        pb.used(t_u)
        pb.close()
        if kind == "halo":
            ucarry_tok[g] = P.I("pool", "tensor_scalar", waits=[t_u, t_c0, CONST], out=ucarry[:, g, :], in0=ub[0][:, N:N + 16],
                                scalar1=hflag[:, 0:1], scalar2=None, op0=ALU.mult)
        else:
            ucarry_tok[g] = P.I("pool", "tensor_copy", waits=[t_u, t_c0], out=ucarry[:, g, :], in_=ub[0][:, N:N + 16])
        L = 16 + N
        src_i, last = 0, [t_u, t_c0]
        sh = 1
        for step in range(g + 1):
            dst_i = 1 if src_i != 1 else 2
            t_a = P.I("pool", "tensor_tensor", waits=last + wub, out=ub[dst_i][:, sh:L], in0=ub[src_i][:, sh:L],
                      in1=ub[src_i][:, 0:L - sh], op=ALU.add)
            last = [t_a]
            src_i = dst_i
            sh *= 2
        po = pooled.next()
        wpo = po.acquire()
        t_p = P.I("dve", "scalar_tensor_tensor", waits=last + [t_u, ucarry_tok[g]] + wpo, out=po.t[:, 0:N], in0=ub[src_i][:, 16:16 + N],
                  scalar=1.0 / w, in1=ub[0][:, 16:16 + N], op0=ALU.mult, op1=ALU.subtract)
        if kind == "own" and m == 0:
            t_p1 = P.I("dve", "tensor_tensor", waits=[t_p, CONST], out=tmp16[:, g, :], in0=ub[src_i][:, 16:32],
                       in1=invcnt[:, g, :], op=ALU.mult)
            t_p = P.I("dve", "tensor_tensor", waits=[t_p1], out=po.t[:, 0:16], in0=tmp16[:, g, :], in1=ub[0][:, 16:32],
                      op=ALU.subtract)
        ub_free[g] = [t_p]
        yield
        yield
        yield
        if ms["pm"] is None:
            ms["pm"] = pmr.next()
            ms["pm_w"] = ms["pm"].acquire()
        pb2 = pf.next()
        t_pm = P.I("pe", "matmul", waits=[t_p, W_POOL] + pb2.acquire(), out=pb2.t[:, 0:N], lhsT=w_pool_bf[:, g, :],
                   rhs=po.t[:, 0:N], start=True, stop=True)
        po.used(t_pm)
        po.close()
        t_ps = P.I("act", "activation", waits=[t_pm, CONST] + ms["pm_w"], out=ms["pm"].t[:, g, 0:N], in_=pb2.t[:, 0:N],
                   func=AF.Identity, scale=pscale[:, g:g + 1])
        ms["pm_w"] = []
        pb2.used(t_ps)
        pb2.close()
        ms["pm_toks"][g] = t_ps

    def taskGate(mi, f):
        kind, m = macros[mi]
        N = 128 if kind == "halo" else 512
        qoff = 0 if kind == "halo" else 128 + m * 512
        ms = mget(mi)
        while len(ms["hT_toks"]) < N // 128:
            yield
        if ms["g0"] is None:
            ms["g0"] = G0st.next()
            ms["gslot"] = (G0st.i - 1) % len(G0st.bufs)
            ms["g0_w"] = ms["g0"].acquire()
            ms["gp"] = GPst.next()
            ms["gp_w"] = ms["gp"].acquire()
        pb = pf.next()
        wpb = pb.acquire()
        for kt in range(8):
            t_mm = P.I("pe", "matmul", waits=(ms["hT_toks"] + [WG] + wpb) if kt == 0 else (), sig=(kt == 7),
                       out=pb.t[:, 0:N], lhsT=w_pg[:, kt, 512 + f * 128:512 + (f + 1) * 128],
                       rhs=ms["hT"].t[:, kt, 0:N], start=(kt == 0), stop=(kt == 7))
        hT_user_done(ms, t_mm, 20)
        t_s0 = P.I("act", "activation", waits=[t_mm, CONST] + ms["g0_w"], out=ms["g0"].t[:, f, 0:N], in_=pb.t[:, 0:N],
                   func=AF.Sigmoid, bias=bgate[:, f:f + 1])
        ms["g0_w"] = []
        pb.used(t_s0)
        pb.close()
        ms["g0_toks"].append(t_s0)
        pb = pf.next()
        wpb = pb.acquire()
        for kt in range(8):
            t_mm = P.I("pe", "matmul", waits=(ms["hT_toks"] + wpb) if kt == 0 else (), sig=(kt == 7),
                       out=pb.t[:, 0:N], lhsT=w_pg[:, kt, 1536 + f * 128:1536 + (f + 1) * 128],
                       rhs=ms["hT"].t[:, kt, 0:N], start=(kt == 0), stop=(kt == 7))
        hT_user_done(ms, t_mm, 20)
        g1 = g1r.next()
        t_s1 = P.I("act", "activation", waits=[t_mm] + g1.acquire(), out=g1.t[:, 0:N], in_=pb.t[:, 0:N],
                   func=AF.Sigmoid, bias=bgate[:, 8 + f:9 + f])
        pb.used(t_s1)
        pb.close()
        yield
        yield
        while len(ms["pm_toks"]) < 4:
            yield
        pb = pf.next()
        wpb = pb.acquire()
        for g in range(4):
            t_mm = P.I("pe", "matmul", waits=(list(ms["pm_toks"].values()) + [W_BP] + wpb) if g == 0 else (), sig=(g == 3),
                       out=pb.t[:, 0:N], lhsT=w_bp_bf[:, g, f * 128:(f + 1) * 128], rhs=ms["pm"].t[:, g, 0:N],
                       start=(g == 0), stop=(g == 3))
        ms["pm"].used(t_mm)
        ms["pm_users"] += 1
        if ms["pm_users"] == 8:
            ms["pm"].close()
        t_gp = P.I("dve", "tensor_tensor", waits=[t_mm, t_s1] + ms["gp_w"], out=ms["gp"].t[:, f, 0:N], in0=pb.t[:, 0:N],
                   in1=g1.t[:, 0:N], op=ALU.mult)
        ms["gp_w"] = []
        pb.used(t_gp)
        pb.close()
        g1.used(t_gp)
        g1.close()
        ms["gp_toks"].append(t_gp)
        if len(ms["gp_toks"]) == 8:
            t_s = P.dma("sp", f"g0st{ms['gslot']}", waits=ms["g0_toks"], out=G0[:, :, qoff:qoff + N].rearrange("f p t -> p f t"), in_=ms["g0"].t[:, :, 0:N])
            ms["g0"].used(t_s)
            ms["g0"].close()
            spill_toks.append(t_s)
            t_s = P.dma("sp", f"gpst{ms['gslot']}", waits=ms["gp_toks"], out=GP[:, :, qoff:qoff + N].rearrange("f p t -> p f t"), in_=ms["gp"].t[:, :, 0:N])
            ms["gp"].used(t_s)
            ms["gp"].close()
            spill_toks.append(t_s)

    gensA2 = []
    nx = lambda mi: (1 if macros[mi][0] == "halo" else 4)
    gensA2 += [taskX(0, 0)]
    for mi in range(len(macros)):
        fm = [taskPool(mi, g) for g in range(4)] + [taskGate(mi, f) for f in range(8)]
        nxt = [taskX(mi + 1, j) for j in range(nx(mi + 1))] if mi + 1 < len(macros) else []
        seq = []
        k = 0
        for i, t in enumerate(fm):
            seq.append(t)
            if i % 3 == 2 and k < len(nxt):
                seq.append(nxt[k])
                k += 1
        seq += nxt[k:]
        gensA2 += seq
    run_tasks(gensA2, 12)
    P.barrier()
    esA2.close()
    esA.close()

    esW = ExitStack()
    w_out_bf = sb(esW, "w_out_bf", [128, 8, D], BF16)
    w_down_bf = sb(esW, "w_down_bf", [128, 22, D], BF16)
    h2T_halo = sb(esW, "h2T_halo", [128, 8, 2], BF16)
    for kt in range(8):
        W_OUT = P.dma("pool", "wc1", out=w_out_bf[:, kt, :], in_=w_out[kt * 128:(kt + 1) * 128, :])
    for c in range(22):
        W_DN = P.dma("pool", "wc3", out=w_down_bf[:, c, :], in_=w_down[c * 128:(c + 1) * 128, :])
    esB = ExitStack()
    attnT = sb(esB, "attnT", [128, 4, NQT], BF16)
    if not do_attn:
        t_at = P.I("pool", "memset", ap=attnT[:], constant=0.0)
        at_toks = [t_at]
    if do_attn:
        at_toks = []
        KTh = Ring([sb(esB, f"KTh{i}", [96, 2 * NTOK], BF16) for i in range(2)])
        QTh = Ring([sb(esB, f"QTh{i}", [96, NQT], BF16) for i in range(2)])
        Vh = Ring([sb(esB, f"Vh{i}", [128, 64, 128], BF16) for i in range(2)])
        Pb = Ring([sb(esB, f"Pb{i}", [128, 3, 512], BF16) for i in range(4)])
        rden = Ring([sb(esB, f"rden{i}", [128, 512], F32) for i in range(2)])
        Sps = Ring([ps(esB, f"Sps{i}", [128, 3, 512], F32) for i in range(2)])
        Ops = Ring([ps(esB, f"Ops{i}", [128, 512], F32) for i in range(2)])
        for r in Vh.bufs:
            r.used(P.I("pool", "memset", ap=r.t[:, :, 64:128], constant=1.0))

        def group_items(gi):
            items = []
            if gi == 0:
                for s in range(15):
                    items += [(2 * s, 0, 128, False, False), (2 * s + 1, 0, 128, False, False)]
                items += [(30, 0, 128, False, True), (31, 0, 128, True, True)]
                return 0, 128, items
            i0 = 2 * (gi - 1)
            for s in range(16 + i0):
                items += [(2 * s, 0, 512, False, False), (2 * s + 1, 0, 512, False, False)]
            s0 = 16 + i0
            s1 = s0 + 1
            items += [(2 * s0, 256, 256, False, False), (2 * s0 + 1, 256, 256, False, False)]
            items += [(2 * s0, 0, 128, True, True), (2 * s0, 128, 128, False, True), (2 * s0 + 1, 128, 128, True, True)]
            items += [(2 * s1, 256, 128, True, True), (2 * s1, 384, 128, False, True), (2 * s1 + 1, 384, 128, True, True)]
            return 128 + i0 * 256, 512, items

        for h in range(NH):
            kb = KTh.next()
            qb = QTh.next()
            vb = Vh.next()
            ld_w = spill_toks if h < 2 else []
            t_lk = P.dma("sp", f"ldk{h % 2}", waits=ld_w + kb.take(), out=kb.t[:], in_=KT[h])
            t_lq = P.dma("sp", f"ldq{h % 2}", waits=qb.take(), out=qb.t[:], in_=QT[h])
            t_lv = P.dma("sp", f"ldv{h % 2}", waits=vb.take(), out=vb.t[:, :, 0:64], in_=VS[h])
            users = []
            for gi in range(9):
                q0, nq, items = group_items(gi)
                batches = []
                for it in items:
                    if batches and len(batches[-1]) < 3 and batches[-1][0][1:3] == it[1:3] and batches[-1][0][4] == it[4]:
                        batches[-1].append(it)
                    else:
                        batches.append([it])
                ob = Ops.next()
                ob_w = ob.take()
                nb = len(batches)
                exp_tok = [None] * nb
                pbuf = [None] * nb
                first_pv = [True]

                def emit_pv(bi):
                    for k, (ktile, qc0, qn, causal, own) in enumerate(batches[bi]):
                        is_last = (bi == nb - 1) and (k == len(batches[bi]) - 1)
                        t = P.I("pe", "matmul", waits=[exp_tok[bi], t_lv] + (ob_w if first_pv[0] else []), sig=True,
                                out=ob.t[:, qc0:qc0 + qn], lhsT=vb.t[:, ktile, :], rhs=pbuf[bi].t[:, k, 0:qn],
                                start=first_pv[0], stop=is_last, skip_group_check=True)
                        first_pv[0] = False
                    pbuf[bi].used(t)
                    return t

                t_pv = None
                for bi, batch in enumerate(batches):
                    sbuf_ = Sps.next()
                    ws = sbuf_.take()
                    for k, (ktile, qc0, qn, causal, own) in enumerate(batch):
                        nr = 96
                        t_qk = P.I("pe", "matmul", waits=([t_lk, t_lq] + ws) if k == 0 else (), sig=True,
                                   out=sbuf_.t[:, k, 0:qn], lhsT=kb.t[0:nr, ktile * 128:(ktile + 1) * 128],
                                   rhs=qb.t[0:nr, q0 + qc0:q0 + qc0 + qn], start=True, stop=not causal)
                        if causal:
                            t_qk = P.I("pe", "matmul", waits=[CONST], sig=True, out=sbuf_.t[:, k, 0:qn], lhsT=ident[:],
                                       rhs=cbias[:, 0:qn], start=False, stop=True)
                    qn = batch[0][2]
                    pb_ = Pb.next()
                    pbuf[bi] = pb_
                    exp_tok[bi] = P.I("act", "activation", waits=[t_qk] + pb_.take(), out=pb_.t[:, 0:len(batch), 0:qn],
                                      in_=sbuf_.t[:, 0:len(batch), 0:qn], func=AF.Exp, scale=HD ** -0.5)
                    sbuf_.used(exp_tok[bi])
                    if bi >= 1:
                        t_pv = emit_pv(bi - 1)
                t_pv = emit_pv(nb - 1)
                users.append(t_pv)
                rd = rden.next()
                t_rc = P.I("dve", "reciprocal", waits=[t_pv] + rd.take(), out=rd.t[64:128, 0:nq], in_=ob.t[64:128, 0:nq])
                po = (h % 2) * 64
                t_at = P.I("dve", "tensor_tensor", waits=[t_rc], out=attnT[po:po + 64, h // 2, q0:q0 + nq],
                           in0=ob.t[0:64, 0:nq], in1=rd.t[64:128, 0:nq], op=ALU.mult)
                ob.used(t_at)
                rd.used(t_at)
                at_toks.append(t_at)
            for u in users:
                kb.used(u)
                qb.used(u)
                vb.used(u)
    t_ats = P.dma("sp", "ats", waits=at_toks, out=AT.rearrange("g p t -> p g t"), in_=attnT[:])
    spill_toks.append(t_ats)
    P.barrier()
    esB.close()

    esC = ExitStack()
    w_ba_bf = sb(esC, "w_ba_bf", [128, 4, D], BF16)
    wup = Ring([sb(esC, f"wup{i}", [128, 8, 2, 256], BF16) for i in range(3)])
    G0l = Buf(sb(esC, "G0l", [128, 8, 512], BF16))
    GPl = Buf(sb(esC, "GPl", [128, 8, 512], BF16))
    attl = Buf(sb(esC, "attl", [128, 4, 512], BF16))
    tmpm = Ring([sb(esC, f"tmpm{i}", [128, 512], F32) for i in range(2)])
    x1 = Buf(sb(esC, "x1", [128, 4, D], F32))
    h2b = Ring([sb(esC, f"h2b{i}", [128, D], BF16) for i in range(2)])
    h2T = Buf(sb(esC, "h2T", [128, 8, 512], BF16))
    zb = Ring([sb(esC, f"zb{i}", [128, 2 + 512], F32) for i in range(2)])
    cv = Ring([sb(esC, f"cv{i}", [128, 512], F32) for i in range(3)])
    sg = Ring([sb(esC, f"sg{i}", [128, 512], F32) for i in range(2)])
    aT = Buf(sb(esC, "aT", [128, 22, 512], BF16))
    ob_ = Ring([sb(esC, f"ob{i}", [128, D], F32) for i in range(2)])
    pA = Ring([ps(esC, f"pA{i}", [128, 512], F32) for i in range(2)])
    pY = Ring([ps(esC, f"pY{i}", [128, 512], F32) for i in range(2)])
    trc = Ring([ps(esC, f"trc{i}", [128, 8, 128], BF16) for i in range(1)])
    pz = Ring([ps(esC, f"pz{i}", [128, 512], F32) for i in range(3)])

    W_BA = P.dma("pool", "wc0", out=w_ba_bf[:], in_=w_ba.rearrange("(g p) n -> p g n", p=128))
    w_up_v = w_up.rearrange("(kt p) n -> p kt n", p=128)

    chunk_tab = {}
    N_CHUNKS = 8 * 11
    chunk_ctr = [0]

    def issue_chunk(idx):
        if idx in chunk_tab or idx >= N_CHUNKS:
            return
        c2 = idx % 11
        wb = wup.next()
        key = f"wup{(wup.i - 1) % 3}"
        ww = wb.take()
        P.dma("pool", key, waits=ww, out=wb.t[:, :, 0, :], in_=w_up_v[:, :, c2 * 256:(c2 + 1) * 256])
        tk = P.dma("pool", key, out=wb.t[:, :, 1, :], in_=w_up_v[:, :, DFF + c2 * 256:DFF + (c2 + 1) * 256])
        chunk_tab[idx] = (wb, tk)

    def load_wup(c2):
        idx = chunk_ctr[0]
        chunk_ctr[0] += 1
        assert idx % 11 == c2
        issue_chunk(idx)
        issue_chunk(idx + 1)
        issue_chunk(idx + 2)
        return chunk_tab[idx]
    carry_tok = [None] * 44
    out_toks = []

    c_loads = {}
    halo_tok = [None]

    def issue_c_loads(kind, m):
        N = 128 if kind == "halo" else 512
        qoff = 0 if kind == "halo" else 128 + m * 512
        t_g0 = P.dma("sp", "ldg0", waits=spill_toks + G0l.take(), out=G0l.t[:, :, 0:N], in_=G0[:, :, qoff:qoff + N].rearrange("f p t -> p f t"))
        t_gp = P.dma("sp", "ldgp", waits=GPl.take(), out=GPl.t[:, :, 0:N], in_=GP[:, :, qoff:qoff + N].rearrange("f p t -> p f t"))
        t_al = P.dma("sp", "ldat", waits=attl.take(), out=attl.t[:, :, 0:N], in_=AT[:, :, qoff:qoff + N].rearrange("g p t -> p g t"))
        c_loads[(kind, m)] = (t_g0, t_gp, t_al)

    def phaseC_macro(kind, m):
        nt = 1 if kind == "halo" else 4
        N = nt * 128
        qoff = 0 if kind == "halo" else 128 + m * 512
        src = x_pre if kind == "halo" else x_own
        tok0 = 3968 if kind == "halo" else m * 512
        if (kind, m) not in c_loads:
            issue_c_loads(kind, m)
        t_g0, t_gp, t_al = c_loads[(kind, m)]
        t_x = P.dma("sp", "ldx1", waits=x1.take(), out=x1.t[:, 0:nt, :], in_=src[tok0:tok0 + N, :].rearrange("(j p) d -> p j d", p=128))
        mix_toks = []
        for f in range(8):
            pb = pA.next()
            wpb = pb.take()
            for g in range(4):
                t_mm = P.I("pe", "matmul", waits=([W_BA, t_al] + wpb) if g == 0 else (), sig=(g == 3), out=pb.t[:, 0:N],
                           lhsT=w_ba_bf[:, g, f * 128:(f + 1) * 128], rhs=attl.t[:, g, 0:N], start=(g == 0), stop=(g == 3))
            tm = tmpm.next()
            t_1 = P.I("dve", "tensor_tensor", waits=[t_mm, t_g0] + tm.take(), out=tm.t[:, 0:N], in0=pb.t[:, 0:N], in1=G0l.t[:, f, 0:N], op=ALU.mult)
            pb.used(t_1)
            t_2 = P.I("dve", "tensor_tensor", waits=[t_1, t_gp], out=GPl.t[:, f, 0:N], in0=tm.t[:, 0:N],
                      in1=GPl.t[:, f, 0:N], op=ALU.add)
            tm.used(t_2)
            mix_toks.append(t_2)
        attl.used(t_mm)
        G0l.used(mix_toks[-1])
        h2T_w = h2T.take()
        h2T_toks = []
        x1_toks = [None] * nt
        h2T_first = [True]
        last_h = [None]

        def c1_tile(j):
            xs = []
            for c in range(2):
                pb = pY.next()
                wpb = pb.take()
                for kt in range(8):
                    t_mm = P.I("pe", "matmul", waits=(mix_toks + [W_OUT] + wpb) if kt == 0 else (), sig=(kt == 7), out=pb.t[:],
                               lhsT=GPl.t[:, kt, j * 128:(j + 1) * 128], rhs=w_out_bf[:, kt, c * 512:(c + 1) * 512],
                               start=(kt == 0), stop=(kt == 7))
                GPl.used(t_mm)
                t_a = P.I("dve", "tensor_tensor", waits=[t_mm, t_x], out=x1.t[:, j, c * 512:(c + 1) * 512], in0=pb.t[:],
                          in1=x1.t[:, j, c * 512:(c + 1) * 512], op=ALU.add)
                pb.used(t_a)
                xs.append(t_a)
            x1_toks[j] = xs
            t_rs, ni = rmsnorm_tile(x1.t[:, j, :], gffn, None, xs)
            yield
            yield
            hb = h2b.next()
            t_h = P.I("dve", "scalar_tensor_tensor", waits=[t_rs, CONST] + xs + hb.take(), out=hb.t[:], in0=x1.t[:, j, :],
                      scalar=rs_all[:, ni:ni + 1], in1=gffn[:], op0=ALU.mult, op1=ALU.mult)
            last_h[0] = t_h
            tr = trc.next()
            wtr = tr.take()
            for kt in range(8):
                t_tr = P.I("pe", "transpose", waits=([t_h, CONST] + wtr) if kt == 0 else (), sig=(kt == 7), out=tr.t[:, kt, :],
                           in_=hb.t[:, kt * 128:(kt + 1) * 128], identity=ident[:])
            hb.used(t_tr)
            t_cp = P.I("act", "activation", waits=[t_tr] + (h2T_w if h2T_first[0] else []), out=h2T.t[:, :, j * 128:(j + 1) * 128], in_=tr.t[:], func=AF.Copy)
            h2T_first[0] = False
            tr.used(t_cp)
            h2T_toks.append(t_cp)

        run_tasks([c1_tile(j) for j in range(nt)], 4)
        t_h = last_h[0]
        nxt = c_order[c_order.index((kind, m)) + 1] if c_order.index((kind, m)) + 1 < len(c_order) else None
        if nxt is not None:
            issue_c_loads(*nxt)
        h2T_users = []
        if kind == "halo":
            t_hh = P.I("dve", "tensor_copy", waits=h2T_toks, out=h2T_halo[:], in_=h2T.t[:, :, 126:128])
            halo_tok[0] = t_hh
            h2T.used(t_hh)
            x1.used(t_h)
            return
        aT_w = aT.take()
        aT_toks = []
        for c2 in range(11):
            wb, t_w = load_wup(c2)
            for ci in range(2):
                c = 2 * c2 + ci
                ups = []
                for half in range(2):
                    cc = c + 22 * half
                    if m == 0:
                        pb = pz.next()
                        wpb = pb.take()
                        for kt in range(8):
                            t_mm = P.I("pe", "matmul", waits=([halo_tok[0], t_w] + wpb) if kt == 0 else (), sig=(kt == 7), out=pb.t[:, 0:2],
                                       lhsT=wb.t[:, kt, half, ci * 128:(ci + 1) * 128], rhs=h2T_halo[:, kt, :], start=(kt == 0), stop=(kt == 7))
                        carry_tok[cc] = P.I("dve", "tensor_scalar", waits=[t_mm, CONST], out=carry[:, cc, :], in0=pb.t[:, 0:2], scalar1=hflag[:, 0:1],
                                            scalar2=None, op0=ALU.mult)
                        pb.used(carry_tok[cc])
                    pb = pz.next()
                    wpb = pb.take()
                    for kt in range(8):
                        t_mm = P.I("pe", "matmul", waits=(h2T_toks + [t_w] + wpb) if kt == 0 else (), sig=(kt == 7), out=pb.t[:],
                                   lhsT=wb.t[:, kt, half, ci * 128:(ci + 1) * 128], rhs=h2T.t[:, kt, :], start=(kt == 0), stop=(kt == 7))
                    h2T_users.append(t_mm)
                    z = zb.next()
                    wz = z.take()
                    t_zc = P.I("act", "activation", waits=[carry_tok[cc]] + wz, out=z.t[:, 0:2], in_=carry[:, cc, :], func=AF.Copy)
                    t_z = P.I("act", "activation", waits=[t_mm] + wz, out=z.t[:, 2:514], in_=pb.t[:], func=AF.Copy)
                    v_ = cv.next()
                    t_c2 = P.I("act", "activation", waits=[t_mm, CONST] + v_.take(), out=v_.t[:], in_=pb.t[:], func=AF.Identity,
                               scale=convw[:, 2, cc:cc + 1], bias=convb[:, cc:cc + 1])
                    pb.used(t_z)
                    pb.used(t_c2)
                    carry_tok[cc] = P.I("pool", "tensor_copy", waits=[t_z, t_zc], out=carry[:, cc, :], in_=z.t[:, 512:514])
                    t_c1 = P.I("dve", "scalar_tensor_tensor", waits=[t_z, t_zc, t_c2], out=v_.t[:], in0=z.t[:, 1:513], scalar=convw[:, 1, cc:cc + 1],
                               in1=v_.t[:], op0=ALU.mult, op1=ALU.add)
                    t_c0 = P.I("dve", "scalar_tensor_tensor", waits=[t_c1], out=v_.t[:], in0=z.t[:, 0:512], scalar=convw[:, 0, cc:cc + 1],
                               in1=v_.t[:], op0=ALU.mult, op1=ALU.add)
                    z.used(t_c0)
                    z.used(carry_tok[cc])
                    ups.append((v_, t_c0))
                s_ = sg.next()
                t_si = P.I("act", "activation", waits=[ups[0][1]] + s_.take(), out=s_.t[:], in_=ups[0][0].t[:], func=AF.Silu)
                ups[0][0].used(t_si)
                t_a = P.I("pool", "tensor_tensor", waits=[t_si, ups[1][1]] + (aT_w if c == 0 else []), out=aT.t[:, c, :], in0=s_.t[:], in1=ups[1][0].t[:], op=ALU.mult)
                s_.used(t_a)
                ups[1][0].used(t_a)
                aT_toks.append(t_a)
            wb.used(t_mm)
        for t in h2T_users:
            h2T.used(t)
        aT_users = []
        for j in range(4):
            o = ob_.next()
            wo = o.take()
            for c2 in range(2):
                pb = pY.next()
                wpb = pb.take()
                for c in range(22):
                    t_mm = P.I("pe", "matmul", waits=(aT_toks + [W_DN] + wpb) if c == 0 else (), sig=(c == 21), out=pb.t[:],
                               lhsT=aT.t[:, c, j * 128:(j + 1) * 128], rhs=w_down_bf[:, c, c2 * 512:(c2 + 1) * 512], start=(c == 0), stop=(c == 21))
                aT_users.append(t_mm)
                t_o = P.I("dve", "tensor_tensor", waits=[t_mm] + x1_toks[j] + (wo if c2 == 0 else []), out=o.t[:, c2 * 512:(c2 + 1) * 512], in0=pb.t[:],
                          in1=x1.t[:, j, c2 * 512:(c2 + 1) * 512], op=ALU.add)
                pb.used(t_o)
            t_st = P.dma("sp", f"sty{(ob_.i - 1) % 2}", waits=[t_o], out=y[tok0 + j * 128: tok0 + (j + 1) * 128, :], in_=o.t[:])
            o.used(t_st)
            out_toks.append(t_st)
        x1.used(t_o)
        for t in aT_users:
            aT.used(t)

    c_order = [("halo", 0)] + [("own", m) for m in range(8)]
    for (kind_, m_) in c_order:
        phaseC_macro(kind_, m_)
    P.final_wait("sp", out_toks)

    with ExitStack() as es:
        sems = {k: es.enter_context(nc.semaphore(f"s_{k}")) for k in P.sem_keys()}
        block = es.enter_context(nc.Block())

        def replay(eng_name, E):
            for (waits, method, kw, inc) in P.q[eng_name]:
                for (k, v) in waits:
                    E.wait_ge(sems[k], v)
                if method is None:
                    continue
                ins = getattr(E, method)(**kw)
                if inc is not None:
                    ins.then_inc(sems[inc[0]], inc[1])

        @block.sync
        def _(E):
            replay("sp", E)

        @block.scalar
        def _(E):
            replay("act", E)

        @block.vector
        def _(E):
            replay("dve", E)

        @block.gpsimd
        def _(E):
            replay("pool", E)

        @block.tensor
        def _(E):
            replay("pe", E)
    esC.close()
    esW.close()
    es_all.close()
    return nc


DO_ATTN = True
_CACHE = {}


def _host_consts(half):
    c = {}
    ident = np.eye(128, dtype=np.float32).astype(ml_dtypes.bfloat16)
    kk = np.arange(128)[:, None]
    qq = np.arange(128)[None, :]
    cb = np.where(kk <= qq, 0.0, -BIG).astype(np.float32).astype(ml_dtypes.bfloat16)
    c["ident_bf"] = ident
    c["cbias_bf"] = cb
    hd = 8
    inv_freq = (np.float32(500000.0) ** (-np.arange(hd, dtype=np.float32) / np.float32(hd))).astype(np.float32)
    pos = np.concatenate([np.arange(4096), half * 4096 + np.arange(4096)]).astype(np.float32)
    ang = (pos[:, None] * inv_freq[None, :]).astype(np.float32)
    cos = np.cos(ang).astype(np.float32).reshape(64, 128, 8).transpose(1, 0, 2)
    sin = np.sin(ang).astype(np.float32).reshape(64, 128, 8).transpose(1, 0, 2)
    c["cos_t"] = np.ascontiguousarray(np.concatenate([cos, cos], axis=2))
    c["sin_t"] = np.ascontiguousarray(np.concatenate([-sin, sin], axis=2))
    rb = np.full((33, 32), -BIG, dtype=np.float32)
    if half == 1:
        rb[0, 0:15] = 0.0
    for t in range(32):
        blk = t // 2
        if half == 1:
            rb[1 + t, 0:16] = 0.0
        rb[1 + t, 16:16 + blk] = 0.0
    c["rbias"] = np.ascontiguousarray(np.broadcast_to(rb[None], (128, 33, 32)))
    ic = np.zeros((4, 16), dtype=np.float32)
    for g, w in enumerate((2, 4, 8, 16)):
        tg = half * 4096 + np.arange(16)
        ic[g] = 1.0 / np.minimum(tg + 1.0, float(w))
    c["invcnt"] = np.ascontiguousarray(np.broadcast_to(ic[None], (128, 4, 16)))
    c["hflag"] = np.full((128, 1), float(half), dtype=np.float32)
    return c


def kernel(x, norm_mix_g, w_in, b_gate, q_norm_g, k_norm_g, w_pool, pool_scale,
           w_branch_attn, w_branch_pool, w_out, norm_ffn_g, w_up, conv_w, conv_b, w_down):
    f = lambda a: np.ascontiguousarray(np.asarray(a, dtype=np.float32))
    x = f(x)
    shared = {
        "w_in": f(w_in[0]), "w_pool": f(w_pool[0]), "w_ba": f(w_branch_attn[0]), "w_bp": f(w_branch_pool[0]),
        "w_out": f(w_out[0]), "w_up": f(w_up[0]), "w_down": f(w_down[0]),
        "gmix_b": f(np.broadcast_to(np.asarray(norm_mix_g[0])[None, :], (128, D))),
        "gffn_b": f(np.broadcast_to(np.asarray(norm_ffn_g[0])[None, :], (128, D))),
        "gq_b": f(np.broadcast_to(np.tile(np.asarray(q_norm_g[0]), 8)[None, :], (128, 512))),
        "gk_b": f(np.broadcast_to(np.tile(np.asarray(k_norm_g[0]), 8)[None, :], (128, 512))),
        "bgate_t": f(np.asarray(b_gate[0]).reshape(16, 128).T),
        "gmix_t": f(np.asarray(norm_mix_g[0]).reshape(8, 128).T),
        "pscale_t": f(np.asarray(pool_scale[0]).reshape(4, 128).T),
        "convw_t": f(np.asarray(conv_w[0]).reshape(3, 44, 128).transpose(2, 0, 1)),
        "convb_t": f(np.asarray(conv_b[0]).reshape(44, 128).T),
    }
    if "nc" not in _CACHE:
        _CACHE["nc"] = build_program(DO_ATTN)
    nc = _CACHE["nc"]
    in_maps = []
    zeros = np.zeros((NTOK, D), dtype=np.float32)
    for c in range(8):
        b, half = c // 2, c % 2
        m = dict(shared)
        m["x_own"] = np.ascontiguousarray(x[b, half * NTOK:(half + 1) * NTOK])
        m["x_pre"] = np.ascontiguousarray(x[b, 0:NTOK]) if half == 1 else zeros
        m.update(_host_consts(half))
        in_maps.append(m)
    res = run_bass_kernel_spmd(nc, in_maps, core_ids=list(range(8)))
    out = np.empty((4, 2 * NTOK, D), dtype=np.float32)
    for c in range(8):
        b, half = c // 2, c % 2
        out[b, half * NTOK:(half + 1) * NTOK] = res.results[c]["y"]
    return out
```
